# Optimizing a Trainium2 kernel written in Bass

```python
import math
import jax
import jax.numpy as jnp
from jax import lax
import numpy as np


D_MODEL = 1024
BATCH = 8
SEQ = 4096
DEPTH = 2
DEC_BATCH = 2
DEC_SEQ = 16384
PAST_LEN = 128

N_META = 16
GRID_W = 64
HEAD_DIM = 64
D_MIX = 2 * D_MODEL

SSD_HEADS = 16
SSD_HEAD_DIM = 64
SSD_INNER = SSD_HEADS * SSD_HEAD_DIM
SSD_GROUPS = 2
SSD_STATE = 128
SSD_XBC = SSD_INNER + 2 * SSD_GROUPS * SSD_STATE
SSD_CONV = 5
SSD_CHUNK = 128

WIN_Q_HEADS = 8
WIN_KV_HEADS = 2
WIN_RADIUS = 128
WIN_BLOCK = 128
ROPE_THETA = 500000.0
ROPE_DIM = HEAD_DIM // 4

NA_HEADS = 8
NA_KR = 8
NA_KC = 16
NA_QC = 16
NA_KCB = 2 * NA_KC

D_FF = 2816
FFN_CONV = 3
EPS = 1e-6

D_IN_PROJ = SSD_INNER + SSD_XBC + 2 * SSD_HEADS + (WIN_Q_HEADS + 2 * WIN_KV_HEADS) * HEAD_DIM + 3 * NA_HEADS * HEAD_DIM

kernel_name = 'hybrid_bidir_ssd_window_natten_encoder'


def _rmsnorm(x, w):
    xf = x.astype(jnp.float32)
    y = xf * lax.rsqrt(jnp.mean(xf * xf, axis=-1, keepdims=True) + EPS)
    return (y * w.astype(jnp.float32)).astype(x.dtype)


def _depthwise_conv(x, w, b):
    k_w = w.shape[0]
    pad = k_w // 2
    n = x.shape[1]
    xp = jnp.pad(x, ((0, 0), (pad, pad), (0, 0)))
    y = xp[:, 0:n] * w[0] + b
    for j in range(1, k_w):
        y = y + xp[:, j:j + n] * w[j]
    return y


def _partial_rope(x, pos):
    half = ROPE_DIM // 2
    inv = jnp.power(ROPE_THETA, -jnp.arange(half, dtype=jnp.float32) / half)
    ang = pos.astype(jnp.float32)[:, None] * inv[None, :]
    cos = jnp.cos(ang)[None, :, None, :]
    sin = jnp.sin(ang)[None, :, None, :]
    xr = x[..., :ROPE_DIM].astype(jnp.float32)
    x1, x2 = xr[..., :half], xr[..., half:]
    rot = jnp.concatenate([x1 * cos - x2 * sin, x2 * cos + x1 * sin], axis=-1).astype(x.dtype)
    return jnp.concatenate([rot, x[..., ROPE_DIM:]], axis=-1)


def _split_in_proj(u):
    sizes = [SSD_INNER, SSD_XBC, 2 * SSD_HEADS, WIN_Q_HEADS * HEAD_DIM, WIN_KV_HEADS * HEAD_DIM,
             WIN_KV_HEADS * HEAD_DIM, NA_HEADS * HEAD_DIM, NA_HEADS * HEAD_DIM, NA_HEADS * HEAD_DIM]
    idx = np.cumsum(sizes)[:-1].tolist()
    return jnp.split(u, idx, axis=-1)


def _ssd_scan(xh, dt, a_neg, bm, cm):
    bsz, t_len, n_heads, p_dim = xh.shape
    n_grp, n_state = bm.shape[-2], bm.shape[-1]
    rep = n_heads // n_grp
    q = SSD_CHUNK
    nc = t_len // q
    f32 = jnp.float32
    x = (xh.astype(f32) * dt[..., None]).reshape(bsz, nc, q, n_grp, rep, p_dim)
    a = (dt * a_neg).reshape(bsz, nc, q, n_grp, rep)
    b = bm.astype(f32).reshape(bsz, nc, q, n_grp, n_state)
    c = cm.astype(f32).reshape(bsz, nc, q, n_grp, n_state)
    acs = jnp.cumsum(a, axis=2)
    tri = jnp.tril(jnp.ones((q, q), dtype=bool))[:, :, None, None]
    decay = jnp.exp(jnp.where(tri, acs[:, :, :, None] - acs[:, :, None, :], -jnp.inf))
    cb = jnp.einsum('bclgn,bcsgn->bclsg', c, b)
    y_diag = jnp.einsum('bclsgr,bcsgrp->bclgrp', cb[..., None] * decay, x)
    decay_states = jnp.exp(acs[:, :, -1:] - acs)
    states = jnp.einsum('bclgn,bclgrp->bcgrpn', b, x * decay_states[..., None])
    chunk_decay = jnp.exp(acs[:, :, -1])

    def step(h, inp):
        s_c, d_c = inp
        return d_c[..., None, None] * h + s_c, h

    h0 = jnp.zeros((bsz, n_grp, rep, p_dim, n_state), f32)
    _, h_in = lax.scan(step, h0, (jnp.moveaxis(states, 1, 0), jnp.moveaxis(chunk_decay, 1, 0)))
    h_in = jnp.moveaxis(h_in, 0, 1)
    y_off = jnp.einsum('bclgn,bcgrpn->bclgrp', c, h_in) * jnp.exp(acs)[..., None]
    return (y_diag + y_off).reshape(bsz, t_len, n_heads, p_dim)


def _ssd_mixer(z, xbc, dt_raw, conv_w, conv_b, dt_bias, a_log, d_skip, norm_w):
    bsz, seq_len, _ = xbc.shape
    f32 = jnp.float32
    xbc = jax.nn.silu(_depthwise_conv(xbc, conv_w, conv_b))
    xs = xbc[..., :SSD_INNER].reshape(bsz, seq_len, SSD_HEADS, SSD_HEAD_DIM)
    bm = xbc[..., SSD_INNER:SSD_INNER + SSD_GROUPS * SSD_STATE].reshape(bsz, seq_len, SSD_GROUPS, SSD_STATE)
    cm = xbc[..., SSD_INNER + SSD_GROUPS * SSD_STATE:].reshape(bsz, seq_len, SSD_GROUPS, SSD_STATE)
    dt_raw = dt_raw.astype(f32)
    dt_f = jax.nn.softplus(dt_raw[..., :SSD_HEADS] + dt_bias[0].astype(f32))
    dt_b = jax.nn.softplus(dt_raw[..., SSD_HEADS:] + dt_bias[1].astype(f32))
    a_f = -jnp.exp(a_log[0].astype(f32))
    a_b = -jnp.exp(a_log[1].astype(f32))
    pad = SSD_CHUNK - N_META

    def lay(t):
        zeros = jnp.zeros((bsz, pad) + t.shape[2:], t.dtype)
        return jnp.concatenate([t[:, :N_META], zeros, t[:, N_META:]], axis=1)

    xh, bp, cp = lay(xs), lay(bm), lay(cm)
    dtf, dtb = lay(dt_f), lay(dt_b)
    y_f = _ssd_scan(xh, dtf, a_f, bp, cp)
    flip = lambda t: jnp.flip(t, axis=1)
    y_b = flip(_ssd_scan(flip(xh), flip(dtb), a_b, flip(bp), flip(cp)))
    y = y_f + y_b
    y = jnp.concatenate([y[:, :N_META], y[:, SSD_CHUNK:]], axis=1)
    y = y + d_skip.astype(f32)[:, None] * xs.astype(f32)
    y = y.reshape(bsz, seq_len, SSD_INNER) * jax.nn.silu(z.astype(f32))
    return _rmsnorm(y, norm_w).astype(z.dtype)


def _window_attention(q, k, v, sink):
    bsz, seq_len, n_q, hd = q.shape
    n_kv = k.shape[2]
    rep = n_q // n_kv
    w = WIN_BLOCK
    nb = -(-seq_len // w)
    lp = nb * w
    f32 = jnp.float32
    scale = hd ** -0.5
    qb = jnp.pad(q, ((0, 0), (0, lp - seq_len), (0, 0), (0, 0))).reshape(bsz, nb, w, n_kv, rep, hd)

    def kwin(t):
        tp = jnp.pad(t, ((0, 0), (w, lp - seq_len + w), (0, 0), (0, 0))).reshape(bsz, nb + 2, w, n_kv, hd)
        return jnp.concatenate([tp[:, :-2], tp[:, 1:-1], tp[:, 2:]], axis=2)

    kw, vw = kwin(k), kwin(v)
    km, vm = k[:, :N_META], v[:, :N_META]
    s_w = jnp.einsum('bnqgrd,bnkgd->bngrqk', qb, kw).astype(f32) * scale
    s_m = jnp.einsum('bnqgrd,bmgd->bngrqm', qb, km).astype(f32) * scale
    qi = jnp.arange(nb)[:, None, None] * w + jnp.arange(w)[None, :, None]
    kj = jnp.arange(nb)[:, None, None] * w - w + jnp.arange(3 * w)[None, None, :]
    ok = (jnp.abs(qi - kj) <= WIN_RADIUS) & (kj >= N_META) & (kj < seq_len)
    s_w = jnp.where(ok[None, :, None, None], s_w, -jnp.inf)
    sink_l = jnp.broadcast_to(sink.astype(f32).reshape(n_kv, rep)[None, None, :, :, None, None], s_w.shape[:-1] + (1,))
    p = jax.nn.softmax(jnp.concatenate([s_w, s_m, sink_l], axis=-1), axis=-1)
    p_w = p[..., :3 * w].astype(v.dtype)
    p_m = p[..., 3 * w:3 * w + N_META].astype(v.dtype)
    o = jnp.einsum('bngrqk,bnkgd->bnqgrd', p_w, vw) + jnp.einsum('bngrqm,bmgd->bnqgrd', p_m, vm)
    return o.reshape(bsz, lp, n_q * hd)[:, :seq_len]


def _na_attention(q, k, v, rpb, meta_bias):
    bsz, seq_len, n_heads, hd = q.shape
    n_tok = seq_len - N_META
    rows = n_tok // GRID_W
    kr = min(NA_KR, rows)
    ncb = GRID_W // NA_QC
    scale = hd ** -0.5
    f32 = jnp.float32
    qm, km, vm = q[:, :N_META], k[:, :N_META], v[:, :N_META]
    qg = q[:, N_META:].reshape(bsz, rows, GRID_W, n_heads, hd)
    kg = k[:, N_META:].reshape(bsz, rows, GRID_W, n_heads, hd)
    vg = v[:, N_META:].reshape(bsz, rows, GRID_W, n_heads, hd)
    qcol = np.arange(GRID_W).reshape(ncb, NA_QC)
    qcs = np.clip(qcol - NA_KC // 2, 0, GRID_W - NA_KC)
    kcol = np.clip(np.arange(ncb) * NA_QC - NA_KC // 2, 0, GRID_W - NA_KCB)[:, None] + np.arange(NA_KCB)[None, :]
    col_ok = jnp.asarray((kcol[:, None, :] >= qcs[:, :, None]) & (kcol[:, None, :] < qcs[:, :, None] + NA_KC))
    dc = jnp.asarray(np.clip(kcol[:, None, :] - qcol[:, :, None] + NA_KC - 1, 0, 2 * NA_KC - 2))
    col_idx = jnp.asarray(kcol)
    rpb32 = rpb.astype(f32)
    mb = meta_bias.astype(f32)

    def one_row(r):
        rs = jnp.clip(r - kr // 2, 0, rows - kr)
        k_rows = lax.dynamic_slice_in_dim(kg, rs, kr, axis=1)
        v_rows = lax.dynamic_slice_in_dim(vg, rs, kr, axis=1)
        kb = k_rows[:, :, col_idx]
        vb = v_rows[:, :, col_idx]
        qr = lax.dynamic_index_in_dim(qg, r, axis=1, keepdims=False).reshape(bsz, ncb, NA_QC, n_heads, hd)
        dr = rs + jnp.arange(kr) - r + NA_KR - 1
        bias = rpb32[:, dr[None, None, :, None], dc[:, :, None, :]]
        s_w = jnp.einsum('bcqhd,bicjhd->bhcqij', qr, kb).astype(f32) * scale + bias[None]
        s_w = jnp.where(col_ok[:, :, None, :], s_w, -jnp.inf).reshape(bsz, n_heads, ncb, NA_QC, kr * NA_KCB)
        s_m = jnp.einsum('bcqhd,bmhd->bhcqm', qr, km).astype(f32) * scale + mb[None, :, None, None, :]
        p = jax.nn.softmax(jnp.concatenate([s_w, s_m], axis=-1), axis=-1)
        p_w = p[..., :kr * NA_KCB].reshape(bsz, n_heads, ncb, NA_QC, kr, NA_KCB).astype(v.dtype)
        p_m = p[..., kr * NA_KCB:].astype(v.dtype)
        o = jnp.einsum('bhcqij,bicjhd->bcqhd', p_w, vb) + jnp.einsum('bhcqm,bmhd->bcqhd', p_m, vm)
        return o.reshape(bsz, GRID_W, n_heads * hd)

    o_grid = lax.map(one_row, jnp.arange(rows))
    o_grid = jnp.moveaxis(o_grid, 0, 1).reshape(bsz, n_tok, n_heads * hd)
    bias0 = rpb32[:, NA_KR - 1:NA_KR - 1 + kr, NA_KC - 1:NA_KC - 1 + NA_KC]
    k0, v0 = kg[:, :kr, :NA_KC], vg[:, :kr, :NA_KC]
    s0 = jnp.einsum('bmhd,bijhd->bhmij', qm, k0).astype(f32) * scale + bias0[None, :, None]
    smm = jnp.einsum('bmhd,bnhd->bhmn', qm, km).astype(f32) * scale + mb[None, :, None, :]
    p0 = jax.nn.softmax(jnp.concatenate([s0.reshape(bsz, n_heads, N_META, kr * NA_KC), smm], axis=-1), axis=-1)
    p0w = p0[..., :kr * NA_KC].reshape(bsz, n_heads, N_META, kr, NA_KC).astype(v.dtype)
    p0m = p0[..., kr * NA_KC:].astype(v.dtype)
    o_meta = jnp.einsum('bhmij,bijhd->bmhd', p0w, v0) + jnp.einsum('bhmn,bnhd->bmhd', p0m, vm)
    o_meta = o_meta.reshape(bsz, N_META, n_heads * hd)
    return jnp.concatenate([o_meta, o_grid], axis=1)


def _encode(x, meta_tokens, norm_mix_pre, norm_mix_post, w_in, ssd_conv_w, ssd_conv_b, ssd_dt_bias, ssd_a_log,
            ssd_d, ssd_norm_w, win_sink, na_rpb, na_meta_bias, w_out, norm_ffn_pre, norm_ffn_post, ffn_w_up,
            ffn_conv_w, ffn_conv_b, ffn_w_down):
    bsz, n_tok, _ = x.shape
    seq_len = n_tok + N_META
    meta = jnp.broadcast_to(meta_tokens.astype(x.dtype)[None], (bsz, N_META, D_MODEL))
    h = jnp.concatenate([meta, x], axis=1)
    pos = jnp.arange(seq_len)
    for i in range(DEPTH):
        a = _rmsnorm(h, norm_mix_pre[i])
        u = a @ w_in[i]
        z, xbc, dt_raw, wq, wk, wv, nq, nk, nv = _split_in_proj(u)
        y_ssd = _ssd_mixer(z, xbc, dt_raw, ssd_conv_w[i], ssd_conv_b[i], ssd_dt_bias[i], ssd_a_log[i],
                           ssd_d[i], ssd_norm_w[i])
        q = _partial_rope(wq.reshape(bsz, seq_len, WIN_Q_HEADS, HEAD_DIM), pos)
        k = _partial_rope(wk.reshape(bsz, seq_len, WIN_KV_HEADS, HEAD_DIM), pos)
        v = wv.reshape(bsz, seq_len, WIN_KV_HEADS, HEAD_DIM)
        y_win = _window_attention(q, k, v, win_sink[i])
        y_na = _na_attention(nq.reshape(bsz, seq_len, NA_HEADS, HEAD_DIM), nk.reshape(bsz, seq_len, NA_HEADS, HEAD_DIM),
                             nv.reshape(bsz, seq_len, NA_HEADS, HEAD_DIM), na_rpb[i], na_meta_bias[i])
        mix = jnp.concatenate([y_ssd, y_win, y_na], axis=-1) @ w_out[i]
        h = h + _rmsnorm(mix, norm_mix_post[i])
        f = _rmsnorm(h, norm_ffn_pre[i])
        g = _depthwise_conv(f @ ffn_w_up[i], ffn_conv_w[i], ffn_conv_b[i])
        gate, up = jnp.split(g, 2, axis=-1)
        f = (jax.nn.gelu(gate, approximate=True) * up) @ ffn_w_down[i]
        h = h + _rmsnorm(f, norm_ffn_post[i])
    return h[:, N_META:]


def setup_inputs(seed: int = 0) -> dict:
    key = jax.random.key(seed)
    ks = jax.random.split(key, 24)
    f32 = jnp.float32
    nrm = lambda k, shape, s: jax.random.normal(k, shape, f32) * s
    gain = lambda k, shape: 1.0 + 0.02 * jax.random.normal(k, shape, f32)
    dt0 = jnp.exp(jax.random.uniform(ks[8], (DEPTH, 2, SSD_HEADS), f32, math.log(1e-3), math.log(1e-1)))
    return {
        'x_prompt': nrm(ks[0], (BATCH, SEQ, D_MODEL), 1.0),
        'x_sample': nrm(ks[1], (DEC_BATCH, DEC_SEQ, D_MODEL), 1.0),
        'meta_tokens': nrm(ks[2], (N_META, D_MODEL), 1.0),
        'norm_mix_pre': gain(ks[3], (DEPTH, D_MODEL)),
        'norm_mix_post': gain(ks[4], (DEPTH, D_MODEL)),
        'w_in': nrm(ks[5], (DEPTH, D_MODEL, D_IN_PROJ), D_MODEL ** -0.5),
        'ssd_conv_w': nrm(ks[6], (DEPTH, SSD_CONV, SSD_XBC), SSD_CONV ** -0.5),
        'ssd_conv_b': nrm(ks[7], (DEPTH, SSD_XBC), 0.01),
        'ssd_dt_bias': dt0 + jnp.log(-jnp.expm1(-dt0)),
        'ssd_a_log': jnp.log(jax.random.uniform(ks[9], (DEPTH, 2, SSD_HEADS), f32, 1.0, 16.0)),
        'ssd_d': gain(ks[10], (DEPTH, SSD_HEADS)),
        'ssd_norm_w': gain(ks[11], (DEPTH, SSD_INNER)),
        'win_sink': nrm(ks[12], (DEPTH, WIN_Q_HEADS), 0.5),
        'na_rpb': nrm(ks[13], (DEPTH, NA_HEADS, 2 * NA_KR - 1, 2 * NA_KC - 1), 0.02),
        'na_meta_bias': nrm(ks[14], (DEPTH, NA_HEADS, N_META), 0.02),
        'w_out': nrm(ks[15], (DEPTH, D_MIX, D_MODEL), D_MIX ** -0.5),
        'norm_ffn_pre': gain(ks[16], (DEPTH, D_MODEL)),
        'norm_ffn_post': gain(ks[17], (DEPTH, D_MODEL)),
        'ffn_w_up': nrm(ks[18], (DEPTH, D_MODEL, 2 * D_FF), D_MODEL ** -0.5),
        'ffn_conv_w': nrm(ks[19], (DEPTH, FFN_CONV, 2 * D_FF), FFN_CONV ** -0.5),
        'ffn_conv_b': nrm(ks[20], (DEPTH, 2 * D_FF), 0.01),
        'ffn_w_down': nrm(ks[21], (DEPTH, D_FF, D_MODEL), D_FF ** -0.5),
    }


def reference(x_prompt, x_sample, meta_tokens, norm_mix_pre, norm_mix_post, w_in, ssd_conv_w, ssd_conv_b,
              ssd_dt_bias, ssd_a_log, ssd_d, ssd_norm_w, win_sink, na_rpb, na_meta_bias, w_out, norm_ffn_pre,
              norm_ffn_post, ffn_w_up, ffn_conv_w, ffn_conv_b, ffn_w_down):
    y_prompt = _encode(x_prompt, meta_tokens, norm_mix_pre, norm_mix_post, w_in, ssd_conv_w, ssd_conv_b, ssd_dt_bias,
                       ssd_a_log, ssd_d, ssd_norm_w, win_sink, na_rpb, na_meta_bias, w_out, norm_ffn_pre,
                       norm_ffn_post, ffn_w_up, ffn_conv_w, ffn_conv_b, ffn_w_down)
    y_sample = _encode(x_sample, meta_tokens, norm_mix_pre, norm_mix_post, w_in, ssd_conv_w, ssd_conv_b, ssd_dt_bias,
                       ssd_a_log, ssd_d, ssd_norm_w, win_sink, na_rpb, na_meta_bias, w_out, norm_ffn_pre,
                       norm_ffn_post, ffn_w_up, ffn_conv_w, ffn_conv_b, ffn_w_down)
    return (y_prompt, y_sample)
```

```python
import math
from contextlib import ExitStack
import numpy as np
import concourse.bass as bass
import concourse.mybir as mybir
from concourse.bass_utils import run_bass_kernel_spmd

F32 = mybir.dt.float32
BF16 = mybir.dt.bfloat16
AF = mybir.ActivationFunctionType
ALU = mybir.AluOpType

D = 1024
NMETA = 16
PADL = 112
NEG = -30000.0
EPS = 1e-6
DFF = 2816
NCOL = 5632
C_XBC, C_NQ, C_NK, C_WQ, C_WK, C_WQS, C_WKS, C_DT4, C_Z, C_WV, C_NV = 0, 1536, 2048, 2560, 3072, 3200, 3712, 3840, 3968, 4992, 5120


class _Rec:
    def __init__(self):
        self.call = None

    def __getattr__(self, name):
        def f(*a, **kw):
            self.call = (name, a, kw)
            return None
        return f


class Sched:
    COMPUTE = ('pe', 'act', 'dve', 'pool')
    KDMA = 8

    def __init__(self, nc, dma_queues=('sp', 'pool')):
        self.nc = nc
        self.eng = {'pe': nc.tensor, 'act': nc.scalar, 'dve': nc.vector, 'pool': nc.gpsimd, 'sp': nc.sync}
        self.ops = {e: [] for e in self.eng}
        self.sem = {}
        self.count = {e: 0 for e in self.COMPUTE}
        self.dma_n = {q: 0 for q in dma_queues}
        self.dma_sems = {}
        self.dma_last = {}
        self.waited = {e: {} for e in self.eng}
        self.lastw = {}
        self.readers = {}
        self.final = []
        self.nops = 0
        self.rr = 0

    def alloc(self, stack):
        nc = self.nc
        for e in self.COMPUTE:
            self.sem[e] = stack.enter_context(nc.semaphore('s_' + e))
        for q in self.dma_n:
            self.dma_sems[q] = [stack.enter_context(nc.semaphore('d_%s_%d' % (q, i))) for i in range(self.KDMA)]

    def _need(self, eng, ev, waits):
        if ev is None:
            return
        sem, val, src = ev
        if src == 'pe' and eng == 'pe':
            return
        key = id(sem)
        if self.waited[eng].get(key, 0) >= val:
            return
        cur = waits.get(key)
        if cur is None or cur[1] < val:
            waits[key] = (sem, val)

    def _deps(self, eng, reads, writes):
        waits = {}
        for b in reads:
            self._need(eng, self.lastw.get(b), waits)
        for b in writes:
            self._need(eng, self.lastw.get(b), waits)
            for ev in self.readers.get(b, ()):
                self._need(eng, ev, waits)
        for key, (sem, val) in waits.items():
            self.waited[eng][key] = val
        return list(waits.values())

    def _commit(self, ev, reads, writes):
        for b in reads:
            r = self.readers.setdefault(b, [])
            r.append(ev)
            if len(r) > 64:
                best = {}
                for e in r:
                    k = id(e[0])
                    if k not in best or best[k][1] < e[1]:
                        best[k] = e
                self.readers[b] = list(best.values())
        for b in writes:
            self.lastw[b] = ev
            self.readers[b] = []

    def op(self, eng, fn, reads=(), writes=()):
        rec = _Rec()
        fn(rec)
        name, a, kw = rec.call
        fn = (lambda e, name=name, a=a, kw=kw: getattr(e, name)(*a, **kw))
        waits = self._deps(eng, reads, writes)
        self.count[eng] += 1
        ev = (self.sem[eng], self.count[eng], eng)
        self.ops[eng].append((waits, fn, self.sem[eng], 1))
        self._commit(ev, reads, writes)
        self.nops += 1
        return ev

    def dma(self, out, in_, reads=(), writes=(), final=False, q=None):
        if q is None:
            q = ('sp', 'pool')[self.rr % 2]
            self.rr += 1
        j = self.dma_n[q]
        self.dma_n[q] += 1
        sem = self.dma_sems[q][j % self.KDMA]
        val = 16 * (j // self.KDMA + 1)
        waits = self._deps(q, reads, writes)
        if val > 16:
            key = id(sem)
            if self.waited[q].get(key, 0) < val - 16:
                waits.append((sem, val - 16))
                self.waited[q][key] = val - 16
        ev = (sem, val, 'dma')
        self.dma_last[id(sem)] = ev
        self.ops[q].append((waits, lambda e, o=out, i=in_: e.dma_start(out=o, in_=i), sem, 16))
        self._commit(ev, reads, writes)
        if final:
            self.final.append(ev)
        self.nops += 1
        return ev

    def barrier(self):
        evs = [(self.sem[e], self.count[e], e) for e in self.COMPUTE if self.count[e] > 0]
        evs += list(self.dma_last.values())
        for eng in self.eng:
            waits = []
            for (sem, val, src) in evs:
                if self.waited[eng].get(id(sem), 0) < val:
                    waits.append((sem, val))
                    self.waited[eng][id(sem)] = val
            if waits:
                self.ops[eng].append((waits, None, None, 0))
        self.lastw = {}
        self.readers = {}

    def emit(self, block):
        finals = list(self.final)

        def body_for(name):
            ops = self.ops[name]

            def body(e):
                for waits, fn, sem, inc in ops:
                    for (ws, wv) in waits:
                        e.wait_ge(ws, wv)
                    if fn is not None:
                        fn(e).then_inc(sem, inc)
                if name == 'sp':
                    for (s, v, _) in finals:
                        e.wait_ge(s, v)
            return body

        block.sync(body_for('sp'))
        block.tensor(body_for('pe'))
        block.scalar(body_for('act'))
        block.vector(body_for('dve'))
        block.gpsimd(body_for('pool'))


def _na_table_block(rpb, i, R, offsets):
    H = rpb.shape[0]
    out = np.full((128, H, len(offsets), 128), NEG, np.float32)
    q = np.arange(128)
    qr, qc = q // 64, q % 64
    r = 2 * i + qr
    rs = np.clip(r - 4, 0, R - 8)
    qcs = np.clip(qc - 8, 0, 64 - 16)
    k = np.arange(128)
    kr_l, kc = k // 64, k % 64
    for si, off in enumerate(offsets):
        kb = i + off
        krow = 2 * kb + kr_l
        ok = (krow[:, None] >= rs[None, :]) & (krow[:, None] < rs[None, :] + 8) & \
             (kc[:, None] >= qcs[None, :]) & (kc[:, None] < qcs[None, :] + 16)
        dr = np.clip(krow[:, None] - r[None, :] + 7, 0, 14)
        dc = np.clip(kc[:, None] - qc[None, :] + 15, 0, 30)
        for h in range(H):
            b = rpb[h][dr, dc]
            out[:, h, si, :] = np.where(ok, b, NEG)
    return out


NA_VARIANTS = {
    'int': [-2, -1, 0, 1, 2], 'top1': [0, 1, 2, 3], 'top2': [-1, 0, 1, 2], 'bot2': [-2, -1, 0, 1], 'bot1': [-3, -2, -1, 0],
    'metaq': [0, 1, 2, 3]}
NA_VORDER = ['int', 'top1', 'top2', 'bot2', 'bot1', 'metaq']


def _na_tables(rpb, mb):
    H = rpb.shape[0]
    tabs = np.full((6, 128, H, 6, 128), NEG, np.float32)
    R = 32
    NR = R // 2
    reps = {'int': 5, 'top1': 0, 'top2': 1, 'bot2': NR - 2, 'bot1': NR - 1}
    for vi, v in enumerate(NA_VORDER):
        offs = NA_VARIANTS[v]
        if v != 'metaq':
            tabs[vi, :, :, :len(offs), :] = _na_table_block(rpb, reps[v], R, offs)
        else:
            k = np.arange(128)
            for si in range(4):
                krow = 2 * si + k // 64
                kc = k % 64
                ok = (kc < 16) & (krow < 8)
                for h in range(H):
                    b = rpb[h][np.clip(7 + krow, 0, 14), np.clip(15 + kc, 0, 30)]
                    tabs[vi, :, h, si, :] = np.where(ok, b, NEG)[:, None]
        for h in range(H):
            col = np.full(128, NEG, np.float32)
            col[PADL:] = mb[h]
            tabs[vi, :, h, 5, :] = col[:, None]
    return tabs


def _win_tables():
    t = np.full((128, 4, 128), NEG, np.float32)
    k = np.arange(128)[:, None]
    q = np.arange(128)[None, :]
    t[:, 0, :] = np.where(k >= q, 0.0, NEG)
    t[:, 1, :] = 0.0
    t[:, 2, :] = np.where(k <= q, 0.0, NEG)
    t[:, 3, :] = np.where(k >= PADL, 0.0, NEG) + 0 * q
    return t


def _chunked(v, n):
    return np.ascontiguousarray(v.reshape(n, 128).T)


def _host_layer_consts(i, P):
    w_in = P['w_in'][i]
    z, xbc, dtr, wq, wk, wv, nq, nk, nv = np.split(w_in, np.cumsum([1024, 1536, 32, 512, 128, 128, 512, 512])[:].tolist(), axis=1)

    def swap(w):
        w = w.reshape(w.shape[0], -1, 64).copy()
        a = w[:, :, 0:8].copy()
        w[:, :, 0:8] = w[:, :, 8:16]
        w[:, :, 8:16] = a
        return w.reshape(w.shape[0], -1)
    dt4 = np.concatenate([dtr] * 4, axis=1)
    w_my = np.concatenate([xbc, nq, nk, wq, wk, swap(wq), swap(wk), dt4, z, wv, nv], axis=1)
    assert w_my.shape[1] == NCOL
    c = {}
    c['w_in'] = np.ascontiguousarray(w_my)
    c['npre'] = _chunked(P['norm_mix_pre'][i], 8)
    c['npost'] = _chunked(P['norm_mix_post'][i], 8)
    c['fpre'] = _chunked(P['norm_ffn_pre'][i], 8)
    c['fpost'] = _chunked(P['norm_ffn_post'][i], 8)
    cw = P['ssd_conv_w'][i]
    c['cw'] = np.ascontiguousarray(cw.T.reshape(12, 128, 5).transpose(1, 0, 2)).reshape(128, 60)
    c['cb'] = _chunked(P['ssd_conv_b'][i], 12)
    c['dtb4'] = np.tile(P['ssd_dt_bias'][i].reshape(32), 4).reshape(128, 1).astype(np.float32)
    c['alog4'] = np.tile(P['ssd_a_log'][i].reshape(32), 4).reshape(128, 1).astype(np.float32)
    c['dskip'] = P['ssd_d'][i].reshape(1, 16)
    c['snw'] = P['ssd_norm_w'][i].reshape(1, 1024)
    c['sink'] = P['win_sink'][i].reshape(1, 8)
    c['natab'] = _na_tables(P['na_rpb'][i], P['na_meta_bias'][i]).reshape(6, 128, 8 * 6 * 128)
    c['w_out'] = np.ascontiguousarray(P['w_out'][i])
    c['w_up'] = np.ascontiguousarray(P['ffn_w_up'][i])
    fw = P['ffn_conv_w'][i]
    c['fcw'] = np.ascontiguousarray(fw.T.reshape(44, 128, 3).transpose(1, 0, 2)).reshape(128, 132)
    c['fcb'] = _chunked(P['ffn_conv_b'][i], 44)
    c['w_down'] = np.ascontiguousarray(P['ffn_w_down'][i])
    return {k: np.ascontiguousarray(v, dtype=np.float32) for k, v in c.items()}


def _host_globals():
    g = {}
    g['ident'] = np.eye(128, dtype=np.float32)
    g['ones'] = np.ones((128, 128), np.float32)
    sel = np.zeros((128, 32, 128), np.float32)
    for w in range(32):
        row = w if w < 16 else 32 + w
        sel[row, w, :] = 1.0
    g['sel'] = sel.reshape(128, 32 * 128)
    k = np.arange(128)[:, None]
    l = np.arange(128)[None, :]
    g['maskf'] = np.where(k > l, NEG, 0.0).astype(np.float32)
    g['maskb'] = np.where(k < l, NEG, 0.0).astype(np.float32)
    gc = np.zeros((128, 4), np.float32)
    gc[0:32] = (1, 0, 0, 0)
    gc[32:64] = (-1, 1, 0, 1)
    gc[64:96] = (0, 1, 0, 0)
    gc[96:128] = (0, 0, 1, 0)
    g['gcoef'] = gc
    pm = np.ones((128, 512), np.float32)
    pm[:, :PADL] = 0.0
    g['padmask'] = pm
    g['wintab'] = _win_tables().reshape(128, 512)
    return g


def _rope_tables(TP):
    j = np.arange(TP)
    pos = np.maximum(j - PADL, 0).astype(np.float32)
    inv = np.power(np.float32(500000.0), -np.arange(8, dtype=np.float32) / np.float32(8)).astype(np.float32)
    ang = pos[None, :] * inv[:, None]
    cos = np.ones((128, TP), np.float32)
    sin = np.zeros((128, TP), np.float32)
    for hh in range(2):
        cos[hh * 64:hh * 64 + 8] = np.cos(ang)
        cos[hh * 64 + 8:hh * 64 + 16] = np.cos(ang)
        sin[hh * 64:hh * 64 + 8] = -np.sin(ang)
        sin[hh * 64 + 8:hh * 64 + 16] = np.sin(ang)
    return cos, sin


def tiles_of(TP, w=512):
    t = []
    t0 = 0
    while t0 < TP:
        t.append((t0, min(w, TP - t0)))
        t0 += w
    return t


def build(seq_lens, depth, debug=False):
    nc = bass.Bass("TRN2", target_bir_lowering=False)
    TPs = [L + 128 for L in seq_lens]
    NS = len(seq_lens)
    okind = "ExternalOutput" if debug else "Internal"

    def din(name, shape):
        return nc.dram_tensor(name, list(shape), F32, kind="ExternalInput").ap()

    def dscr(name, shape, dt):
        return nc.dram_tensor(name, list(shape), dt, kind=okind).ap()

    G = {k: din(k, s) for k, s in [('ident', (128, 128)), ('ones', (128, 128)), ('sel', (128, 4096)), ('maskf', (128, 128)),
                                   ('maskb', (128, 128)), ('gcoef', (128, 4)), ('padmask', (128, 512)), ('wintab', (128, 512))]}
    LC = []
    for i in range(depth):
        shapes = dict(w_in=(D, NCOL), npre=(128, 8), npost=(128, 8), fpre=(128, 8), fpost=(128, 8), cw=(128, 60), cb=(128, 12),
                      dtb4=(128, 1), alog4=(128, 1), dskip=(1, 16), snw=(1, 1024), sink=(1, 8), natab=(6, 128, 6144),
                      w_out=(2048, D), w_up=(D, 2 * DFF), fcw=(128, 132), fcb=(128, 44), w_down=(DFF, D))
        LC.append({k: din('L%d_%s' % (i, k), s) for k, s in shapes.items()})
    SEQ = []
    for s in range(NS):
        TP = TPs[s]
        d = {}
        d['h0'] = din('s%d_h0' % s, (D, TP))
        d['cos'] = din('s%d_cos' % s, (128, TP))
        d['sin'] = din('s%d_sin' % s, (128, TP))
        d['out'] = nc.dram_tensor('s%d_out' % s, [D, seq_lens[s]], F32, kind="ExternalOutput").ap()
        d['hmid'] = dscr('s%d_hmid' % s, (D, TP), F32)
        d['h1'] = dscr('s%d_h1' % s, (D, TP), F32)
        d['fm'] = dscr('s%d_fm' % s, (3200, TP), BF16)
        d['dt4'] = dscr('s%d_dt4' % s, (128, TP), F32)
        d['tm'] = dscr('s%d_tm' % s, (TP, 1664), BF16)
        d['xs'] = dscr('s%d_xs' % s, (TP, 1024), BF16)
        d['btm'] = dscr('s%d_btm' % s, (TP, 256), BF16)
        d['bct'] = dscr('s%d_bct' % s, (512, TP), BF16)
        d['yf'] = dscr('s%d_yf' % s, (TP, 1024), F32)
        d['yT'] = dscr('s%d_yT' % s, (2048, TP), BF16)
        d['act'] = dscr('s%d_act' % s, (DFF, TP), BF16)
        SEQ.append(d)

    with ExitStack() as top:
        S = Sched(nc)
        S.alloc(top)
        pb = [top.enter_context(nc.psum_tensor('pb%d' % i, [128, 512], F32)) for i in range(8)]
        pbk = ['pb%d' % i for i in range(8)]

        def act(fn, r, w):
            return S.op('act', fn, r, w)

        def dve(fn, r, w):
            return S.op('dve', fn, r, w)

        def pe(fn, r, w):
            return S.op('pe', fn, r, w)

        def mm(out, lhsT, rhs, start, stop, r, w):
            return S.op('pe', lambda e, o=out, l=lhsT, rr=rhs, s0=start, s1=stop: e.matmul(o, lhsT=l, rhs=rr, start=s0, stop=s1), r, w)

        def tr(out, in_, ident, r, w):
            return S.op('pe', lambda e, o=out, i=in_, d=ident: e.transpose(o, i, d), r, w)

        uniq = [0]

        def T(st, name, shape, dt=F32):
            uniq[0] += 1
            return st.enter_context(nc.sbuf_tensor('sb%d_%s' % (uniq[0], name), list(shape), dt))

        identf = T(top, 'identf', [128, 128])
        identb = T(top, 'identb', [128, 128], BF16)
        onesf = T(top, 'onesf', [128, 128])
        onesb = T(top, 'onesb', [128, 128], BF16)
        padmask = T(top, 'padmask', [128, 512])
        S.dma(identf[:], G['ident'][:, :], writes=['identf'])
        S.dma(onesf[:], G['ones'][:, :], writes=['onesf'])
        S.dma(padmask[:], G['padmask'][:, :], writes=['padmask'])
        dve(lambda e: e.tensor_copy(out=identb[:], in_=identf[:]), ['identf'], ['identb'])
        dve(lambda e: e.tensor_copy(out=onesb[:], in_=onesf[:]), ['onesf'], ['onesb'])

        def load_weight_bf16(st, name, wsrc, K, N, scale_src=None, piece=512):
            KC = K // 128
            wt = T(st, name, [128, KC, N], BF16)
            tmpst = ExitStack()
            if KC > 8:
                piece = 256
            stg = [T(tmpst, name + '_stg%d' % i, [128, KC, piece]) for i in range(2)]
            sc = None
            if scale_src is not None:
                sc = T(tmpst, name + '_sc', [128, KC])
                S.dma(sc[:], scale_src[:, :], writes=[name + '_sc'])
            src = wsrc.rearrange("(c p) n -> p c n", p=128)
            n0 = 0
            pi = 0
            while n0 < N:
                nw = min(piece, N - n0)
                sg = stg[pi % 2]
                sk = name + '_stg%d' % (pi % 2)
                S.dma(sg[:, :, :nw], src[:, :, n0:n0 + nw], writes=[sk])
                for c in range(KC):
                    eng = 'dve' if (c % 2 == 0) else 'act'
                    if sc is not None:
                        if eng == 'dve':
                            S.op('dve', lambda e, o=wt[:, c, n0:n0 + nw], i=sg[:, c, :nw], s=sc[:, c:c + 1]:
                                 e.tensor_scalar(out=o, in0=i, scalar1=s, scalar2=None, op0=ALU.mult), [sk, name + '_sc'], [name])
                        else:
                            S.op('act', lambda e, o=wt[:, c, n0:n0 + nw], i=sg[:, c, :nw], s=sc[:, c:c + 1]:
                                 e.activation(out=o, in_=i, func=AF.Copy, scale=s), [sk, name + '_sc'], [name])
                    else:
                        if eng == 'dve':
                            S.op('dve', lambda e, o=wt[:, c, n0:n0 + nw], i=sg[:, c, :nw]: e.tensor_copy(out=o, in_=i), [sk], [name])
                        else:
                            S.op('act', lambda e, o=wt[:, c, n0:n0 + nw], i=sg[:, c, :nw]: e.activation(out=o, in_=i, func=AF.Copy), [sk], [name])
                n0 += nw
                pi += 1
            S.barrier()
            tmpst.close()
            return wt

        def norm_tile(st_tiles, hsrc, t0, tw, slot, zero_cols=None, src_lo=None):
            ht, sq, rstd, aT = st_tiles['ht'][slot], st_tiles['sq'], st_tiles['rstd'], st_tiles['aT'][slot]
            hk, ak = 'ht%d' % slot, 'aT%d' % slot
            S.dma(ht[:, :, :tw], hsrc.rearrange("(c p) t -> p c t", p=128)[:, :, t0:t0 + tw], writes=[hk])
            act(lambda e: e.activation(out=sq[:, :, :tw], in_=ht[:, :, :tw], func=AF.Square), [hk], ['sq'])
            for c in range(8):
                mm(pb[7][:, :tw], onesb[:], sq[:, c, :tw], c == 0, c == 7, ['sq', 'onesb'], ['pb7'])
            act(lambda e: e.activation(out=rstd[:, :tw], in_=pb[7][:, :tw], func=AF.Ln, scale=1.0 / D, bias=EPS), ['pb7'], ['rstd'])
            act(lambda e: e.activation(out=rstd[:, :tw], in_=rstd[:, :tw], func=AF.Exp, scale=-0.5), ['rstd'], ['rstd'])
            dve(lambda e: e.tensor_tensor(out=aT[:, :, :tw], in0=ht[:, :, :tw],
                                          in1=rstd[:, :tw].unsqueeze(1).broadcast_to([128, 8, tw]), op=ALU.mult), [hk, 'rstd'], [ak])
            return ht, aT, hk, ak

        def post_norm_residual(mix, mixk, hres, hresk, wpost, wpostk, tw, sq, rstd, outt, outk, mask_pad):
            act(lambda e: e.activation(out=sq[:, :, :tw], in_=mix[:, :, :tw], func=AF.Square), [mixk], ['sq'])
            for c in range(8):
                mm(pb[7][:, :tw], onesb[:], sq[:, c, :tw], c == 0, c == 7, ['sq', 'onesb'], ['pb7'])
            act(lambda e: e.activation(out=rstd[:, :tw], in_=pb[7][:, :tw], func=AF.Ln, scale=1.0 / D, bias=EPS), ['pb7'], ['rstd'])
            act(lambda e: e.activation(out=rstd[:, :tw], in_=rstd[:, :tw], func=AF.Exp, scale=-0.5), ['rstd'], ['rstd'])
            dve(lambda e: e.tensor_tensor(out=mix[:, :, :tw], in0=mix[:, :, :tw],
                                          in1=rstd[:, :tw].unsqueeze(1).broadcast_to([128, 8, tw]), op=ALU.mult), [mixk, 'rstd'], [mixk])
            for c in range(8):
                dve(lambda e, c=c: e.scalar_tensor_tensor(out=outt[:, c, :tw], in0=mix[:, c, :tw], scalar=wpost[:, c:c + 1],
                                                         in1=hres[:, c, :tw], op0=ALU.mult, op1=ALU.add), [mixk, hresk, wpostk], [outk])
            if mask_pad:
                dve(lambda e: e.tensor_tensor(out=outt[:, :, :tw], in0=outt[:, :, :tw],
                                              in1=padmask[:, :tw].unsqueeze(1).broadcast_to([128, 8, tw]), op=ALU.mult), [outk, 'padmask'], [outk])

        for li in range(depth):
            C = LC[li]
            last_layer = (li == depth - 1)
            with ExitStack() as st:
                W = load_weight_bf16(st, 'W', C['w_in'], D, NCOL, C['npre'])
                tl = {'ht': [T(st, 'ht%d' % i, [128, 8, 512]) for i in range(2)], 'sq': T(st, 'sq', [128, 8, 512], BF16),
                      'rstd': T(st, 'rstd', [128, 512]), 'aT': [T(st, 'aT%d' % i, [128, 8, 512], BF16) for i in range(2)]}
                ev = [T(st, 'ev%d' % i, [128, 512], BF16) for i in range(4)]
                evf = [T(st, 'evf%d' % i, [128, 512]) for i in range(2)]
                cs = [T(st, 'cs%d' % i, [128, 512]) for i in range(2)]
                sn = [T(st, 'sn%d' % i, [128, 512]) for i in range(2)]
                r1 = T(st, 'r1', [128, 512])
                r2 = T(st, 'r2', [128, 512])
                dtb4 = T(st, 'dtb4', [128, 1])
                S.dma(dtb4[:], C['dtb4'][:, :], writes=['dtb4'])
                tme = [T(st, 'tme%d' % i, [128, 1664], BF16) for i in range(2)]
                nev = 0
                npb = 0
                ti = 0
                for s in range(NS):
                    Q = SEQ[s]
                    hsrc = Q['h0'] if li == 0 else Q['h1']
                    for (t0, tw) in tiles_of(TPs[s]):
                        slot = ti % 2
                        ti += 1
                        ht, aT, hk, ak = norm_tile(tl, hsrc, t0, tw, slot)
                        for m in range(20):
                            p = pb[npb % 4]
                            pk = pbk[npb % 4]
                            npb += 1
                            for c in range(8):
                                mm(p[:, :tw], W[:, c, m * 128:(m + 1) * 128], aT[:, c, :tw], c == 0, c == 7, ['W', ak], [pk])
                            e_ = ev[nev % 4]
                            ek = 'ev%d' % (nev % 4)
                            nev += 1
                            if m % 2 == 0:
                                act(lambda e, o=e_, p=p: e.activation(out=o[:, :tw], in_=p[:, :tw], func=AF.Copy), [pk], [ek])
                            else:
                                dve(lambda e, o=e_, p=p: e.tensor_copy(out=o[:, :tw], in_=p[:, :tw]), [pk], [ek])
                            S.dma(Q['fm'][m * 128:(m + 1) * 128, t0:t0 + tw], e_[:, :tw], reads=[ek], writes=['fm%d' % s])
                        cst, snt = cs[slot], sn[slot]
                        S.dma(cst[:, :tw], Q['cos'][:, t0:t0 + tw], writes=['cs%d' % slot])
                        S.dma(snt[:, :tw], Q['sin'][:, t0:t0 + tw], writes=['sn%d' % slot])
                        for m in range(5):
                            ca = C_WQ + m * 128
                            cbb = C_WQS + m * 128
                            pA, pB = pb[4], pb[5]
                            for c in range(8):
                                mm(pA[:, :tw], W[:, c, ca:ca + 128], aT[:, c, :tw], c == 0, c == 7, ['W', ak], ['pb4'])
                            for c in range(8):
                                mm(pB[:, :tw], W[:, c, cbb:cbb + 128], aT[:, c, :tw], c == 0, c == 7, ['W', ak], ['pb5'])
                            dve(lambda e, pA=pA: e.tensor_tensor(out=r1[:, :tw], in0=pA[:, :tw], in1=cst[:, :tw], op=ALU.mult), ['pb4', 'cs%d' % slot], ['r1'])
                            dve(lambda e, pB=pB: e.tensor_tensor(out=r2[:, :tw], in0=pB[:, :tw], in1=snt[:, :tw], op=ALU.mult), ['pb5', 'sn%d' % slot], ['r2'])
                            e_ = ev[nev % 4]
                            ek = 'ev%d' % (nev % 4)
                            nev += 1
                            dve(lambda e, o=e_: e.tensor_tensor(out=o[:, :tw], in0=r1[:, :tw], in1=r2[:, :tw], op=ALU.add), ['r1', 'r2'], [ek])
                            S.dma(Q['fm'][2560 + m * 128:2560 + (m + 1) * 128, t0:t0 + tw], e_[:, :tw], reads=[ek], writes=['fm%d' % s])
                        p = pb[6]
                        for c in range(8):
                            mm(p[:, :tw], W[:, c, C_DT4:C_DT4 + 128], aT[:, c, :tw], c == 0, c == 7, ['W', ak], ['pb6'])
                        ef = evf[slot]
                        efk = 'evf%d' % slot
                        act(lambda e, o=ef, p=p: e.activation(out=o[:, :tw], in_=p[:, :tw], func=AF.Exp, bias=dtb4[:, 0:1]), ['pb6', 'dtb4'], [efk])
                        act(lambda e, o=ef: e.activation(out=o[:, :tw], in_=o[:, :tw], func=AF.Ln, bias=1.0), [efk], [efk])
                        if t0 == 0:
                            dve(lambda e, o=ef: e.tensor_tensor(out=o[:, :tw], in0=o[:, :tw], in1=padmask[:, :tw], op=ALU.mult), [efk, 'padmask'], [efk])
                        S.dma(Q['dt4'][:, t0:t0 + tw], ef[:, :tw], reads=[efk], writes=['dt4%d' % s])
                        for sub in range(tw // 128):
                            te = tme[sub % 2]
                            tk = 'tme%d' % (sub % 2)
                            for (n0, nw) in [(0, 512), (512, 512), (1024, 512), (1536, 128)]:
                                p = pb[npb % 4]
                                pk = pbk[npb % 4]
                                npb += 1
                                for c in range(8):
                                    mm(p[:, :nw], aT[:, c, sub * 128:(sub + 1) * 128], W[:, c, C_Z + n0:C_Z + n0 + nw], c == 0, c == 7, ['W', ak], [pk])
                                if (n0 // 512) % 2 == 0:
                                    act(lambda e, o=te, p=p, n0=n0, nw=nw: e.activation(out=o[:, n0:n0 + nw], in_=p[:, :nw], func=AF.Copy), [pk], [tk])
                                else:
                                    dve(lambda e, o=te, p=p, n0=n0, nw=nw: e.tensor_copy(out=o[:, n0:n0 + nw], in_=p[:, :nw]), [pk], [tk])
                            S.dma(Q['tm'][t0 + sub * 128:t0 + (sub + 1) * 128, :], te[:, :], reads=[tk], writes=['tm%d' % s])
            S.barrier()

            with ExitStack() as st:
                cw = T(st, 'cw', [128, 60])
                cbt = T(st, 'cbt', [128, 12])
                S.dma(cw[:], C['cw'][:, :], writes=['cw'])
                S.dma(cbt[:], C['cb'][:, :], writes=['cbt'])
                xin = [T(st, 'xin%d' % i, [128, 12, 516], BF16) for i in range(2)]
                acc = [T(st, 'acc%d' % i, [128, 512]) for i in range(2)]
                xo = [T(st, 'xo%d' % i, [128, 12, 512], BF16) for i in range(2)]
                xt = [T(st, 'xt%d' % i, [128, 1280], BF16) for i in range(2)]
                ti = 0
                na = 0
                for s in range(NS):
                    Q = SEQ[s]
                    TP = TPs[s]
                    src = Q['fm'][0:1536, :].rearrange("(c p) t -> p c t", p=128)
                    for (t0, tw) in tiles_of(TP):
                        slot = ti % 2
                        ti += 1
                        xi, xik = xin[slot], 'xin%d' % slot
                        lo, hi = max(t0 - 2, 0), min(t0 + tw + 2, TP)
                        if lo != t0 - 2 or hi != t0 + tw + 2:
                            S.op('pool', lambda e, xi=xi: e.memset(xi[:], 0.0), [], [xik])
                        S.dma(xi[:, :, lo - (t0 - 2):hi - (t0 - 2)], src[:, :, lo:hi], writes=[xik])
                        xoo, xok = xo[slot], 'xo%d' % slot
                        for c in range(12):
                            a_, ak_ = acc[na % 2], 'acc%d' % (na % 2)
                            na += 1
                            dve(lambda e, a_=a_, c=c: e.tensor_scalar(out=a_[:, :tw], in0=xi[:, c, 0:tw], scalar1=cw[:, c * 5:c * 5 + 1],
                                                                    scalar2=cbt[:, c:c + 1], op0=ALU.mult, op1=ALU.add), [xik, 'cw', 'cbt'], [ak_])
                            for j in range(1, 5):
                                dve(lambda e, a_=a_, c=c, j=j: e.scalar_tensor_tensor(out=a_[:, :tw], in0=xi[:, c, j:j + tw], scalar=cw[:, c * 5 + j:c * 5 + j + 1],
                                                                                    in1=a_[:, :tw], op0=ALU.mult, op1=ALU.add), [xik, 'cw', ak_], [ak_])
                            act(lambda e, a_=a_, c=c: e.activation(out=xoo[:, c, :tw], in_=a_[:, :tw], func=AF.Silu), [ak_], [xok])
                        S.dma(Q['bct'].rearrange("(c p) t -> p c t", p=128)[:, :, t0:t0 + tw], xoo[:, 8:12, :tw], reads=[xok], writes=['bct%d' % s])
                        for sub in range(tw // 128):
                            xtt, xtk = xt[sub % 2], 'xt%d' % (sub % 2)
                            for half in range(3):
                                cl = [(0, 1, 2, 3), (4, 5, 6, 7), (8, 9)][half]
                                p = pb[half]
                                pv = p[:].bitcast(BF16)
                                for ii, c in enumerate(cl):
                                    tr(pv[:, ii * 128:(ii + 1) * 128], xoo[:, c, sub * 128:(sub + 1) * 128], identb[:], [xok, 'identb'], [pbk[half]])
                                n = len(cl) * 128
                                if half == 1:
                                    act(lambda e, pv=pv, n=n, o=xtt, c0=cl[0]: e.activation(out=o[:, c0 * 128:c0 * 128 + n], in_=pv[:, :n], func=AF.Copy), [pbk[half]], [xtk])
                                else:
                                    dve(lambda e, pv=pv, n=n, o=xtt, c0=cl[0]: e.tensor_copy(out=o[:, c0 * 128:c0 * 128 + n], in_=pv[:, :n]), [pbk[half]], [xtk])
                            S.dma(Q['xs'][t0 + sub * 128:t0 + (sub + 1) * 128, :], xtt[:, 0:1024], reads=[xtk], writes=['xs%d' % s])
                            S.dma(Q['btm'][t0 + sub * 128:t0 + (sub + 1) * 128, :], xtt[:, 1024:1280], reads=[xtk], writes=['btm%d' % s])
            S.barrier()

            with ExitStack() as st:
                selb = T(st, 'selb', [128, 32, 128], BF16)
                maskb_ = [T(st, 'mk%d' % i, [128, 128], BF16) for i in range(2)]
                gco = T(st, 'gco', [128, 4])
                a4 = T(st, 'a4c', [128, 1])
                dsk = T(st, 'dsk', [128, 16])
                snw = T(st, 'snw', [128, 1024])
                with ExitStack() as tmp:
                    stg = T(tmp, 'selstg', [128, 4096])
                    S.dma(stg[:], G['sel'][:, :], writes=['selstg'])
                    dve(lambda e: e.tensor_copy(out=selb[:].rearrange("p a b -> p (a b)"), in_=stg[:]), ['selstg'], ['selb'])
                    S.dma(stg[:, 0:128], G['maskf'][:, :], writes=['selstg'])
                    dve(lambda e: e.tensor_copy(out=maskb_[0][:], in_=stg[:, 0:128]), ['selstg'], ['mk0'])
                    S.dma(stg[:, 0:128], G['maskb'][:, :], writes=['selstg'])
                    dve(lambda e: e.tensor_copy(out=maskb_[1][:], in_=stg[:, 0:128]), ['selstg'], ['mk1'])
                    S.barrier()
                S.dma(gco[:], G['gcoef'][:, :], writes=['gco'])
                S.dma(a4[:], C['alog4'][:, :], writes=['a4c'])
                act(lambda e: e.activation(out=a4[:], in_=a4[:], func=AF.Exp), ['a4c'], ['a4c'])
                dve(lambda e: e.tensor_scalar(out=a4[:], in0=a4[:], scalar1=-1.0, scalar2=None, op0=ALU.mult), ['a4c'], ['a4c'])
                S.dma(dsk[:], C['dskip'].partition_broadcast(128).rearrange("p a b -> p (a b)"), writes=['dsk'])
                S.dma(snw[:], C['snw'].partition_broadcast(128).rearrange("p a b -> p (a b)"), writes=['snw'])
                dtt = [T(st, 'dtt%d' % i, [128, 128]) for i in range(2)]
                a4t = T(st, 'a4t', [128, 128])
                cum = T(st, 'cum', [128, 128])
                Gt = T(st, 'Gt', [128, 128])
                Ghi = T(st, 'Ghi', [128, 128], BF16)
                Glo = T(st, 'Glo', [128, 128], BF16)
                Gtmp = T(st, 'Gtmp', [128, 128])
                cols = T(st, 'cols', [128, 128])
                ncol = T(st, 'ncol', [128, 32])
                scol = T(st, 'scol', [128, 16])
                dcol = T(st, 'dcol', [128, 16])
                cdb = T(st, 'cdb', [128, 16])
                Lm = T(st, 'Lm', [128, 16, 128], BF16)
                CBt = T(st, 'CBt', [128, 2, 128], BF16)
                MT = T(st, 'MT', [128, 16, 128], BF16)
                xs_t = [T(st, 'xs_t%d' % i, [128, 1024], BF16) for i in range(2)]
                b_t = [T(st, 'b_t%d' % i, [128, 256], BF16) for i in range(2)]
                bc_t = [T(st, 'bc_t%d' % i, [128, 4, 128], BF16) for i in range(2)]
                xdt = T(st, 'xdt', [128, 1024], BF16)
                xdd = T(st, 'xdd', [128, 1024], BF16)
                Hs = T(st, 'Hs', [128, 1024])
                Hb = T(st, 'Hb', [128, 1024], BF16)
                yacc = [T(st, 'yacc%d' % i, [128, 1024]) for i in range(2)]
                ytmp = T(st, 'ytmp', [128, 1024])
                yfl = [T(st, 'yfl%d' % i, [128, 1024]) for i in range(2)]
                zt = [T(st, 'zt%d' % i, [128, 1024], BF16) for i in range(2)]
                zs = T(st, 'zs', [128, 1024])
                ssq = T(st, 'ssq', [128, 1])
                ybf = T(st, 'ybf', [128, 1024], BF16)
                yTs = [T(st, 'yTs%d' % i, [128, 8, 128], BF16) for i in range(2)]
                onesrow = T(st, 'onesrow', [128, 128])
                dve(lambda e: e.tensor_copy(out=onesrow[:], in_=onesf[:]), ['onesf'], ['onesrow'])
                bi = 0
                for s in range(NS):
                    Q = SEQ[s]
                    TP = TPs[s]
                    NB = TP // 128
                    for dr in range(2):
                        dve(lambda e: e.memset(Hs[:], 0.0), [], ['Hs'])
                        dve(lambda e: e.memset(Hb[:], 0.0), [], ['Hb'])
                        blocks = range(NB) if dr == 0 else range(NB - 1, -1, -1)
                        hd0 = dr * 16
                        for b in blocks:
                            slot = bi % 2
                            bi += 1
                            c0 = b * 128
                            dt_, dtk = dtt[slot], 'dtt%d' % slot
                            S.dma(dt_[:], Q['dt4'][:, c0:c0 + 128], reads=['dt4%d' % s], writes=[dtk])
                            xst, xsk = xs_t[slot], 'xs_t%d' % slot
                            S.dma(xst[:], Q['xs'][c0:c0 + 128, :], reads=['xs%d' % s], writes=[xsk])
                            bt, bk = b_t[slot], 'b_t%d' % slot
                            S.dma(bt[:], Q['btm'][c0:c0 + 128, :], reads=['btm%d' % s], writes=[bk])
                            bct, bck = bc_t[slot], 'bc_t%d' % slot
                            S.dma(bct[:], Q['bct'].rearrange("(c p) t -> p c t", p=128)[:, :, c0:c0 + 128], reads=['bct%d' % s], writes=[bck])
                            dve(lambda e, dt_=dt_: e.tensor_scalar(out=a4t[:], in0=dt_[:], scalar1=a4[:, 0:1], scalar2=None, op0=ALU.mult), [dtk, 'a4c'], ['a4t'])
                            dve(lambda e: e.tensor_tensor_scan(out=cum[:], data0=onesrow[:], data1=a4t[:], initial=0.0, op0=ALU.mult, op1=ALU.add), ['a4t', 'onesrow'], ['cum'])
                            dve(lambda e: e.tensor_scalar(out=Gtmp[:], in0=cum[:, 127:128].broadcast_to([128, 128]), scalar1=gco[:, 3:4], scalar2=None, op0=ALU.mult), ['cum', 'gco'], ['Gtmp'])
                            dve(lambda e: e.scalar_tensor_tensor(out=Gtmp[:], in0=cum[:], scalar=gco[:, 0:1], in1=Gtmp[:], op0=ALU.mult, op1=ALU.add), ['cum', 'gco', 'Gtmp'], ['Gtmp'])
                            dve(lambda e: e.scalar_tensor_tensor(out=Gtmp[:], in0=a4t[:], scalar=gco[:, 1:2], in1=Gtmp[:], op0=ALU.mult, op1=ALU.add), ['a4t', 'gco', 'Gtmp'], ['Gtmp'])
                            dve(lambda e, dt_=dt_: e.scalar_tensor_tensor(out=Gt[:], in0=dt_[:], scalar=gco[:, 2:3], in1=Gtmp[:], op0=ALU.mult, op1=ALU.add), [dtk, 'gco', 'Gtmp'], ['Gt'])
                            dve(lambda e: e.tensor_copy(out=Ghi[:], in_=Gt[:]), ['Gt'], ['Ghi'])
                            dve(lambda e: e.tensor_tensor(out=Gtmp[:], in0=Gt[:], in1=Ghi[:], op=ALU.subtract), ['Gt', 'Ghi'], ['Gtmp'])
                            dve(lambda e: e.tensor_copy(out=Glo[:], in_=Gtmp[:]), ['Gtmp'], ['Glo'])
                            tr(pb[6][:, 0:128], Gt[:], identf[:], ['Gt', 'identf'], ['pb6'])
                            act(lambda e: e.activation(out=cols[:], in_=pb[6][:, 0:128], func=AF.Copy), ['pb6'], ['cols'])
                            cx0 = 0 if dr == 0 else 48
                            ot0 = 32 if dr == 0 else 16
                            a0 = 64 + hd0
                            d0 = 96 + hd0
                            dve(lambda e, cx0=cx0: e.tensor_scalar(out=ncol[:, 0:16], in0=cols[:, cx0:cx0 + 16], scalar1=-1.0, scalar2=None, op0=ALU.mult), ['cols'], ['ncol'])
                            act(lambda e, cx0=cx0: e.activation(out=scol[:], in_=cols[:, cx0:cx0 + 16], func=AF.Exp), ['cols'], ['scol'])
                            dve(lambda e, ot0=ot0, a0=a0: e.tensor_tensor(out=dcol[:], in0=cols[:, ot0:ot0 + 16], in1=cols[:, a0:a0 + 16], op=ALU.subtract), ['cols'], ['dcol'])
                            act(lambda e: e.activation(out=dcol[:], in_=dcol[:], func=AF.Exp), ['dcol'], ['dcol'])
                            mm(pb[6][:, 256:272], onesf[:], cols[:, a0:a0 + 16], True, True, ['onesf', 'cols'], ['pb6'])
                            act(lambda e: e.activation(out=cdb[:], in_=pb[6][:, 256:272], func=AF.Exp), ['pb6'], ['cdb'])
                            for half in range(2):
                                for hh in range(8):
                                    h = half * 8 + hh
                                    w = hd0 + h
                                    o = pb[half * 2 + hh // 4][:, (hh % 4) * 128:(hh % 4 + 1) * 128]
                                    pk = pbk[half * 2 + hh // 4]
                                    mm(o, selb[:, w, :], Ghi[:], True, False, ['selb', 'Ghi'], [pk])
                                    mm(o, selb[:, w, :], Glo[:], False, False, ['selb', 'Glo'], [pk])
                                    mm(o, identb[:], maskb_[dr][:], False, True, ['identb', 'mk%d' % dr], [pk])
                                    act(lambda e, o=o, h=h: e.activation(out=Lm[:, h, :], in_=o, func=AF.Exp, bias=ncol[:, h:h + 1]), [pk, 'ncol'], ['Lm'])
                            for g in range(2):
                                mm(pb[4][:, g * 128:(g + 1) * 128], bct[:, g, :], bct[:, 2 + g, :], True, True, [bck], ['pb4'])
                            act(lambda e: e.activation(out=CBt[:].rearrange("p a b -> p (a b)"), in_=pb[4][:, 0:256], func=AF.Copy), ['pb4'], ['CBt'])
                            for g in range(2):
                                dve(lambda e, g=g: e.tensor_tensor(out=MT[:, g * 8:(g + 1) * 8, :], in0=Lm[:, g * 8:(g + 1) * 8, :],
                                                                  in1=CBt[:, g:g + 1, :].broadcast_to([128, 8, 128]), op=ALU.mult), ['Lm', 'CBt'], ['MT'])
                            dve(lambda e, xst=xst, d0=d0: e.tensor_tensor(out=xdt[:].rearrange("p (h d) -> p h d", d=64), in0=xst[:].rearrange("p (h d) -> p h d", d=64),
                                                                      in1=cols[:, d0:d0 + 16].unsqueeze(2).broadcast_to([128, 16, 64]), op=ALU.mult), [xsk, 'cols'], ['xdt'])
                            dve(lambda e: e.tensor_tensor(out=xdd[:].rearrange("p (h d) -> p h d", d=64), in0=xdt[:].rearrange("p (h d) -> p h d", d=64),
                                                          in1=dcol[:].unsqueeze(2).broadcast_to([128, 16, 64]), op=ALU.mult), ['xdt', 'dcol'], ['xdd'])
                            for h in range(16):
                                mm(pb[h // 8][:, (h % 8) * 64:(h % 8 + 1) * 64], MT[:, h, :], xdt[:, h * 64:(h + 1) * 64], True, True, ['MT', 'xdt'], [pbk[h // 8]])
                            for g in range(2):
                                mm(pb[2 + g][:, :], bct[:, 2 + g, :], Hb[:, g * 512:(g + 1) * 512], True, True, [bck, 'Hb'], [pbk[2 + g]])
                            ya, yak = yacc[slot], 'yacc%d' % slot
                            for g in range(2):
                                dve(lambda e, g=g: e.tensor_tensor(out=ytmp[:, g * 512:(g + 1) * 512].rearrange("p (h d) -> p h d", d=64),
                                                                  in0=pb[2 + g][:, :].rearrange("p (h d) -> p h d", d=64),
                                                                  in1=scol[:, g * 8:(g + 1) * 8].unsqueeze(2).broadcast_to([128, 8, 64]), op=ALU.mult), [pbk[2 + g], 'scol'], ['ytmp'])
                                dve(lambda e, g=g, ya=ya: e.tensor_tensor(out=ya[:, g * 512:(g + 1) * 512], in0=pb[g][:, :], in1=ytmp[:, g * 512:(g + 1) * 512], op=ALU.add), [pbk[g], 'ytmp'], [yak])
                            for g in range(2):
                                mm(pb[4 + g][:, :], bt[:, g * 128:(g + 1) * 128], xdd[:, g * 512:(g + 1) * 512], True, True, [bk, 'xdd'], [pbk[4 + g]])
                            dve(lambda e: e.tensor_tensor(out=Hs[:].rearrange("p (h d) -> p h d", d=64), in0=Hs[:].rearrange("p (h d) -> p h d", d=64),
                                                          in1=cdb[:].unsqueeze(2).broadcast_to([128, 16, 64]), op=ALU.mult), ['Hs', 'cdb'], ['Hs'])
                            for g in range(2):
                                dve(lambda e, g=g: e.tensor_tensor(out=Hs[:, g * 512:(g + 1) * 512], in0=Hs[:, g * 512:(g + 1) * 512], in1=pb[4 + g][:, :], op=ALU.add), ['Hs', pbk[4 + g]], ['Hs'])
                            act(lambda e: e.activation(out=Hb[:], in_=Hs[:], func=AF.Copy), ['Hs'], ['Hb'])
                            if dr == 0:
                                S.dma(Q['yf'][c0:c0 + 128, :], ya[:], reads=[yak], writes=['yf%d' % s])
                            else:
                                yf_, yfk = yfl[slot], 'yfl%d' % slot
                                S.dma(yf_[:], Q['yf'][c0:c0 + 128, :], reads=['yf%d' % s], writes=[yfk])
                                z_, zk = zt[slot], 'zt%d' % slot
                                S.dma(z_[:], Q['tm'][c0:c0 + 128, 0:1024], reads=['tm%d' % s], writes=[zk])
                                dve(lambda e, ya=ya, yf_=yf_: e.tensor_tensor(out=ya[:], in0=ya[:], in1=yf_[:], op=ALU.add), [yak, yfk], [yak])
                                dve(lambda e, xst=xst: e.tensor_tensor(out=ytmp[:].rearrange("p (h d) -> p h d", d=64), in0=xst[:].rearrange("p (h d) -> p h d", d=64),
                                                                  in1=dsk[:].unsqueeze(2).broadcast_to([128, 16, 64]), op=ALU.mult), [xsk, 'dsk'], ['ytmp'])
                                dve(lambda e, ya=ya: e.tensor_tensor(out=ya[:], in0=ya[:], in1=ytmp[:], op=ALU.add), [yak, 'ytmp'], [yak])
                                act(lambda e, z_=z_: e.activation(out=zs[:], in_=z_[:], func=AF.Silu), [zk], ['zs'])
                                dve(lambda e, ya=ya: e.tensor_tensor(out=ya[:], in0=ya[:], in1=zs[:], op=ALU.mult), [yak, 'zs'], [yak])
                                act(lambda e, ya=ya: e.activation(out=zs[:], in_=ya[:], func=AF.Square, accum_out=ssq[:]), [yak], ['zs', 'ssq'])
                                act(lambda e: e.activation(out=ssq[:], in_=ssq[:], func=AF.Ln, scale=1.0 / 1024, bias=EPS), ['ssq'], ['ssq'])
                                act(lambda e: e.activation(out=ssq[:], in_=ssq[:], func=AF.Exp, scale=-0.5), ['ssq'], ['ssq'])
                                dve(lambda e, ya=ya: e.scalar_tensor_tensor(out=ybf[:], in0=ya[:], scalar=ssq[:, 0:1], in1=snw[:], op0=ALU.mult, op1=ALU.mult), [yak, 'ssq', 'snw'], ['ybf'])
                                yT_, yTk = yTs[slot], 'yTs%d' % slot
                                pv = pb[7][:].bitcast(BF16)
                                for c in range(8):
                                    tr(pv[:, c * 128:(c + 1) * 128], ybf[:, c * 128:(c + 1) * 128], identb[:], ['ybf', 'identb'], ['pb7'])
                                act(lambda e, yT_=yT_, pv=pv: e.activation(out=yT_[:].rearrange("p a b -> p (a b)"), in_=pv[:, :], func=AF.Copy), ['pb7'], [yTk])
                                S.dma(Q['yT'][0:1024, :].rearrange("(c p) t -> p c t", p=128)[:, :, c0:c0 + 128], yT_[:], reads=[yTk], writes=['yT%d' % s])
            S.barrier()

            with ExitStack() as st:
                Ena = T(st, 'Ena', [128, 6, 8 * 6 * 128], BF16)
                Ewin = T(st, 'Ewin', [128, 4, 128], BF16)
                esink = T(st, 'esink', [128, 8])
                with ExitStack() as tmp:
                    stg = [T(tmp, 'nastg%d' % i, [128, 3072]) for i in range(2)]
                    k = 0
                    for v in range(6):
                        for hf in range(2):
                            sg, sk = stg[k % 2], 'nastg%d' % (k % 2)
                            k += 1
                            S.dma(sg[:], C['natab'][v, :, hf * 3072:(hf + 1) * 3072], writes=[sk])
                            act(lambda e, sg=sg, v=v, hf=hf: e.activation(out=Ena[:, v, hf * 3072:(hf + 1) * 3072], in_=sg[:], func=AF.Exp), [sk], ['Ena'])
                    S.dma(stg[0][:, 0:512], G['wintab'][:, :], writes=['nastg0'])
                    act(lambda e: e.activation(out=Ewin[:].rearrange("p a b -> p (a b)"), in_=stg[0][:, 0:512], func=AF.Exp), ['nastg0'], ['Ewin'])
                    S.dma(esink[:], C['sink'].partition_broadcast(128).rearrange("p a b -> p (a b)"), writes=['esink'])
                    act(lambda e: e.activation(out=esink[:], in_=esink[:], func=AF.Exp), ['esink'], ['esink'])
                    S.barrier()
                qn = [T(st, 'qn%d' % i, [128, 4, 128], BF16) for i in range(2)]
                qw = [T(st, 'qw%d' % i, [128, 4, 128], BF16) for i in range(2)]
                kn = [T(st, 'kn%d' % i, [128, 4, 6, 128], BF16) for i in range(2)]
                kw = [T(st, 'kw%d' % i, [128, 2, 4, 128], BF16) for i in range(2)]
                vn = [T(st, 'vn%d' % i, [128, 6, 8, 65], BF16) for i in range(2)]
                vw = [T(st, 'vw%d' % i, [128, 4, 2, 65], BF16) for i in range(2)]
                for i in range(2):
                    S.op('pool', lambda e, t=vn[i]: e.memset(t[:], 1.0), [], ['vn%d' % i])
                    S.op('pool', lambda e, t=vw[i]: e.memset(t[:], 1.0), [], ['vw%d' % i])
                Pt = [T(st, 'Pt%d' % i, [128, 768], BF16) for i in range(2)]
                P2 = [T(st, 'P2%d' % i, [128, 768], BF16) for i in range(2)]
                rec = [T(st, 'rec%d' % i, [128, 1]) for i in range(2)]
                yat = [T(st, 'yat%d' % i, [128, 1024], BF16) for i in range(2)]
                yTa = [T(st, 'yTa%d' % i, [128, 8, 128], BF16) for i in range(2)]
                qi = 0
                ui = 0
                for s in range(NS):
                    Q = SEQ[s]
                    TP = TPs[s]
                    NB = TP // 128
                    fmv = Q['fm'].rearrange("(c p) t -> p c t", p=128)
                    for qb in range(NB):
                        slot = qi % 2
                        qi += 1
                        c0 = qb * 128
                        if qb == 0:
                            vi, kbs = 5, [1, 2, 3, 4]
                        elif qb == 1:
                            vi, kbs = 1, [1, 2, 3, 4]
                        elif qb == 2:
                            vi, kbs = 2, [1, 2, 3, 4]
                        elif qb == NB - 1:
                            vi, kbs = 4, [qb - 3, qb - 2, qb - 1, qb]
                        elif qb == NB - 2:
                            vi, kbs = 3, [qb - 2, qb - 1, qb, qb + 1]
                        else:
                            vi, kbs = 0, [qb - 2, qb - 1, qb, qb + 1, qb + 2]
                        na_slots = [(si, kb) for si, kb in enumerate(kbs)] + [(5, 0)]
                        if qb == 0:
                            w_slots = [(2, 1), (3, 0)]
                        else:
                            w_slots = ([(0, qb - 1)] if qb >= 2 else []) + [(1, qb)] + ([(2, qb + 1)] if qb + 1 < NB else []) + [(3, 0)]
                        qn_, qnk = qn[slot], 'qn%d' % slot
                        qw_, qwk = qw[slot], 'qw%d' % slot
                        kn_, knk = kn[slot], 'kn%d' % slot
                        kw_, kwk = kw[slot], 'kw%d' % slot
                        vn_, vnk = vn[slot], 'vn%d' % slot
                        vw_, vwk = vw[slot], 'vw%d' % slot
                        S.dma(qn_[:], fmv[:, 12:16, c0:c0 + 128], reads=['fm%d' % s], writes=[qnk])
                        S.dma(qw_[:], fmv[:, 20:24, c0:c0 + 128], reads=['fm%d' % s], writes=[qwk])
                        for (si, kb) in na_slots:
                            S.dma(kn_[:, :, si, :], fmv[:, 16:20, kb * 128:(kb + 1) * 128], reads=['fm%d' % s], writes=[knk])
                            S.dma(vn_[:, si, :, 0:64], Q['tm'][kb * 128:(kb + 1) * 128, 1152:1664].rearrange("t (h d) -> t h d", d=64), reads=['tm%d' % s], writes=[vnk])
                        for (si, kb) in w_slots:
                            for g in range(2):
                                for hf in range(2):
                                    S.dma(kw_[hf * 64:(hf + 1) * 64, g, si, :], Q['fm'][3072 + g * 64:3072 + (g + 1) * 64, kb * 128:(kb + 1) * 128], reads=['fm%d' % s], writes=[kwk])
                            S.dma(vw_[:, si, :, 0:64], Q['tm'][kb * 128:(kb + 1) * 128, 1024:1152].rearrange("t (h d) -> t h d", d=64), reads=['tm%d' % s], writes=[vwk])
                        ya_, yk_ = yat[slot], 'yat%d' % slot
                        for hu in range(16):
                            u = ui % 2
                            ui += 1
                            is_win = hu < 8
                            h = hu if is_win else hu - 8
                            slots = w_slots if is_win else na_slots
                            ns = len(slots)
                            pS = [pb[u * 2], pb[u * 2 + 1]]
                            pSk = [pbk[u * 2], pbk[u * 2 + 1]]
                            pO, pOk = pb[4 + u], pbk[4 + u]
                            hp = (h % 2) * 64
                            for j, (si, kb) in enumerate(slots):
                                o = pS[j // 4][:, (j % 4) * 128:(j % 4 + 1) * 128]
                                if is_win:
                                    g = h // 4
                                    mm(o, kw_[hp:hp + 64, g, si, :], qw_[hp:hp + 64, h // 2, :], True, True, [kwk, qwk], [pSk[j // 4]])
                                else:
                                    mm(o, kn_[hp:hp + 64, h // 2, si, :], qn_[hp:hp + 64, h // 2, :], True, True, [knk, qnk], [pSk[j // 4]])
                            P_, Pk = Pt[u], 'Pt%d' % u
                            P2_, P2k = P2[u], 'P2%d' % u
                            n1 = min(ns, 4) * 128
                            act(lambda e, P_=P_, p=pS[0], n1=n1: e.activation(out=P_[:, 0:n1], in_=p[:, 0:n1], func=AF.Exp, scale=0.125), [pSk[0]], [Pk])
                            if ns > 4:
                                n2 = (ns - 4) * 128
                                act(lambda e, P_=P_, p=pS[1], n2=n2: e.activation(out=P_[:, 512:512 + n2], in_=p[:, 0:n2], func=AF.Exp, scale=0.125), [pSk[1]], [Pk])
                            for j, (si, kb) in enumerate(slots):
                                if is_win:
                                    E = Ewin[:, si, :]
                                    ek = 'Ewin'
                                else:
                                    E = Ena[:, vi, (h * 6 + si) * 128:(h * 6 + si + 1) * 128]
                                    ek = 'Ena'
                                dve(lambda e, P2_=P2_, P_=P_, j=j, E=E: e.tensor_tensor(out=P2_[:, j * 128:(j + 1) * 128], in0=P_[:, j * 128:(j + 1) * 128], in1=E, op=ALU.mult), [Pk, ek], [P2k])
                            for j, (si, kb) in enumerate(slots):
                                if is_win:
                                    V = vw_[:, si, h // 4, :]
                                    vk = vwk
                                else:
                                    V = vn_[:, si, h, :]
                                    vk = vnk
                                mm(pO[:, 0:65], P2_[:, j * 128:(j + 1) * 128], V, j == 0, j == ns - 1, [P2k, vk], [pOk])
                            r_, rk = rec[u], 'rec%d' % u
                            if is_win:
                                dve(lambda e, r_=r_, pO=pO, h=h: e.tensor_tensor(out=r_[:], in0=pO[:, 64:65], in1=esink[:, h:h + 1], op=ALU.add), [pOk, 'esink'], [rk])
                                dve(lambda e, r_=r_: e.reciprocal(out=r_[:], in_=r_[:]), [rk], [rk])
                            else:
                                dve(lambda e, r_=r_, pO=pO: e.reciprocal(out=r_[:], in_=pO[:, 64:65]), [pOk], [rk])
                            act(lambda e, ya_=ya_, pO=pO, r_=r_, hu=hu: e.activation(out=ya_[:, hu * 64:(hu + 1) * 64], in_=pO[:, 0:64], func=AF.Copy, scale=r_[:, 0:1]), [pOk, rk], [yk_])
                        yT_, yTk = yTa[slot], 'yTa%d' % slot
                        pv = pb[6 + slot][:].bitcast(BF16)
                        for c in range(8):
                            tr(pv[:, c * 128:(c + 1) * 128], ya_[:, c * 128:(c + 1) * 128], identb[:], [yk_, 'identb'], [pbk[6 + slot]])
                        dve(lambda e, yT_=yT_, pv=pv: e.tensor_copy(out=yT_[:].rearrange("p a b -> p (a b)"), in_=pv[:, :]), [pbk[6 + slot]], [yTk])
                        S.dma(Q['yT'][1024:2048, :].rearrange("(c p) t -> p c t", p=128)[:, :, c0:c0 + 128], yT_[:], reads=[yTk], writes=['yT%d' % s])
            S.barrier()

            with ExitStack() as st:
                Wo = load_weight_bf16(st, 'Wo', C['w_out'], 2048, D, None)
                npost = T(st, 'npost', [128, 8])
                S.dma(npost[:], C['npost'][:, :], writes=['npost'])
                yin = [T(st, 'yin%d' % i, [128, 16, 512], BF16) for i in range(2)]
                hres = [T(st, 'hres%d' % i, [128, 8, 512]) for i in range(2)]
                mix = T(st, 'mix', [128, 8, 512])
                sq = T(st, 'sq', [128, 8, 512], BF16)
                rstd = T(st, 'rstd', [128, 512])
                hout = [T(st, 'hout%d' % i, [128, 8, 512]) for i in range(2)]
                ti = 0
                npb = 0
                for s in range(NS):
                    Q = SEQ[s]
                    hsrc = Q['h0'] if li == 0 else Q['h1']
                    for (t0, tw) in tiles_of(TPs[s]):
                        slot = ti % 2
                        ti += 1
                        yi, yik = yin[slot], 'yin%d' % slot
                        S.dma(yi[:, :, :tw], Q['yT'].rearrange("(c p) t -> p c t", p=128)[:, :, t0:t0 + tw], reads=['yT%d' % s], writes=[yik])
                        hr, hrk = hres[slot], 'hres%d' % slot
                        S.dma(hr[:, :, :tw], hsrc.rearrange("(c p) t -> p c t", p=128)[:, :, t0:t0 + tw], writes=[hrk])
                        for m in range(8):
                            p, pk = pb[npb % 4], pbk[npb % 4]
                            npb += 1
                            for c in range(16):
                                mm(p[:, :tw], Wo[:, c, m * 128:(m + 1) * 128], yi[:, c, :tw], c == 0, c == 15, ['Wo', yik], [pk])
                            if m % 2 == 0:
                                act(lambda e, p=p, m=m: e.activation(out=mix[:, m, :tw], in_=p[:, :tw], func=AF.Copy), [pk], ['mix'])
                            else:
                                dve(lambda e, p=p, m=m: e.tensor_copy(out=mix[:, m, :tw], in_=p[:, :tw]), [pk], ['mix'])
                        ho, hok = hout[slot], 'hout%d' % slot
                        post_norm_residual(mix, 'mix', hr, hrk, npost, 'npost', tw, sq, rstd, ho, hok, t0 == 0)
                        S.dma(Q['hmid'].rearrange("(c p) t -> p c t", p=128)[:, :, t0:t0 + tw], ho[:, :, :tw], reads=[hok], writes=['hmid%d' % s])
            S.barrier()

            with ExitStack() as st:
                Wu = load_weight_bf16(st, 'Wu', C['w_up'], D, 2 * DFF, C['fpre'])
                fcw = T(st, 'fcw', [128, 132])
                fcb = T(st, 'fcb', [128, 44])
                S.dma(fcw[:], C['fcw'][:, :], writes=['fcw'])
                S.dma(fcb[:], C['fcb'][:, :], writes=['fcb'])
                tl = {'ht': [T(st, 'ht%d' % i, [128, 8, 512]) for i in range(2)], 'sq': T(st, 'sq', [128, 8, 512], BF16),
                      'rstd': T(st, 'rstd', [128, 512]), 'aT': [T(st, 'aT%d' % i, [128, 8, 512], BF16) for i in range(2)]}
                gpre = [T(st, 'gpre%d' % i, [128, 512]) for i in range(4)]
                gc = [T(st, 'gc%d' % i, [128, 512]) for i in range(4)]
                t3 = [T(st, 't3%d' % i, [128, 512]) for i in range(2)]
                ao = [T(st, 'ao%d' % i, [128, 512], BF16) for i in range(2)]
                ti = 0
                npb = 0
                ng = 0
                nt = 0
                for s in range(NS):
                    Q = SEQ[s]
                    TP = TPs[s]
                    t0 = 0
                    while t0 < TP:
                        tw = min(510, TP - t0)
                        slot = ti % 2
                        ti += 1
                        lo, hi = max(t0 - 1, 0), min(t0 + tw + 1, TP)
                        iw = tw + 2
                        ht_, aT, sq_, rstd_ = tl['ht'][slot], tl['aT'][slot], tl['sq'], tl['rstd']
                        hk, ak = 'ht%d' % slot, 'aT%d' % slot
                        if lo != t0 - 1 or hi != t0 + tw + 1:
                            S.op('pool', lambda e, ht_=ht_: e.memset(ht_[:], 0.0), [], [hk])
                        S.dma(ht_[:, :, lo - (t0 - 1):hi - (t0 - 1)], Q['hmid'].rearrange("(c p) t -> p c t", p=128)[:, :, lo:hi], reads=['hmid%d' % s], writes=[hk])
                        act(lambda e, ht_=ht_, iw=iw: e.activation(out=sq_[:, :, :iw], in_=ht_[:, :, :iw], func=AF.Square), [hk], ['sq'])
                        for c in range(8):
                            mm(pb[7][:, :iw], onesb[:], sq_[:, c, :iw], c == 0, c == 7, ['sq', 'onesb'], ['pb7'])
                        act(lambda e, iw=iw: e.activation(out=rstd_[:, :iw], in_=pb[7][:, :iw], func=AF.Ln, scale=1.0 / D, bias=EPS), ['pb7'], ['rstd'])
                        act(lambda e, iw=iw: e.activation(out=rstd_[:, :iw], in_=rstd_[:, :iw], func=AF.Exp, scale=-0.5), ['rstd'], ['rstd'])
                        dve(lambda e, ht_=ht_, aT=aT, iw=iw: e.tensor_tensor(out=aT[:, :, :iw], in0=ht_[:, :, :iw],
                                                                            in1=rstd_[:, :iw].unsqueeze(1).broadcast_to([128, 8, iw]), op=ALU.mult), [hk, 'rstd'], [ak])
                        for jj in range(22):
                            gcs = []
                            for which in range(2):
                                m = jj + 22 * which
                                p, pk = pb[npb % 4], pbk[npb % 4]
                                npb += 1
                                for c in range(8):
                                    mm(p[:, :iw], Wu[:, c, m * 128:(m + 1) * 128], aT[:, c, :iw], c == 0, c == 7, ['Wu', ak], [pk])
                                gp, gpk = gpre[ng % 4], 'gpre%d' % (ng % 4)
                                g_, gk = gc[ng % 4], 'gc%d' % (ng % 4)
                                ng += 1
                                act(lambda e, gp=gp, p=p, iw=iw: e.activation(out=gp[:, :iw], in_=p[:, :iw], func=AF.Copy), [pk], [gpk])
                                eng = 'dve' if which == 0 else 'pool'
                                S.op(eng, lambda e, g_=g_, gp=gp, m=m, tw=tw: e.tensor_scalar(out=g_[:, :tw], in0=gp[:, 0:tw], scalar1=fcw[:, m * 3:m * 3 + 1], scalar2=fcb[:, m:m + 1],
                                                                                        op0=ALU.mult, op1=ALU.add), [gpk, 'fcw', 'fcb'], [gk])
                                for j in (1, 2):
                                    dve(lambda e, g_=g_, gp=gp, m=m, j=j, tw=tw: e.scalar_tensor_tensor(out=g_[:, :tw], in0=gp[:, j:j + tw], scalar=fcw[:, m * 3 + j:m * 3 + j + 1],
                                                                                                  in1=g_[:, :tw], op0=ALU.mult, op1=ALU.add), [gpk, 'fcw', gk], [gk])
                                gcs.append((g_, gk))
                            (gg, ggk), (gu, guk) = gcs
                            t_, tk = t3[nt % 2], 't3%d' % (nt % 2)
                            a_, ak2 = ao[nt % 2], 'ao%d' % (nt % 2)
                            nt += 1
                            act(lambda e, t_=t_, gg=gg, tw=tw: e.activation(out=t_[:, :tw], in_=gg[:, :tw], func=AF.Square), [ggk], [tk])
                            S.op('pool', lambda e, t_=t_, tw=tw: e.tensor_scalar(out=t_[:, :tw], in0=t_[:, :tw], scalar1=0.044715, scalar2=1.0, op0=ALU.mult, op1=ALU.add), [tk], [tk])
                            S.op('pool', lambda e, t_=t_, gg=gg, tw=tw: e.tensor_tensor(out=t_[:, :tw], in0=t_[:, :tw], in1=gg[:, :tw], op=ALU.mult), [tk, ggk], [tk])
                            act(lambda e, t_=t_, tw=tw: e.activation(out=t_[:, :tw], in_=t_[:, :tw], func=AF.Sigmoid, scale=1.5957691216057308), [tk], [tk])
                            S.op('pool', lambda e, t_=t_, gg=gg, tw=tw: e.tensor_tensor(out=t_[:, :tw], in0=t_[:, :tw], in1=gg[:, :tw], op=ALU.mult), [tk, ggk], [tk])
                            dve(lambda e, t_=t_, gu=gu, a_=a_, tw=tw: e.tensor_tensor(out=a_[:, :tw], in0=t_[:, :tw], in1=gu[:, :tw], op=ALU.mult), [tk, guk], [ak2])
                            S.dma(Q['act'][jj * 128:(jj + 1) * 128, t0:t0 + tw], a_[:, :tw], reads=[ak2], writes=['actd%d' % s])
                        t0 += tw
            S.barrier()

            with ExitStack() as st:
                Wd = load_weight_bf16(st, 'Wd', C['w_down'], DFF, D, None)
                fpost = T(st, 'fpost', [128, 8])
                S.dma(fpost[:], C['fpost'][:, :], writes=['fpost'])
                ain = [T(st, 'ain%d' % i, [128, 22, 512], BF16) for i in range(2)]
                hres = [T(st, 'hres%d' % i, [128, 8, 512]) for i in range(2)]
                mix = T(st, 'mix', [128, 8, 512])
                sq = T(st, 'sq', [128, 8, 512], BF16)
                rstd = T(st, 'rstd', [128, 512])
                hout = [T(st, 'hout%d' % i, [128, 8, 512]) for i in range(2)]
                ti = 0
                npb = 0
                for s in range(NS):
                    Q = SEQ[s]
                    for (t0, tw) in tiles_of(TPs[s]):
                        slot = ti % 2
                        ti += 1
                        ai, aik = ain[slot], 'ain%d' % slot
                        S.dma(ai[:, :, :tw], Q['act'].rearrange("(c p) t -> p c t", p=128)[:, :, t0:t0 + tw], reads=['actd%d' % s], writes=[aik])
                        hr, hrk = hres[slot], 'hres%d' % slot
                        S.dma(hr[:, :, :tw], Q['hmid'].rearrange("(c p) t -> p c t", p=128)[:, :, t0:t0 + tw], reads=['hmid%d' % s], writes=[hrk])
                        for m in range(8):
                            p, pk = pb[npb % 4], pbk[npb % 4]
                            npb += 1
                            for c in range(22):
                                mm(p[:, :tw], Wd[:, c, m * 128:(m + 1) * 128], ai[:, c, :tw], c == 0, c == 21, ['Wd', aik], [pk])
                            if m % 2 == 0:
                                act(lambda e, p=p, m=m: e.activation(out=mix[:, m, :tw], in_=p[:, :tw], func=AF.Copy), [pk], ['mix'])
                            else:
                                dve(lambda e, p=p, m=m: e.tensor_copy(out=mix[:, m, :tw], in_=p[:, :tw]), [pk], ['mix'])
                        ho, hok = hout[slot], 'hout%d' % slot
                        post_norm_residual(mix, 'mix', hr, hrk, fpost, 'fpost', tw, sq, rstd, ho, hok, t0 == 0)
                        if last_layer:
                            lo = max(t0, 128)
                            if lo < t0 + tw:
                                S.dma(Q['out'].rearrange("(c p) t -> p c t", p=128)[:, :, lo - 128:t0 + tw - 128], ho[:, :, lo - t0:tw], reads=[hok], writes=['out%d' % s], final=True)
                        else:
                            S.dma(Q['h1'].rearrange("(c p) t -> p c t", p=128)[:, :, t0:t0 + tw], ho[:, :, :tw], reads=[hok], writes=['h1%d' % s])
            S.barrier()

        with nc.Block() as block:
            S.emit(block)
        nops = S.nops
    return nc, nops


def make_seq_inputs(x, meta):
    L = x.shape[0]
    h = np.zeros((D, L + 128), np.float32)
    h[:, PADL:128] = meta.T
    h[:, 128:] = x.T
    return h


def run(seqs_per_core, P, depth, debug=False):
    ncores = len(seqs_per_core)
    seq_lens = [x.shape[0] for x in seqs_per_core[0]]
    nc, nops = build(seq_lens, depth, debug)
    g = _host_globals()
    lcs = [_host_layer_consts(i, P) for i in range(depth)]
    in_maps = []
    for c in range(ncores):
        m = dict(g)
        for i in range(depth):
            for k, v in lcs[i].items():
                m['L%d_%s' % (i, k)] = v
        for s, x in enumerate(seqs_per_core[c]):
            m['s%d_h0' % s] = make_seq_inputs(x, P['meta_tokens'])
            cos, sin = _rope_tables(x.shape[0] + 128)
            m['s%d_cos' % s] = cos
            m['s%d_sin' % s] = sin
        in_maps.append(m)
    res = run_bass_kernel_spmd(nc, in_maps, core_ids=list(range(ncores)))
    return res, nops


def kernel(**inputs):
    P = {k: np.asarray(v, dtype=np.float32) for k, v in inputs.items()}
    xp, xsm = P['x_prompt'], P['x_sample']
    depth = P['w_in'].shape[0]
    nb, ns = xp.shape[0], xsm.shape[0]
    seqs = [[xp[c % nb], xsm[c % ns]] for c in range(8)]
    res, _ = run(seqs, P, depth)
    yp = np.stack([np.ascontiguousarray(res.results[c]['s0_out'].T) for c in range(nb)], axis=0)
    ys = np.stack([np.ascontiguousarray(res.results[c]['s1_out'].T) for c in range(ns)], axis=0)
    return (yp.astype(np.float32), ys.astype(np.float32))
```

```python
import math
from contextlib import ExitStack
import numpy as np
import concourse.bass as bass
import concourse.mybir as mybir
from concourse.bass_utils import run_bass_kernel_spmd

F32 = mybir.dt.float32
BF16 = mybir.dt.bfloat16
AF = mybir.ActivationFunctionType
ALU = mybir.AluOpType

D = 1024
NMETA = 16
PADL = 112
NEG = -30000.0
EPS = 1e-6
DFF = 2816
NCOL = 5632
C_XBC, C_NQ, C_NK, C_WQ, C_WK, C_WQS, C_WKS, C_DT4, C_Z, C_WV, C_NV = 0, 1536, 2048, 2560, 3072, 3200, 3712, 3840, 3968, 4992, 5120


class _Rec:
    def __init__(self):
        self.call = None

    def __getattr__(self, name):
        def f(*a, **kw):
            self.call = (name, a, kw)
            return None
        return f


class Sched:
    COMPUTE = ('pe', 'act', 'dve', 'pool')
    KDMA = 8

    def __init__(self, nc, dma_queues=('sp', 'pool')):
        self.nc = nc
        self.eng = {'pe': nc.tensor, 'act': nc.scalar, 'dve': nc.vector, 'pool': nc.gpsimd, 'sp': nc.sync}
        self.ops = {e: [] for e in self.eng}
        self.sem = {}
        self.count = {e: 0 for e in self.COMPUTE}
        self.dma_n = {q: 0 for q in dma_queues}
        self.dma_sems = {}
        self.dma_last = {}
        self.waited = {e: {} for e in self.eng}
        self.lastw = {}
        self.readers = {}
        self.final = []
        self.nops = 0
        self.rr = 0

    def alloc(self, stack):
        nc = self.nc
        for e in self.COMPUTE:
            self.sem[e] = stack.enter_context(nc.semaphore('s_' + e))
        for q in self.dma_n:
            self.dma_sems[q] = [stack.enter_context(nc.semaphore('d_%s_%d' % (q, i))) for i in range(self.KDMA)]

    def _need(self, eng, ev, waits):
        if ev is None:
            return
        sem, val, src = ev
        if src == 'pe' and eng == 'pe':
            return
        key = id(sem)
        if self.waited[eng].get(key, 0) >= val:
            return
        cur = waits.get(key)
        if cur is None or cur[1] < val:
            waits[key] = (sem, val)

    def _deps(self, eng, reads, writes):
        waits = {}
        for b in reads:
            self._need(eng, self.lastw.get(b), waits)
        for b in writes:
            self._need(eng, self.lastw.get(b), waits)
            for ev in self.readers.get(b, ()):
                self._need(eng, ev, waits)
        for key, (sem, val) in waits.items():
            self.waited[eng][key] = val
        return list(waits.values())

    def _commit(self, ev, reads, writes):
        for b in reads:
            r = self.readers.setdefault(b, [])
            r.append(ev)
            if len(r) > 64:
                best = {}
                for e in r:
                    k = id(e[0])
                    if k not in best or best[k][1] < e[1]:
                        best[k] = e
                self.readers[b] = list(best.values())
        for b in writes:
            self.lastw[b] = ev
            self.readers[b] = []

    def op(self, eng, fn, reads=(), writes=()):
        rec = _Rec()
        fn(rec)
        name, a, kw = rec.call
        fn = (lambda e, name=name, a=a, kw=kw: getattr(e, name)(*a, **kw))
        waits = self._deps(eng, reads, writes)
        self.count[eng] += 1
        ev = (self.sem[eng], self.count[eng], eng)
        self.ops[eng].append((waits, fn, self.sem[eng], 1))
        self._commit(ev, reads, writes)
        self.nops += 1
        return ev

    def dma(self, out, in_, reads=(), writes=(), final=False, q=None):
        if q is None:
            q = ('sp', 'pool')[self.rr % 2]
            self.rr += 1
        j = self.dma_n[q]
        self.dma_n[q] += 1
        sem = self.dma_sems[q][j % self.KDMA]
        val = 16 * (j // self.KDMA + 1)
        waits = self._deps(q, reads, writes)
        if val > 16:
            key = id(sem)
            if self.waited[q].get(key, 0) < val - 16:
                waits.append((sem, val - 16))
                self.waited[q][key] = val - 16
        ev = (sem, val, 'dma')
        self.dma_last[id(sem)] = ev
        self.ops[q].append((waits, lambda e, o=out, i=in_: e.dma_start(out=o, in_=i), sem, 16))
        self._commit(ev, reads, writes)
        if final:
            self.final.append(ev)
        self.nops += 1
        return ev

    def coll(self, fn, reads=(), writes=()):
        q = 'pool'
        j = self.dma_n[q]
        self.dma_n[q] += 1
        sem = self.dma_sems[q][j % self.KDMA]
        val = 16 * (j // self.KDMA + 1)
        waits = self._deps(q, reads, writes)
        if val > 16:
            key = id(sem)
            if self.waited[q].get(key, 0) < val - 16:
                waits.append((sem, val - 16))
                self.waited[q][key] = val - 16
        ev = (sem, val, 'dma')
        self.dma_last[id(sem)] = ev
        self.ops[q].append((waits, fn, sem, 16))
        self._commit(ev, reads, writes)
        self.nops += 1
        return ev

    def barrier(self):
        evs = [(self.sem[e], self.count[e], e) for e in self.COMPUTE if self.count[e] > 0]
        evs += list(self.dma_last.values())
        for eng in self.eng:
            waits = []
            for (sem, val, src) in evs:
                if self.waited[eng].get(id(sem), 0) < val:
                    waits.append((sem, val))
                    self.waited[eng][id(sem)] = val
            if waits:
                self.ops[eng].append((waits, None, None, 0))
        self.lastw = {}
        self.readers = {}

    def emit(self, block):
        finals = list(self.final)

        def body_for(name):
            ops = self.ops[name]

            def body(e):
                for waits, fn, sem, inc in ops:
                    for (ws, wv) in waits:
                        e.wait_ge(ws, wv)
                    if fn is not None:
                        fn(e).then_inc(sem, inc)
                if name == 'sp':
                    for (s, v, _) in finals:
                        e.wait_ge(s, v)
            return body

        block.sync(body_for('sp'))
        block.tensor(body_for('pe'))
        block.scalar(body_for('act'))
        block.vector(body_for('dve'))
        block.gpsimd(body_for('pool'))


def _na_table_block(rpb, i, R, offsets):
    H = rpb.shape[0]
    out = np.full((128, H, len(offsets), 128), NEG, np.float32)
    q = np.arange(128)
    qr, qc = q // 64, q % 64
    r = 2 * i + qr
    rs = np.clip(r - 4, 0, R - 8)
    qcs = np.clip(qc - 8, 0, 64 - 16)
    k = np.arange(128)
    kr_l, kc = k // 64, k % 64
    for si, off in enumerate(offsets):
        kb = i + off
        krow = 2 * kb + kr_l
        ok = (krow[:, None] >= rs[None, :]) & (krow[:, None] < rs[None, :] + 8) & \
             (kc[:, None] >= qcs[None, :]) & (kc[:, None] < qcs[None, :] + 16)
        dr = np.clip(krow[:, None] - r[None, :] + 7, 0, 14)
        dc = np.clip(kc[:, None] - qc[None, :] + 15, 0, 30)
        for h in range(H):
            b = rpb[h][dr, dc]
            out[:, h, si, :] = np.where(ok, b, NEG)
    return out


NA_VARIANTS = {
    'int': [-2, -1, 0, 1, 2], 'top1': [0, 1, 2, 3], 'top2': [-1, 0, 1, 2], 'bot2': [-2, -1, 0, 1], 'bot1': [-3, -2, -1, 0],
    'metaq': [0, 1, 2, 3]}
NA_VORDER = ['int', 'top1', 'top2', 'bot2', 'bot1', 'metaq']


def _na_tables(rpb, mb):
    H = rpb.shape[0]
    tabs = np.full((6, 128, H, 6, 128), NEG, np.float32)
    R = 32
    NR = R // 2
    reps = {'int': 5, 'top1': 0, 'top2': 1, 'bot2': NR - 2, 'bot1': NR - 1}
    for vi, v in enumerate(NA_VORDER):
        offs = NA_VARIANTS[v]
        if v != 'metaq':
            tabs[vi, :, :, :len(offs), :] = _na_table_block(rpb, reps[v], R, offs)
        else:
            k = np.arange(128)
            for si in range(4):
                krow = 2 * si + k // 64
                kc = k % 64
                ok = (kc < 16) & (krow < 8)
                for h in range(H):
                    b = rpb[h][np.clip(7 + krow, 0, 14), np.clip(15 + kc, 0, 30)]
                    tabs[vi, :, h, si, :] = np.where(ok, b, NEG)[:, None]
        for h in range(H):
            col = np.full(128, NEG, np.float32)
            col[PADL:] = mb[h]
            tabs[vi, :, h, 5, :] = col[:, None]
    return tabs


def _win_tables():
    t = np.full((128, 4, 128), NEG, np.float32)
    k = np.arange(128)[:, None]
    q = np.arange(128)[None, :]
    t[:, 0, :] = np.where(k >= q, 0.0, NEG)
    t[:, 1, :] = 0.0
    t[:, 2, :] = np.where(k <= q, 0.0, NEG)
    t[:, 3, :] = np.where(k >= PADL, 0.0, NEG) + 0 * q
    return t


def _chunked(v, n):
    return np.ascontiguousarray(v.reshape(n, 128).T)


def _host_layer_consts(i, P):
    w_in = P['w_in'][i]
    z, xbc, dtr, wq, wk, wv, nq, nk, nv = np.split(w_in, np.cumsum([1024, 1536, 32, 512, 128, 128, 512, 512])[:].tolist(), axis=1)

    def swap(w):
        w = w.reshape(w.shape[0], -1, 64).copy()
        a = w[:, :, 0:8].copy()
        w[:, :, 0:8] = w[:, :, 8:16]
        w[:, :, 8:16] = a
        return w.reshape(w.shape[0], -1)
    dt4 = np.concatenate([dtr] * 4, axis=1)
    w_my = np.concatenate([xbc, nq, nk, wq, wk, swap(wq), swap(wk), dt4, z, wv, nv], axis=1)
    assert w_my.shape[1] == NCOL
    c = {}
    c['w_in'] = np.ascontiguousarray(w_my)
    c['npre'] = _chunked(P['norm_mix_pre'][i], 8)
    c['npost'] = _chunked(P['norm_mix_post'][i], 8)
    c['fpre'] = _chunked(P['norm_ffn_pre'][i], 8)
    c['fpost'] = _chunked(P['norm_ffn_post'][i], 8)
    cw = P['ssd_conv_w'][i]
    c['cw'] = np.ascontiguousarray(cw.T.reshape(12, 128, 5).transpose(1, 0, 2)).reshape(128, 60)
    c['cb'] = _chunked(P['ssd_conv_b'][i], 12)
    c['dtb4'] = np.tile(P['ssd_dt_bias'][i].reshape(32), 4).reshape(128, 1).astype(np.float32)
    c['alog4'] = np.tile(P['ssd_a_log'][i].reshape(32), 4).reshape(128, 1).astype(np.float32)
    c['dskip'] = P['ssd_d'][i].reshape(1, 16)
    c['snw'] = P['ssd_norm_w'][i].reshape(1, 1024)
    c['sink'] = P['win_sink'][i].reshape(1, 8)
    c['natab'] = _na_tables(P['na_rpb'][i], P['na_meta_bias'][i]).reshape(6, 128, 8 * 6 * 128)
    c['w_out'] = np.ascontiguousarray(P['w_out'][i])
    c['w_up'] = np.ascontiguousarray(P['ffn_w_up'][i])
    fw = P['ffn_conv_w'][i]
    c['fcw'] = np.ascontiguousarray(fw.T.reshape(44, 128, 3).transpose(1, 0, 2)).reshape(128, 132)
    c['fcb'] = _chunked(P['ffn_conv_b'][i], 44)
    c['w_down'] = np.ascontiguousarray(P['ffn_w_down'][i])
    return {k: np.ascontiguousarray(v, dtype=np.float32) for k, v in c.items()}


def _host_globals():
    g = {}
    g['ident'] = np.eye(128, dtype=np.float32)
    g['ones'] = np.ones((128, 128), np.float32)
    sel = np.zeros((128, 32, 128), np.float32)
    for w in range(32):
        row = w if w < 16 else 32 + w
        sel[row, w, :] = 1.0
    g['sel'] = sel.reshape(128, 32 * 128)
    k = np.arange(128)[:, None]
    l = np.arange(128)[None, :]
    g['maskf'] = np.where(k > l, NEG, 0.0).astype(np.float32)
    g['maskb'] = np.where(k < l, NEG, 0.0).astype(np.float32)
    gc = np.zeros((128, 4), np.float32)
    gc[0:32] = (1, 0, 0, 0)
    gc[32:64] = (-1, 1, 0, 1)
    gc[64:96] = (0, 1, 0, 0)
    gc[96:128] = (0, 0, 1, 0)
    g['gcoef'] = gc
    pm = np.ones((128, 512), np.float32)
    pm[:, :PADL] = 0.0
    g['padmask'] = pm
    g['wintab'] = _win_tables().reshape(128, 512)
    return g


def _rope_tables(TP):
    j = np.arange(TP)
    pos = np.maximum(j - PADL, 0).astype(np.float32)
    inv = np.power(np.float32(500000.0), -np.arange(8, dtype=np.float32) / np.float32(8)).astype(np.float32)
    ang = pos[None, :] * inv[:, None]
    cos = np.ones((128, TP), np.float32)
    sin = np.zeros((128, TP), np.float32)
    for hh in range(2):
        cos[hh * 64:hh * 64 + 8] = np.cos(ang)
        cos[hh * 64 + 8:hh * 64 + 16] = np.cos(ang)
        sin[hh * 64:hh * 64 + 8] = -np.sin(ang)
        sin[hh * 64 + 8:hh * 64 + 16] = np.sin(ang)
    return cos, sin


def tiles_of(TP, w=512):
    t = []
    t0 = 0
    while t0 < TP:
        t.append((t0, min(w, TP - t0)))
        t0 += w
    return t


def build(seq_lens, depth, debug=False):
    nc = bass.Bass("TRN2", target_bir_lowering=False)
    TPs = [L + 128 for L in seq_lens]
    NS = len(seq_lens)
    okind = "ExternalOutput" if debug else "Internal"

    def din(name, shape):
        return nc.dram_tensor(name, list(shape), F32, kind="ExternalInput").ap()

    def dscr(name, shape, dt):
        return nc.dram_tensor(name, list(shape), dt, kind=okind).ap()

    G = {k: din(k, s) for k, s in [('ident', (128, 128)), ('ones', (128, 128)), ('sel', (128, 4096)), ('maskf', (128, 128)),
                                   ('maskb', (128, 128)), ('gcoef', (128, 4)), ('padmask', (128, 512)), ('wintab', (128, 512))]}
    LC = []
    for i in range(depth):
        shapes = dict(w_in=(D, NCOL), npre=(128, 8), npost=(128, 8), fpre=(128, 8), fpost=(128, 8), cw=(128, 60), cb=(128, 12),
                      dtb4=(128, 1), alog4=(128, 1), dskip=(1, 16), snw=(1, 1024), sink=(1, 8), natab=(6, 128, 6144),
                      w_out=(2048, D), w_up=(D, 2 * DFF), fcw=(128, 132), fcb=(128, 44), w_down=(DFF, D))
        LC.append({k: din('L%d_%s' % (i, k), s) for k, s in shapes.items()})
    SEQ = []
    for s in range(NS):
        TP = TPs[s]
        d = {}
        d['h0'] = din('s%d_h0' % s, (D, TP))
        d['cos'] = din('s%d_cos' % s, (128, TP))
        d['sin'] = din('s%d_sin' % s, (128, TP))
        d['out'] = nc.dram_tensor('s%d_out' % s, [D, seq_lens[s]], F32, kind="ExternalOutput").ap()
        d['hmid'] = dscr('s%d_hmid' % s, (D, TP), F32)
        d['h1'] = dscr('s%d_h1' % s, (D, TP), F32)
        d['fm'] = dscr('s%d_fm' % s, (3200, TP), BF16)
        d['dt4'] = dscr('s%d_dt4' % s, (128, TP), F32)
        d['tm'] = dscr('s%d_tm' % s, (TP, 1664), BF16)
        d['xs'] = dscr('s%d_xs' % s, (TP, 1024), BF16)
        d['btm'] = dscr('s%d_btm' % s, (TP, 256), BF16)
        d['bct'] = dscr('s%d_bct' % s, (512, TP), BF16)
        d['yf'] = dscr('s%d_yf' % s, (TP, 1024), F32)
        d['yT'] = dscr('s%d_yT' % s, (2048, TP), BF16)
        d['act'] = dscr('s%d_act' % s, (DFF, TP), BF16)
        SEQ.append(d)

    with ExitStack() as top:
        S = Sched(nc)
        S.alloc(top)
        pb = [top.enter_context(nc.psum_tensor('pb%d' % i, [128, 512], F32)) for i in range(8)]
        pbk = ['pb%d' % i for i in range(8)]

        def act(fn, r, w):
            return S.op('act', fn, r, w)

        def dve(fn, r, w):
            return S.op('dve', fn, r, w)

        def pe(fn, r, w):
            return S.op('pe', fn, r, w)

        def mm(out, lhsT, rhs, start, stop, r, w):
            return S.op('pe', lambda e, o=out, l=lhsT, rr=rhs, s0=start, s1=stop: e.matmul(o, lhsT=l, rhs=rr, start=s0, stop=s1), r, w)

        def tr(out, in_, ident, r, w):
            return S.op('pe', lambda e, o=out, i=in_, d=ident: e.transpose(o, i, d), r, w)

        uniq = [0]

        def T(st, name, shape, dt=F32):
            uniq[0] += 1
            return st.enter_context(nc.sbuf_tensor('sb%d_%s' % (uniq[0], name), list(shape), dt))

        identf = T(top, 'identf', [128, 128])
        identb = T(top, 'identb', [128, 128], BF16)
        onesf = T(top, 'onesf', [128, 128])
        onesb = T(top, 'onesb', [128, 128], BF16)
        padmask = T(top, 'padmask', [128, 512])
        S.dma(identf[:], G['ident'][:, :], writes=['identf'])
        S.dma(onesf[:], G['ones'][:, :], writes=['onesf'])
        S.dma(padmask[:], G['padmask'][:, :], writes=['padmask'])
        dve(lambda e: e.tensor_copy(out=identb[:], in_=identf[:]), ['identf'], ['identb'])
        dve(lambda e: e.tensor_copy(out=onesb[:], in_=onesf[:]), ['onesf'], ['onesb'])

        def load_weight_bf16(st, name, wsrc, K, N, scale_src=None, piece=512):
            KC = K // 128
            wt = T(st, name, [128, KC, N], BF16)
            tmpst = ExitStack()
            if KC > 8:
                piece = 256
            stg = [T(tmpst, name + '_stg%d' % i, [128, KC, piece]) for i in range(2)]
            sc = None
            if scale_src is not None:
                sc = T(tmpst, name + '_sc', [128, KC])
                S.dma(sc[:], scale_src[:, :], writes=[name + '_sc'])
            src = wsrc.rearrange("(c p) n -> p c n", p=128)
            n0 = 0
            pi = 0
            while n0 < N:
                nw = min(piece, N - n0)
                sg = stg[pi % 2]
                sk = name + '_stg%d' % (pi % 2)
                S.dma(sg[:, :, :nw], src[:, :, n0:n0 + nw], writes=[sk])
                for c in range(KC):
                    eng = 'dve' if (c % 2 == 0) else 'act'
                    if sc is not None:
                        if eng == 'dve':
                            S.op('dve', lambda e, o=wt[:, c, n0:n0 + nw], i=sg[:, c, :nw], s=sc[:, c:c + 1]:
                                 e.tensor_scalar(out=o, in0=i, scalar1=s, scalar2=None, op0=ALU.mult), [sk, name + '_sc'], [name])
                        else:
                            S.op('act', lambda e, o=wt[:, c, n0:n0 + nw], i=sg[:, c, :nw], s=sc[:, c:c + 1]:
                                 e.activation(out=o, in_=i, func=AF.Copy, scale=s), [sk, name + '_sc'], [name])
                    else:
                        if eng == 'dve':
                            S.op('dve', lambda e, o=wt[:, c, n0:n0 + nw], i=sg[:, c, :nw]: e.tensor_copy(out=o, in_=i), [sk], [name])
                        else:
                            S.op('act', lambda e, o=wt[:, c, n0:n0 + nw], i=sg[:, c, :nw]: e.activation(out=o, in_=i, func=AF.Copy), [sk], [name])
                n0 += nw
                pi += 1
            S.barrier()
            tmpst.close()
            return wt

        def norm_tile(st_tiles, hsrc, t0, tw, slot, zero_cols=None, src_lo=None):
            ht, sq, rstd, aT = st_tiles['ht'][slot], st_tiles['sq'], st_tiles['rstd'], st_tiles['aT'][slot]
            hk, ak = 'ht%d' % slot, 'aT%d' % slot
            S.dma(ht[:, :, :tw], hsrc.rearrange("(c p) t -> p c t", p=128)[:, :, t0:t0 + tw], writes=[hk])
            act(lambda e: e.activation(out=sq[:, :, :tw], in_=ht[:, :, :tw], func=AF.Square), [hk], ['sq'])
            for c in range(8):
                mm(pb[7][:, :tw], onesb[:], sq[:, c, :tw], c == 0, c == 7, ['sq', 'onesb'], ['pb7'])
            act(lambda e: e.activation(out=rstd[:, :tw], in_=pb[7][:, :tw], func=AF.Ln, scale=1.0 / D, bias=EPS), ['pb7'], ['rstd'])
            act(lambda e: e.activation(out=rstd[:, :tw], in_=rstd[:, :tw], func=AF.Exp, scale=-0.5), ['rstd'], ['rstd'])
            dve(lambda e: e.tensor_tensor(out=aT[:, :, :tw], in0=ht[:, :, :tw],
                                          in1=rstd[:, :tw].unsqueeze(1).broadcast_to([128, 8, tw]), op=ALU.mult), [hk, 'rstd'], [ak])
            return ht, aT, hk, ak

        def post_norm_residual(mix, mixk, hres, hresk, wpost, wpostk, tw, sq, rstd, outt, outk, mask_pad):
            act(lambda e: e.activation(out=sq[:, :, :tw], in_=mix[:, :, :tw], func=AF.Square), [mixk], ['sq'])
            for c in range(8):
                mm(pb[7][:, :tw], onesb[:], sq[:, c, :tw], c == 0, c == 7, ['sq', 'onesb'], ['pb7'])
            act(lambda e: e.activation(out=rstd[:, :tw], in_=pb[7][:, :tw], func=AF.Ln, scale=1.0 / D, bias=EPS), ['pb7'], ['rstd'])
            act(lambda e: e.activation(out=rstd[:, :tw], in_=rstd[:, :tw], func=AF.Exp, scale=-0.5), ['rstd'], ['rstd'])
            dve(lambda e: e.tensor_tensor(out=mix[:, :, :tw], in0=mix[:, :, :tw],
                                          in1=rstd[:, :tw].unsqueeze(1).broadcast_to([128, 8, tw]), op=ALU.mult), [mixk, 'rstd'], [mixk])
            for c in range(8):
                dve(lambda e, c=c: e.scalar_tensor_tensor(out=outt[:, c, :tw], in0=mix[:, c, :tw], scalar=wpost[:, c:c + 1],
                                                         in1=hres[:, c, :tw], op0=ALU.mult, op1=ALU.add), [mixk, hresk, wpostk], [outk])
            if mask_pad:
                dve(lambda e: e.tensor_tensor(out=outt[:, :, :tw], in0=outt[:, :, :tw],
                                              in1=padmask[:, :tw].unsqueeze(1).broadcast_to([128, 8, tw]), op=ALU.mult), [outk, 'padmask'], [outk])

        for li in range(depth):
            C = LC[li]
            last_layer = (li == depth - 1)
            with ExitStack() as st:
                W = load_weight_bf16(st, 'W', C['w_in'], D, NCOL, C['npre'])
                tl = {'ht': [T(st, 'ht%d' % i, [128, 8, 512]) for i in range(2)], 'sq': T(st, 'sq', [128, 8, 512], BF16),
                      'rstd': T(st, 'rstd', [128, 512]), 'aT': [T(st, 'aT%d' % i, [128, 8, 512], BF16) for i in range(2)]}
                ev = [T(st, 'ev%d' % i, [128, 512], BF16) for i in range(4)]
                evf = [T(st, 'evf%d' % i, [128, 512]) for i in range(2)]
                cs = [T(st, 'cs%d' % i, [128, 512]) for i in range(2)]
                sn = [T(st, 'sn%d' % i, [128, 512]) for i in range(2)]
                r1 = T(st, 'r1', [128, 512])
                r2 = T(st, 'r2', [128, 512])
                dtb4 = T(st, 'dtb4', [128, 1])
                S.dma(dtb4[:], C['dtb4'][:, :], writes=['dtb4'])
                tme = [T(st, 'tme%d' % i, [128, 1664], BF16) for i in range(2)]
                nev = 0
                npb = 0
                ti = 0
                for s in range(NS):
                    Q = SEQ[s]
                    hsrc = Q['h0'] if li == 0 else Q['h1']
                    for (t0, tw) in tiles_of(TPs[s]):
                        slot = ti % 2
                        ti += 1
                        ht, aT, hk, ak = norm_tile(tl, hsrc, t0, tw, slot)
                        for m in range(20):
                            p = pb[npb % 4]
                            pk = pbk[npb % 4]
                            npb += 1
                            for c in range(8):
                                mm(p[:, :tw], W[:, c, m * 128:(m + 1) * 128], aT[:, c, :tw], c == 0, c == 7, ['W', ak], [pk])
                            e_ = ev[nev % 4]
                            ek = 'ev%d' % (nev % 4)
                            nev += 1
                            if m % 2 == 0:
                                act(lambda e, o=e_, p=p: e.activation(out=o[:, :tw], in_=p[:, :tw], func=AF.Copy), [pk], [ek])
                            else:
                                dve(lambda e, o=e_, p=p: e.tensor_copy(out=o[:, :tw], in_=p[:, :tw]), [pk], [ek])
                            S.dma(Q['fm'][m * 128:(m + 1) * 128, t0:t0 + tw], e_[:, :tw], reads=[ek], writes=['fm%d' % s])
                        cst, snt = cs[slot], sn[slot]
                        S.dma(cst[:, :tw], Q['cos'][:, t0:t0 + tw], writes=['cs%d' % slot])
                        S.dma(snt[:, :tw], Q['sin'][:, t0:t0 + tw], writes=['sn%d' % slot])
                        for m in range(5):
                            ca = C_WQ + m * 128
                            cbb = C_WQS + m * 128
                            pA, pB = pb[4], pb[5]
                            for c in range(8):
                                mm(pA[:, :tw], W[:, c, ca:ca + 128], aT[:, c, :tw], c == 0, c == 7, ['W', ak], ['pb4'])
                            for c in range(8):
                                mm(pB[:, :tw], W[:, c, cbb:cbb + 128], aT[:, c, :tw], c == 0, c == 7, ['W', ak], ['pb5'])
                            dve(lambda e, pA=pA: e.tensor_tensor(out=r1[:, :tw], in0=pA[:, :tw], in1=cst[:, :tw], op=ALU.mult), ['pb4', 'cs%d' % slot], ['r1'])
                            dve(lambda e, pB=pB: e.tensor_tensor(out=r2[:, :tw], in0=pB[:, :tw], in1=snt[:, :tw], op=ALU.mult), ['pb5', 'sn%d' % slot], ['r2'])
                            e_ = ev[nev % 4]
                            ek = 'ev%d' % (nev % 4)
                            nev += 1
                            dve(lambda e, o=e_: e.tensor_tensor(out=o[:, :tw], in0=r1[:, :tw], in1=r2[:, :tw], op=ALU.add), ['r1', 'r2'], [ek])
                            S.dma(Q['fm'][2560 + m * 128:2560 + (m + 1) * 128, t0:t0 + tw], e_[:, :tw], reads=[ek], writes=['fm%d' % s])
                        p = pb[6]
                        for c in range(8):
                            mm(p[:, :tw], W[:, c, C_DT4:C_DT4 + 128], aT[:, c, :tw], c == 0, c == 7, ['W', ak], ['pb6'])
                        ef = evf[slot]
                        efk = 'evf%d' % slot
                        act(lambda e, o=ef, p=p: e.activation(out=o[:, :tw], in_=p[:, :tw], func=AF.Exp, bias=dtb4[:, 0:1]), ['pb6', 'dtb4'], [efk])
                        act(lambda e, o=ef: e.activation(out=o[:, :tw], in_=o[:, :tw], func=AF.Ln, bias=1.0), [efk], [efk])
                        if t0 == 0:
                            dve(lambda e, o=ef: e.tensor_tensor(out=o[:, :tw], in0=o[:, :tw], in1=padmask[:, :tw], op=ALU.mult), [efk, 'padmask'], [efk])
                        S.dma(Q['dt4'][:, t0:t0 + tw], ef[:, :tw], reads=[efk], writes=['dt4%d' % s])
                        for sub in range(tw // 128):
                            te = tme[sub % 2]
                            tk = 'tme%d' % (sub % 2)
                            for (n0, nw) in [(0, 512), (512, 512), (1024, 512), (1536, 128)]:
                                p = pb[npb % 4]
                                pk = pbk[npb % 4]
                                npb += 1
                                for c in range(8):
                                    mm(p[:, :nw], aT[:, c, sub * 128:(sub + 1) * 128], W[:, c, C_Z + n0:C_Z + n0 + nw], c == 0, c == 7, ['W', ak], [pk])
                                if (n0 // 512) % 2 == 0:
                                    act(lambda e, o=te, p=p, n0=n0, nw=nw: e.activation(out=o[:, n0:n0 + nw], in_=p[:, :nw], func=AF.Copy), [pk], [tk])
                                else:
                                    dve(lambda e, o=te, p=p, n0=n0, nw=nw: e.tensor_copy(out=o[:, n0:n0 + nw], in_=p[:, :nw]), [pk], [tk])
                            S.dma(Q['tm'][t0 + sub * 128:t0 + (sub + 1) * 128, :], te[:, :], reads=[tk], writes=['tm%d' % s])
            S.barrier()

            with ExitStack() as st:
                cw = T(st, 'cw', [128, 60])
                cbt = T(st, 'cbt', [128, 12])
                S.dma(cw[:], C['cw'][:, :], writes=['cw'])
                S.dma(cbt[:], C['cb'][:, :], writes=['cbt'])
                xin = [T(st, 'xin%d' % i, [128, 12, 516], BF16) for i in range(2)]
                acc = [T(st, 'acc%d' % i, [128, 512]) for i in range(2)]
                xo = [T(st, 'xo%d' % i, [128, 12, 512], BF16) for i in range(2)]
                xt = [T(st, 'xt%d' % i, [128, 1280], BF16) for i in range(2)]
                ti = 0
                na = 0
                for s in range(NS):
                    Q = SEQ[s]
                    TP = TPs[s]
                    src = Q['fm'][0:1536, :].rearrange("(c p) t -> p c t", p=128)
                    for (t0, tw) in tiles_of(TP):
                        slot = ti % 2
                        ti += 1
                        xi, xik = xin[slot], 'xin%d' % slot
                        lo, hi = max(t0 - 2, 0), min(t0 + tw + 2, TP)
                        if lo != t0 - 2 or hi != t0 + tw + 2:
                            S.op('pool', lambda e, xi=xi: e.memset(xi[:], 0.0), [], [xik])
                        S.dma(xi[:, :, lo - (t0 - 2):hi - (t0 - 2)], src[:, :, lo:hi], writes=[xik])
                        xoo, xok = xo[slot], 'xo%d' % slot
                        for c in range(12):
                            a_, ak_ = acc[na % 2], 'acc%d' % (na % 2)
                            na += 1
                            dve(lambda e, a_=a_, c=c: e.tensor_scalar(out=a_[:, :tw], in0=xi[:, c, 0:tw], scalar1=cw[:, c * 5:c * 5 + 1],
                                                                    scalar2=cbt[:, c:c + 1], op0=ALU.mult, op1=ALU.add), [xik, 'cw', 'cbt'], [ak_])
                            for j in range(1, 5):
                                dve(lambda e, a_=a_, c=c, j=j: e.scalar_tensor_tensor(out=a_[:, :tw], in0=xi[:, c, j:j + tw], scalar=cw[:, c * 5 + j:c * 5 + j + 1],
                                                                                    in1=a_[:, :tw], op0=ALU.mult, op1=ALU.add), [xik, 'cw', ak_], [ak_])
                            act(lambda e, a_=a_, c=c: e.activation(out=xoo[:, c, :tw], in_=a_[:, :tw], func=AF.Silu), [ak_], [xok])
                        S.dma(Q['bct'].rearrange("(c p) t -> p c t", p=128)[:, :, t0:t0 + tw], xoo[:, 8:12, :tw], reads=[xok], writes=['bct%d' % s])
                        for sub in range(tw // 128):
                            xtt, xtk = xt[sub % 2], 'xt%d' % (sub % 2)
                            for half in range(3):
                                cl = [(0, 1, 2, 3), (4, 5, 6, 7), (8, 9)][half]
                                p = pb[half]
                                pv = p[:].bitcast(BF16)
                                for ii, c in enumerate(cl):
                                    tr(pv[:, ii * 128:(ii + 1) * 128], xoo[:, c, sub * 128:(sub + 1) * 128], identb[:], [xok, 'identb'], [pbk[half]])
                                n = len(cl) * 128
                                if half == 1:
                                    act(lambda e, pv=pv, n=n, o=xtt, c0=cl[0]: e.activation(out=o[:, c0 * 128:c0 * 128 + n], in_=pv[:, :n], func=AF.Copy), [pbk[half]], [xtk])
                                else:
                                    dve(lambda e, pv=pv, n=n, o=xtt, c0=cl[0]: e.tensor_copy(out=o[:, c0 * 128:c0 * 128 + n], in_=pv[:, :n]), [pbk[half]], [xtk])
                            S.dma(Q['xs'][t0 + sub * 128:t0 + (sub + 1) * 128, :], xtt[:, 0:1024], reads=[xtk], writes=['xs%d' % s])
                            S.dma(Q['btm'][t0 + sub * 128:t0 + (sub + 1) * 128, :], xtt[:, 1024:1280], reads=[xtk], writes=['btm%d' % s])
            S.barrier()

            with ExitStack() as st:
                selb = T(st, 'selb', [128, 32, 128], BF16)
                maskb_ = [T(st, 'mk%d' % i, [128, 128], BF16) for i in range(2)]
                gco = T(st, 'gco', [128, 4])
                a4 = T(st, 'a4c', [128, 1])
                dsk = T(st, 'dsk', [128, 16])
                snw = T(st, 'snw', [128, 1024])
                with ExitStack() as tmp:
                    stg = T(tmp, 'selstg', [128, 4096])
                    S.dma(stg[:], G['sel'][:, :], writes=['selstg'])
                    dve(lambda e: e.tensor_copy(out=selb[:].rearrange("p a b -> p (a b)"), in_=stg[:]), ['selstg'], ['selb'])
                    S.dma(stg[:, 0:128], G['maskf'][:, :], writes=['selstg'])
                    dve(lambda e: e.tensor_copy(out=maskb_[0][:], in_=stg[:, 0:128]), ['selstg'], ['mk0'])
                    S.dma(stg[:, 0:128], G['maskb'][:, :], writes=['selstg'])
                    dve(lambda e: e.tensor_copy(out=maskb_[1][:], in_=stg[:, 0:128]), ['selstg'], ['mk1'])
                    S.barrier()
                S.dma(gco[:], G['gcoef'][:, :], writes=['gco'])
                S.dma(a4[:], C['alog4'][:, :], writes=['a4c'])
                act(lambda e: e.activation(out=a4[:], in_=a4[:], func=AF.Exp), ['a4c'], ['a4c'])
                dve(lambda e: e.tensor_scalar(out=a4[:], in0=a4[:], scalar1=-1.0, scalar2=None, op0=ALU.mult), ['a4c'], ['a4c'])
                S.dma(dsk[:], C['dskip'].partition_broadcast(128).rearrange("p a b -> p (a b)"), writes=['dsk'])
                S.dma(snw[:], C['snw'].partition_broadcast(128).rearrange("p a b -> p (a b)"), writes=['snw'])
                dtt = [T(st, 'dtt%d' % i, [128, 128]) for i in range(2)]
                a4t = T(st, 'a4t', [128, 128])
                cum = T(st, 'cum', [128, 128])
                Gt = T(st, 'Gt', [128, 128])
                Ghi = T(st, 'Ghi', [128, 128], BF16)
                Glo = T(st, 'Glo', [128, 128], BF16)
                Gtmp = T(st, 'Gtmp', [128, 128])
                cols = T(st, 'cols', [128, 128])
                ncol = T(st, 'ncol', [128, 32])
                scol = T(st, 'scol', [128, 16])
                dcol = T(st, 'dcol', [128, 16])
                cdb = T(st, 'cdb', [128, 16])
                Lm = T(st, 'Lm', [128, 16, 128], BF16)
                CBt = T(st, 'CBt', [128, 2, 128], BF16)
                MT = T(st, 'MT', [128, 16, 128], BF16)
                xs_t = [T(st, 'xs_t%d' % i, [128, 1024], BF16) for i in range(2)]
                b_t = [T(st, 'b_t%d' % i, [128, 256], BF16) for i in range(2)]
                bc_t = [T(st, 'bc_t%d' % i, [128, 4, 128], BF16) for i in range(2)]
                xdt = T(st, 'xdt', [128, 1024], BF16)
                xdd = T(st, 'xdd', [128, 1024], BF16)
                Hs = T(st, 'Hs', [128, 1024])
                Hb = T(st, 'Hb', [128, 1024], BF16)
                yacc = [T(st, 'yacc%d' % i, [128, 1024]) for i in range(2)]
                ytmp = T(st, 'ytmp', [128, 1024])
                yfl = [T(st, 'yfl%d' % i, [128, 1024]) for i in range(2)]
                zt = [T(st, 'zt%d' % i, [128, 1024], BF16) for i in range(2)]
                zs = T(st, 'zs', [128, 1024])
                ssq = T(st, 'ssq', [128, 1])
                ybf = T(st, 'ybf', [128, 1024], BF16)
                yTs = [T(st, 'yTs%d' % i, [128, 8, 128], BF16) for i in range(2)]
                onesrow = T(st, 'onesrow', [128, 128])
                dve(lambda e: e.tensor_copy(out=onesrow[:], in_=onesf[:]), ['onesf'], ['onesrow'])
                bi = 0
                for s in range(NS):
                    Q = SEQ[s]
                    TP = TPs[s]
                    NB = TP // 128
                    for dr in range(2):
                        dve(lambda e: e.memset(Hs[:], 0.0), [], ['Hs'])
                        dve(lambda e: e.memset(Hb[:], 0.0), [], ['Hb'])
                        blocks = range(NB) if dr == 0 else range(NB - 1, -1, -1)
                        hd0 = dr * 16
                        for b in blocks:
                            slot = bi % 2
                            bi += 1
                            c0 = b * 128
                            dt_, dtk = dtt[slot], 'dtt%d' % slot
                            S.dma(dt_[:], Q['dt4'][:, c0:c0 + 128], reads=['dt4%d' % s], writes=[dtk])
                            xst, xsk = xs_t[slot], 'xs_t%d' % slot
                            S.dma(xst[:], Q['xs'][c0:c0 + 128, :], reads=['xs%d' % s], writes=[xsk])
                            bt, bk = b_t[slot], 'b_t%d' % slot
                            S.dma(bt[:], Q['btm'][c0:c0 + 128, :], reads=['btm%d' % s], writes=[bk])
                            bct, bck = bc_t[slot], 'bc_t%d' % slot
                            S.dma(bct[:], Q['bct'].rearrange("(c p) t -> p c t", p=128)[:, :, c0:c0 + 128], reads=['bct%d' % s], writes=[bck])
                            dve(lambda e, dt_=dt_: e.tensor_scalar(out=a4t[:], in0=dt_[:], scalar1=a4[:, 0:1], scalar2=None, op0=ALU.mult), [dtk, 'a4c'], ['a4t'])
                            dve(lambda e: e.tensor_tensor_scan(out=cum[:], data0=onesrow[:], data1=a4t[:], initial=0.0, op0=ALU.mult, op1=ALU.add), ['a4t', 'onesrow'], ['cum'])
                            dve(lambda e: e.tensor_scalar(out=Gtmp[:], in0=cum[:, 127:128].broadcast_to([128, 128]), scalar1=gco[:, 3:4], scalar2=None, op0=ALU.mult), ['cum', 'gco'], ['Gtmp'])
                            dve(lambda e: e.scalar_tensor_tensor(out=Gtmp[:], in0=cum[:], scalar=gco[:, 0:1], in1=Gtmp[:], op0=ALU.mult, op1=ALU.add), ['cum', 'gco', 'Gtmp'], ['Gtmp'])
                            dve(lambda e: e.scalar_tensor_tensor(out=Gtmp[:], in0=a4t[:], scalar=gco[:, 1:2], in1=Gtmp[:], op0=ALU.mult, op1=ALU.add), ['a4t', 'gco', 'Gtmp'], ['Gtmp'])
                            dve(lambda e, dt_=dt_: e.scalar_tensor_tensor(out=Gt[:], in0=dt_[:], scalar=gco[:, 2:3], in1=Gtmp[:], op0=ALU.mult, op1=ALU.add), [dtk, 'gco', 'Gtmp'], ['Gt'])
                            dve(lambda e: e.tensor_copy(out=Ghi[:], in_=Gt[:]), ['Gt'], ['Ghi'])
                            dve(lambda e: e.tensor_tensor(out=Gtmp[:], in0=Gt[:], in1=Ghi[:], op=ALU.subtract), ['Gt', 'Ghi'], ['Gtmp'])
                            dve(lambda e: e.tensor_copy(out=Glo[:], in_=Gtmp[:]), ['Gtmp'], ['Glo'])
                            tr(pb[6][:, 0:128], Gt[:], identf[:], ['Gt', 'identf'], ['pb6'])
                            act(lambda e: e.activation(out=cols[:], in_=pb[6][:, 0:128], func=AF.Copy), ['pb6'], ['cols'])
                            cx0 = 0 if dr == 0 else 48
                            ot0 = 32 if dr == 0 else 16
                            a0 = 64 + hd0
                            d0 = 96 + hd0
                            dve(lambda e, cx0=cx0: e.tensor_scalar(out=ncol[:, 0:16], in0=cols[:, cx0:cx0 + 16], scalar1=-1.0, scalar2=None, op0=ALU.mult), ['cols'], ['ncol'])
                            act(lambda e, cx0=cx0: e.activation(out=scol[:], in_=cols[:, cx0:cx0 + 16], func=AF.Exp), ['cols'], ['scol'])
                            dve(lambda e, ot0=ot0, a0=a0: e.tensor_tensor(out=dcol[:], in0=cols[:, ot0:ot0 + 16], in1=cols[:, a0:a0 + 16], op=ALU.subtract), ['cols'], ['dcol'])
                            act(lambda e: e.activation(out=dcol[:], in_=dcol[:], func=AF.Exp), ['dcol'], ['dcol'])
                            mm(pb[6][:, 256:272], onesf[:], cols[:, a0:a0 + 16], True, True, ['onesf', 'cols'], ['pb6'])
                            act(lambda e: e.activation(out=cdb[:], in_=pb[6][:, 256:272], func=AF.Exp), ['pb6'], ['cdb'])
                            for half in range(2):
                                for hh in range(8):
                                    h = half * 8 + hh
                                    w = hd0 + h
                                    o = pb[half * 2 + hh // 4][:, (hh % 4) * 128:(hh % 4 + 1) * 128]
                                    pk = pbk[half * 2 + hh // 4]
                                    mm(o, selb[:, w, :], Ghi[:], True, False, ['selb', 'Ghi'], [pk])
                                    mm(o, selb[:, w, :], Glo[:], False, False, ['selb', 'Glo'], [pk])
                                    mm(o, identb[:], maskb_[dr][:], False, True, ['identb', 'mk%d' % dr], [pk])
                                    act(lambda e, o=o, h=h: e.activation(out=Lm[:, h, :], in_=o, func=AF.Exp, bias=ncol[:, h:h + 1]), [pk, 'ncol'], ['Lm'])
                            for g in range(2):
                                mm(pb[4][:, g * 128:(g + 1) * 128], bct[:, g, :], bct[:, 2 + g, :], True, True, [bck], ['pb4'])
                            act(lambda e: e.activation(out=CBt[:].rearrange("p a b -> p (a b)"), in_=pb[4][:, 0:256], func=AF.Copy), ['pb4'], ['CBt'])
                            for g in range(2):
                                dve(lambda e, g=g: e.tensor_tensor(out=MT[:, g * 8:(g + 1) * 8, :], in0=Lm[:, g * 8:(g + 1) * 8, :],
                                                                  in1=CBt[:, g:g + 1, :].broadcast_to([128, 8, 128]), op=ALU.mult), ['Lm', 'CBt'], ['MT'])
                            dve(lambda e, xst=xst, d0=d0: e.tensor_tensor(out=xdt[:].rearrange("p (h d) -> p h d", d=64), in0=xst[:].rearrange("p (h d) -> p h d", d=64),
                                                                      in1=cols[:, d0:d0 + 16].unsqueeze(2).broadcast_to([128, 16, 64]), op=ALU.mult), [xsk, 'cols'], ['xdt'])
                            dve(lambda e: e.tensor_tensor(out=xdd[:].rearrange("p (h d) -> p h d", d=64), in0=xdt[:].rearrange("p (h d) -> p h d", d=64),
                                                          in1=dcol[:].unsqueeze(2).broadcast_to([128, 16, 64]), op=ALU.mult), ['xdt', 'dcol'], ['xdd'])
                            for h in range(16):
                                mm(pb[h // 8][:, (h % 8) * 64:(h % 8 + 1) * 64], MT[:, h, :], xdt[:, h * 64:(h + 1) * 64], True, True, ['MT', 'xdt'], [pbk[h // 8]])
                            for g in range(2):
                                mm(pb[2 + g][:, :], bct[:, 2 + g, :], Hb[:, g * 512:(g + 1) * 512], True, True, [bck, 'Hb'], [pbk[2 + g]])
                            ya, yak = yacc[slot], 'yacc%d' % slot
                            for g in range(2):
                                dve(lambda e, g=g: e.tensor_tensor(out=ytmp[:, g * 512:(g + 1) * 512].rearrange("p (h d) -> p h d", d=64),
                                                                  in0=pb[2 + g][:, :].rearrange("p (h d) -> p h d", d=64),
                                                                  in1=scol[:, g * 8:(g + 1) * 8].unsqueeze(2).broadcast_to([128, 8, 64]), op=ALU.mult), [pbk[2 + g], 'scol'], ['ytmp'])
                                dve(lambda e, g=g, ya=ya: e.tensor_tensor(out=ya[:, g * 512:(g + 1) * 512], in0=pb[g][:, :], in1=ytmp[:, g * 512:(g + 1) * 512], op=ALU.add), [pbk[g], 'ytmp'], [yak])
                            for g in range(2):
                                mm(pb[4 + g][:, :], bt[:, g * 128:(g + 1) * 128], xdd[:, g * 512:(g + 1) * 512], True, True, [bk, 'xdd'], [pbk[4 + g]])
                            dve(lambda e: e.tensor_tensor(out=Hs[:].rearrange("p (h d) -> p h d", d=64), in0=Hs[:].rearrange("p (h d) -> p h d", d=64),
                                                          in1=cdb[:].unsqueeze(2).broadcast_to([128, 16, 64]), op=ALU.mult), ['Hs', 'cdb'], ['Hs'])
                            for g in range(2):
                                dve(lambda e, g=g: e.tensor_tensor(out=Hs[:, g * 512:(g + 1) * 512], in0=Hs[:, g * 512:(g + 1) * 512], in1=pb[4 + g][:, :], op=ALU.add), ['Hs', pbk[4 + g]], ['Hs'])
                            act(lambda e: e.activation(out=Hb[:], in_=Hs[:], func=AF.Copy), ['Hs'], ['Hb'])
                            if dr == 0:
                                S.dma(Q['yf'][c0:c0 + 128, :], ya[:], reads=[yak], writes=['yf%d' % s])
                            else:
                                yf_, yfk = yfl[slot], 'yfl%d' % slot
                                S.dma(yf_[:], Q['yf'][c0:c0 + 128, :], reads=['yf%d' % s], writes=[yfk])
                                z_, zk = zt[slot], 'zt%d' % slot
                                S.dma(z_[:], Q['tm'][c0:c0 + 128, 0:1024], reads=['tm%d' % s], writes=[zk])
                                dve(lambda e, ya=ya, yf_=yf_: e.tensor_tensor(out=ya[:], in0=ya[:], in1=yf_[:], op=ALU.add), [yak, yfk], [yak])
                                dve(lambda e, xst=xst: e.tensor_tensor(out=ytmp[:].rearrange("p (h d) -> p h d", d=64), in0=xst[:].rearrange("p (h d) -> p h d", d=64),
                                                                  in1=dsk[:].unsqueeze(2).broadcast_to([128, 16, 64]), op=ALU.mult), [xsk, 'dsk'], ['ytmp'])
                                dve(lambda e, ya=ya: e.tensor_tensor(out=ya[:], in0=ya[:], in1=ytmp[:], op=ALU.add), [yak, 'ytmp'], [yak])
                                act(lambda e, z_=z_: e.activation(out=zs[:], in_=z_[:], func=AF.Silu), [zk], ['zs'])
                                dve(lambda e, ya=ya: e.tensor_tensor(out=ya[:], in0=ya[:], in1=zs[:], op=ALU.mult), [yak, 'zs'], [yak])
                                act(lambda e, ya=ya: e.activation(out=zs[:], in_=ya[:], func=AF.Square, accum_out=ssq[:]), [yak], ['zs', 'ssq'])
                                act(lambda e: e.activation(out=ssq[:], in_=ssq[:], func=AF.Ln, scale=1.0 / 1024, bias=EPS), ['ssq'], ['ssq'])
                                act(lambda e: e.activation(out=ssq[:], in_=ssq[:], func=AF.Exp, scale=-0.5), ['ssq'], ['ssq'])
                                dve(lambda e, ya=ya: e.scalar_tensor_tensor(out=ybf[:], in0=ya[:], scalar=ssq[:, 0:1], in1=snw[:], op0=ALU.mult, op1=ALU.mult), [yak, 'ssq', 'snw'], ['ybf'])
                                yT_, yTk = yTs[slot], 'yTs%d' % slot
                                pv = pb[7][:].bitcast(BF16)
                                for c in range(8):
                                    tr(pv[:, c * 128:(c + 1) * 128], ybf[:, c * 128:(c + 1) * 128], identb[:], ['ybf', 'identb'], ['pb7'])
                                act(lambda e, yT_=yT_, pv=pv: e.activation(out=yT_[:].rearrange("p a b -> p (a b)"), in_=pv[:, :], func=AF.Copy), ['pb7'], [yTk])
                                S.dma(Q['yT'][0:1024, :].rearrange("(c p) t -> p c t", p=128)[:, :, c0:c0 + 128], yT_[:], reads=[yTk], writes=['yT%d' % s])
            S.barrier()

            with ExitStack() as st:
                Ena = T(st, 'Ena', [128, 6, 8 * 6 * 128], BF16)
                Ewin = T(st, 'Ewin', [128, 4, 128], BF16)
                esink = T(st, 'esink', [128, 8])
                with ExitStack() as tmp:
                    stg = [T(tmp, 'nastg%d' % i, [128, 3072]) for i in range(2)]
                    k = 0
                    for v in range(6):
                        for hf in range(2):
                            sg, sk = stg[k % 2], 'nastg%d' % (k % 2)
                            k += 1
                            S.dma(sg[:], C['natab'][v, :, hf * 3072:(hf + 1) * 3072], writes=[sk])
                            act(lambda e, sg=sg, v=v, hf=hf: e.activation(out=Ena[:, v, hf * 3072:(hf + 1) * 3072], in_=sg[:], func=AF.Exp), [sk], ['Ena'])
                    S.dma(stg[0][:, 0:512], G['wintab'][:, :], writes=['nastg0'])
                    act(lambda e: e.activation(out=Ewin[:].rearrange("p a b -> p (a b)"), in_=stg[0][:, 0:512], func=AF.Exp), ['nastg0'], ['Ewin'])
                    S.dma(esink[:], C['sink'].partition_broadcast(128).rearrange("p a b -> p (a b)"), writes=['esink'])
                    act(lambda e: e.activation(out=esink[:], in_=esink[:], func=AF.Exp), ['esink'], ['esink'])
                    S.barrier()
                qn = [T(st, 'qn%d' % i, [128, 4, 128], BF16) for i in range(2)]
                qw = [T(st, 'qw%d' % i, [128, 4, 128], BF16) for i in range(2)]
                kn = [T(st, 'kn%d' % i, [128, 4, 6, 128], BF16) for i in range(2)]
                kw = [T(st, 'kw%d' % i, [128, 2, 4, 128], BF16) for i in range(2)]
                vn = [T(st, 'vn%d' % i, [128, 6, 8, 65], BF16) for i in range(2)]
                vw = [T(st, 'vw%d' % i, [128, 4, 2, 65], BF16) for i in range(2)]
                for i in range(2):
                    S.op('pool', lambda e, t=vn[i]: e.memset(t[:], 1.0), [], ['vn%d' % i])
                    S.op('pool', lambda e, t=vw[i]: e.memset(t[:], 1.0), [], ['vw%d' % i])
                Pt = [T(st, 'Pt%d' % i, [128, 768], BF16) for i in range(2)]
                P2 = [T(st, 'P2%d' % i, [128, 768], BF16) for i in range(2)]
                rec = [T(st, 'rec%d' % i, [128, 1]) for i in range(2)]
                yat = [T(st, 'yat%d' % i, [128, 1024], BF16) for i in range(2)]
                yTa = [T(st, 'yTa%d' % i, [128, 8, 128], BF16) for i in range(2)]
                qi = 0
                ui = 0
                for s in range(NS):
                    Q = SEQ[s]
                    TP = TPs[s]
                    NB = TP // 128
                    fmv = Q['fm'].rearrange("(c p) t -> p c t", p=128)
                    for qb in range(NB):
                        slot = qi % 2
                        qi += 1
                        c0 = qb * 128
                        if qb == 0:
                            vi, kbs = 5, [1, 2, 3, 4]
                        elif qb == 1:
                            vi, kbs = 1, [1, 2, 3, 4]
                        elif qb == 2:
                            vi, kbs = 2, [1, 2, 3, 4]
                        elif qb == NB - 1:
                            vi, kbs = 4, [qb - 3, qb - 2, qb - 1, qb]
                        elif qb == NB - 2:
                            vi, kbs = 3, [qb - 2, qb - 1, qb, qb + 1]
                        else:
                            vi, kbs = 0, [qb - 2, qb - 1, qb, qb + 1, qb + 2]
                        na_slots = [(si, kb) for si, kb in enumerate(kbs)] + [(5, 0)]
                        if qb == 0:
                            w_slots = [(2, 1), (3, 0)]
                        else:
                            w_slots = ([(0, qb - 1)] if qb >= 2 else []) + [(1, qb)] + ([(2, qb + 1)] if qb + 1 < NB else []) + [(3, 0)]
                        qn_, qnk = qn[slot], 'qn%d' % slot
                        qw_, qwk = qw[slot], 'qw%d' % slot
                        kn_, knk = kn[slot], 'kn%d' % slot
                        kw_, kwk = kw[slot], 'kw%d' % slot
                        vn_, vnk = vn[slot], 'vn%d' % slot
                        vw_, vwk = vw[slot], 'vw%d' % slot
                        S.dma(qn_[:], fmv[:, 12:16, c0:c0 + 128], reads=['fm%d' % s], writes=[qnk])
                        S.dma(qw_[:], fmv[:, 20:24, c0:c0 + 128], reads=['fm%d' % s], writes=[qwk])
                        k0, nkb = kbs[0], len(kbs)
                        S.dma(kn_[:, :, 0:nkb, :], fmv[:, 16:20, k0 * 128:(k0 + nkb) * 128].rearrange("p c (s t) -> p c s t", t=128), reads=['fm%d' % s], writes=[knk])
                        S.dma(kn_[:, :, 5, :], fmv[:, 16:20, 0:128], reads=['fm%d' % s], writes=[knk])
                        for (si, kb) in na_slots:
                            S.dma(vn_[:, si, :, 0:64], Q['tm'][kb * 128:(kb + 1) * 128, 1152:1664].rearrange("t (h d) -> t h d", d=64), reads=['tm%d' % s], writes=[vnk])
                        wreal = [(si, kb) for (si, kb) in w_slots if si != 3]
                        ws0, wk0, wn = wreal[0][0], wreal[0][1], len(wreal)
                        for hf in range(2):
                            S.dma(kw_[hf * 64:(hf + 1) * 64, :, ws0:ws0 + wn, :],
                                  Q['fm'][3072:3200, wk0 * 128:(wk0 + wn) * 128].rearrange("(g p) (s t) -> p g s t", p=64, t=128), reads=['fm%d' % s], writes=[kwk])
                            S.dma(kw_[hf * 64:(hf + 1) * 64, :, 3, :], Q['fm'][3072:3200, 0:128].rearrange("(g p) t -> p g t", p=64), reads=['fm%d' % s], writes=[kwk])
                        for (si, kb) in w_slots:
                            S.dma(vw_[:, si, :, 0:64], Q['tm'][kb * 128:(kb + 1) * 128, 1024:1152].rearrange("t (h d) -> t h d", d=64), reads=['tm%d' % s], writes=[vwk])
                        ya_, yk_ = yat[slot], 'yat%d' % slot
                        for hu in range(16):
                            u = ui % 2
                            ui += 1
                            is_win = hu < 8
                            h = hu if is_win else hu - 8
                            slots = w_slots if is_win else na_slots
                            ns = len(slots)
                            pS = [pb[u * 2], pb[u * 2 + 1]]
                            pSk = [pbk[u * 2], pbk[u * 2 + 1]]
                            pO, pOk = pb[4 + u], pbk[4 + u]
                            hp = (h % 2) * 64
                            for j, (si, kb) in enumerate(slots):
                                o = pS[j // 4][:, (j % 4) * 128:(j % 4 + 1) * 128]
                                if is_win:
                                    g = h // 4
                                    mm(o, kw_[hp:hp + 64, g, si, :], qw_[hp:hp + 64, h // 2, :], True, True, [kwk, qwk], [pSk[j // 4]])
                                else:
                                    mm(o, kn_[hp:hp + 64, h // 2, si, :], qn_[hp:hp + 64, h // 2, :], True, True, [knk, qnk], [pSk[j // 4]])
                            P_, Pk = Pt[u], 'Pt%d' % u
                            P2_, P2k = P2[u], 'P2%d' % u
                            n1 = min(ns, 4) * 128
                            act(lambda e, P_=P_, p=pS[0], n1=n1: e.activation(out=P_[:, 0:n1], in_=p[:, 0:n1], func=AF.Exp, scale=0.125), [pSk[0]], [Pk])
                            if ns > 4:
                                n2 = (ns - 4) * 128
                                act(lambda e, P_=P_, p=pS[1], n2=n2: e.activation(out=P_[:, 512:512 + n2], in_=p[:, 0:n2], func=AF.Exp, scale=0.125), [pSk[1]], [Pk])
                            for j, (si, kb) in enumerate(slots):
                                if is_win:
                                    E = Ewin[:, si, :]
                                    ek = 'Ewin'
                                else:
                                    E = Ena[:, vi, (h * 6 + si) * 128:(h * 6 + si + 1) * 128]
                                    ek = 'Ena'
                                dve(lambda e, P2_=P2_, P_=P_, j=j, E=E: e.tensor_tensor(out=P2_[:, j * 128:(j + 1) * 128], in0=P_[:, j * 128:(j + 1) * 128], in1=E, op=ALU.mult), [Pk, ek], [P2k])
                            for j, (si, kb) in enumerate(slots):
                                if is_win:
                                    V = vw_[:, si, h // 4, :]
                                    vk = vwk
                                else:
                                    V = vn_[:, si, h, :]
                                    vk = vnk
                                mm(pO[:, 0:65], P2_[:, j * 128:(j + 1) * 128], V, j == 0, j == ns - 1, [P2k, vk], [pOk])
                            r_, rk = rec[u], 'rec%d' % u
                            if is_win:
                                dve(lambda e, r_=r_, pO=pO, h=h: e.tensor_tensor(out=r_[:], in0=pO[:, 64:65], in1=esink[:, h:h + 1], op=ALU.add), [pOk, 'esink'], [rk])
                                dve(lambda e, r_=r_: e.reciprocal(out=r_[:], in_=r_[:]), [rk], [rk])
                            else:
                                dve(lambda e, r_=r_, pO=pO: e.reciprocal(out=r_[:], in_=pO[:, 64:65]), [pOk], [rk])
                            act(lambda e, ya_=ya_, pO=pO, r_=r_, hu=hu: e.activation(out=ya_[:, hu * 64:(hu + 1) * 64], in_=pO[:, 0:64], func=AF.Copy, scale=r_[:, 0:1]), [pOk, rk], [yk_])
                        yT_, yTk = yTa[slot], 'yTa%d' % slot
                        pv = pb[6 + slot][:].bitcast(BF16)
                        for c in range(8):
                            tr(pv[:, c * 128:(c + 1) * 128], ya_[:, c * 128:(c + 1) * 128], identb[:], [yk_, 'identb'], [pbk[6 + slot]])
                        dve(lambda e, yT_=yT_, pv=pv: e.tensor_copy(out=yT_[:].rearrange("p a b -> p (a b)"), in_=pv[:, :]), [pbk[6 + slot]], [yTk])
                        S.dma(Q['yT'][1024:2048, :].rearrange("(c p) t -> p c t", p=128)[:, :, c0:c0 + 128], yT_[:], reads=[yTk], writes=['yT%d' % s])
            S.barrier()

            with ExitStack() as st:
                Wo = load_weight_bf16(st, 'Wo', C['w_out'], 2048, D, None)
                npost = T(st, 'npost', [128, 8])
                S.dma(npost[:], C['npost'][:, :], writes=['npost'])
                yin = [T(st, 'yin%d' % i, [128, 16, 512], BF16) for i in range(2)]
                hres = [T(st, 'hres%d' % i, [128, 8, 512]) for i in range(2)]
                mix = T(st, 'mix', [128, 8, 512])
                sq = T(st, 'sq', [128, 8, 512], BF16)
                rstd = T(st, 'rstd', [128, 512])
                hout = [T(st, 'hout%d' % i, [128, 8, 512]) for i in range(2)]
                ti = 0
                npb = 0
                for s in range(NS):
                    Q = SEQ[s]
                    hsrc = Q['h0'] if li == 0 else Q['h1']
                    for (t0, tw) in tiles_of(TPs[s]):
                        slot = ti % 2
                        ti += 1
                        yi, yik = yin[slot], 'yin%d' % slot
                        S.dma(yi[:, :, :tw], Q['yT'].rearrange("(c p) t -> p c t", p=128)[:, :, t0:t0 + tw], reads=['yT%d' % s], writes=[yik])
                        hr, hrk = hres[slot], 'hres%d' % slot
                        S.dma(hr[:, :, :tw], hsrc.rearrange("(c p) t -> p c t", p=128)[:, :, t0:t0 + tw], writes=[hrk])
                        for m in range(8):
                            p, pk = pb[npb % 4], pbk[npb % 4]
                            npb += 1
                            for c in range(16):
                                mm(p[:, :tw], Wo[:, c, m * 128:(m + 1) * 128], yi[:, c, :tw], c == 0, c == 15, ['Wo', yik], [pk])
                            if m % 2 == 0:
                                act(lambda e, p=p, m=m: e.activation(out=mix[:, m, :tw], in_=p[:, :tw], func=AF.Copy), [pk], ['mix'])
                            else:
                                dve(lambda e, p=p, m=m: e.tensor_copy(out=mix[:, m, :tw], in_=p[:, :tw]), [pk], ['mix'])
                        ho, hok = hout[slot], 'hout%d' % slot
                        post_norm_residual(mix, 'mix', hr, hrk, npost, 'npost', tw, sq, rstd, ho, hok, t0 == 0)
                        S.dma(Q['hmid'].rearrange("(c p) t -> p c t", p=128)[:, :, t0:t0 + tw], ho[:, :, :tw], reads=[hok], writes=['hmid%d' % s])
            S.barrier()

            with ExitStack() as st:
                Wu = load_weight_bf16(st, 'Wu', C['w_up'], D, 2 * DFF, C['fpre'])
                fcw = T(st, 'fcw', [128, 132])
                fcb = T(st, 'fcb', [128, 44])
                S.dma(fcw[:], C['fcw'][:, :], writes=['fcw'])
                S.dma(fcb[:], C['fcb'][:, :], writes=['fcb'])
                tl = {'ht': [T(st, 'ht%d' % i, [128, 8, 512]) for i in range(2)], 'sq': T(st, 'sq', [128, 8, 512], BF16),
                      'rstd': T(st, 'rstd', [128, 512]), 'aT': [T(st, 'aT%d' % i, [128, 8, 512], BF16) for i in range(2)]}
                gpre = [T(st, 'gpre%d' % i, [128, 512]) for i in range(4)]
                gc = [T(st, 'gc%d' % i, [128, 512]) for i in range(4)]
                t3 = [T(st, 't3%d' % i, [128, 512]) for i in range(2)]
                ao = [T(st, 'ao%d' % i, [128, 512], BF16) for i in range(2)]
                ti = 0
                npb = 0
                ng = 0
                nt = 0
                for s in range(NS):
                    Q = SEQ[s]
                    TP = TPs[s]
                    t0 = 0
                    while t0 < TP:
                        tw = min(510, TP - t0)
                        slot = ti % 2
                        ti += 1
                        lo, hi = max(t0 - 1, 0), min(t0 + tw + 1, TP)
                        iw = tw + 2
                        ht_, aT, sq_, rstd_ = tl['ht'][slot], tl['aT'][slot], tl['sq'], tl['rstd']
                        hk, ak = 'ht%d' % slot, 'aT%d' % slot
                        if lo != t0 - 1 or hi != t0 + tw + 1:
                            S.op('pool', lambda e, ht_=ht_: e.memset(ht_[:], 0.0), [], [hk])
                        S.dma(ht_[:, :, lo - (t0 - 1):hi - (t0 - 1)], Q['hmid'].rearrange("(c p) t -> p c t", p=128)[:, :, lo:hi], reads=['hmid%d' % s], writes=[hk])
                        act(lambda e, ht_=ht_, iw=iw: e.activation(out=sq_[:, :, :iw], in_=ht_[:, :, :iw], func=AF.Square), [hk], ['sq'])
                        for c in range(8):
                            mm(pb[7][:, :iw], onesb[:], sq_[:, c, :iw], c == 0, c == 7, ['sq', 'onesb'], ['pb7'])
                        act(lambda e, iw=iw: e.activation(out=rstd_[:, :iw], in_=pb[7][:, :iw], func=AF.Ln, scale=1.0 / D, bias=EPS), ['pb7'], ['rstd'])
                        act(lambda e, iw=iw: e.activation(out=rstd_[:, :iw], in_=rstd_[:, :iw], func=AF.Exp, scale=-0.5), ['rstd'], ['rstd'])
                        dve(lambda e, ht_=ht_, aT=aT, iw=iw: e.tensor_tensor(out=aT[:, :, :iw], in0=ht_[:, :, :iw],
                                                                            in1=rstd_[:, :iw].unsqueeze(1).broadcast_to([128, 8, iw]), op=ALU.mult), [hk, 'rstd'], [ak])
                        for jj in range(22):
                            gcs = []
                            for which in range(2):
                                m = jj + 22 * which
                                p, pk = pb[npb % 4], pbk[npb % 4]
                                npb += 1
                                for c in range(8):
                                    mm(p[:, :iw], Wu[:, c, m * 128:(m + 1) * 128], aT[:, c, :iw], c == 0, c == 7, ['Wu', ak], [pk])
                                gp, gpk = gpre[ng % 4], 'gpre%d' % (ng % 4)
                                g_, gk = gc[ng % 4], 'gc%d' % (ng % 4)
                                ng += 1
                                act(lambda e, gp=gp, p=p, iw=iw: e.activation(out=gp[:, :iw], in_=p[:, :iw], func=AF.Copy), [pk], [gpk])
                                eng = 'dve' if which == 0 else 'pool'
                                S.op(eng, lambda e, g_=g_, gp=gp, m=m, tw=tw: e.tensor_scalar(out=g_[:, :tw], in0=gp[:, 0:tw], scalar1=fcw[:, m * 3:m * 3 + 1], scalar2=fcb[:, m:m + 1],
                                                                                        op0=ALU.mult, op1=ALU.add), [gpk, 'fcw', 'fcb'], [gk])
                                for j in (1, 2):
                                    dve(lambda e, g_=g_, gp=gp, m=m, j=j, tw=tw: e.scalar_tensor_tensor(out=g_[:, :tw], in0=gp[:, j:j + tw], scalar=fcw[:, m * 3 + j:m * 3 + j + 1],
                                                                                                  in1=g_[:, :tw], op0=ALU.mult, op1=ALU.add), [gpk, 'fcw', gk], [gk])
                                gcs.append((g_, gk))
                            (gg, ggk), (gu, guk) = gcs
                            t_, tk = t3[nt % 2], 't3%d' % (nt % 2)
                            a_, ak2 = ao[nt % 2], 'ao%d' % (nt % 2)
                            nt += 1
                            act(lambda e, t_=t_, gg=gg, tw=tw: e.activation(out=t_[:, :tw], in_=gg[:, :tw], func=AF.Square), [ggk], [tk])
                            S.op('pool', lambda e, t_=t_, tw=tw: e.tensor_scalar(out=t_[:, :tw], in0=t_[:, :tw], scalar1=0.044715, scalar2=1.0, op0=ALU.mult, op1=ALU.add), [tk], [tk])
                            S.op('pool', lambda e, t_=t_, gg=gg, tw=tw: e.tensor_tensor(out=t_[:, :tw], in0=t_[:, :tw], in1=gg[:, :tw], op=ALU.mult), [tk, ggk], [tk])
                            act(lambda e, t_=t_, tw=tw: e.activation(out=t_[:, :tw], in_=t_[:, :tw], func=AF.Sigmoid, scale=1.5957691216057308), [tk], [tk])
                            S.op('pool', lambda e, t_=t_, gg=gg, tw=tw: e.tensor_tensor(out=t_[:, :tw], in0=t_[:, :tw], in1=gg[:, :tw], op=ALU.mult), [tk, ggk], [tk])
                            dve(lambda e, t_=t_, gu=gu, a_=a_, tw=tw: e.tensor_tensor(out=a_[:, :tw], in0=t_[:, :tw], in1=gu[:, :tw], op=ALU.mult), [tk, guk], [ak2])
                            S.dma(Q['act'][jj * 128:(jj + 1) * 128, t0:t0 + tw], a_[:, :tw], reads=[ak2], writes=['actd%d' % s])
                        t0 += tw
            S.barrier()

            with ExitStack() as st:
                Wd = load_weight_bf16(st, 'Wd', C['w_down'], DFF, D, None)
                fpost = T(st, 'fpost', [128, 8])
                S.dma(fpost[:], C['fpost'][:, :], writes=['fpost'])
                ain = [T(st, 'ain%d' % i, [128, 22, 512], BF16) for i in range(2)]
                hres = [T(st, 'hres%d' % i, [128, 8, 512]) for i in range(2)]
                mix = T(st, 'mix', [128, 8, 512])
                sq = T(st, 'sq', [128, 8, 512], BF16)
                rstd = T(st, 'rstd', [128, 512])
                hout = [T(st, 'hout%d' % i, [128, 8, 512]) for i in range(2)]
                ti = 0
                npb = 0
                for s in range(NS):
                    Q = SEQ[s]
                    for (t0, tw) in tiles_of(TPs[s]):
                        slot = ti % 2
                        ti += 1
                        ai, aik = ain[slot], 'ain%d' % slot
                        S.dma(ai[:, :, :tw], Q['act'].rearrange("(c p) t -> p c t", p=128)[:, :, t0:t0 + tw], reads=['actd%d' % s], writes=[aik])
                        hr, hrk = hres[slot], 'hres%d' % slot
                        S.dma(hr[:, :, :tw], Q['hmid'].rearrange("(c p) t -> p c t", p=128)[:, :, t0:t0 + tw], reads=['hmid%d' % s], writes=[hrk])
                        for m in range(8):
                            p, pk = pb[npb % 4], pbk[npb % 4]
                            npb += 1
                            for c in range(22):
                                mm(p[:, :tw], Wd[:, c, m * 128:(m + 1) * 128], ai[:, c, :tw], c == 0, c == 21, ['Wd', aik], [pk])
                            if m % 2 == 0:
                                act(lambda e, p=p, m=m: e.activation(out=mix[:, m, :tw], in_=p[:, :tw], func=AF.Copy), [pk], ['mix'])
                            else:
                                dve(lambda e, p=p, m=m: e.tensor_copy(out=mix[:, m, :tw], in_=p[:, :tw]), [pk], ['mix'])
                        ho, hok = hout[slot], 'hout%d' % slot
                        post_norm_residual(mix, 'mix', hr, hrk, fpost, 'fpost', tw, sq, rstd, ho, hok, t0 == 0)
                        if last_layer:
                            lo = max(t0, 128)
                            if lo < t0 + tw:
                                S.dma(Q['out'].rearrange("(c p) t -> p c t", p=128)[:, :, lo - 128:t0 + tw - 128], ho[:, :, lo - t0:tw], reads=[hok], writes=['out%d' % s], final=True)
                        else:
                            S.dma(Q['h1'].rearrange("(c p) t -> p c t", p=128)[:, :, t0:t0 + tw], ho[:, :, :tw], reads=[hok], writes=['h1%d' % s])
            S.barrier()

        with nc.Block() as block:
            S.emit(block)
        nops = S.nops
    return nc, nops


def make_seq_inputs(x, meta):
    L = x.shape[0]
    h = np.zeros((D, L + 128), np.float32)
    h[:, PADL:128] = meta.T
    h[:, 128:] = x.T
    return h


def run(seqs_per_core, P, depth, debug=False):
    ncores = len(seqs_per_core)
    seq_lens = [x.shape[0] for x in seqs_per_core[0]]
    nc, nops = build(seq_lens, depth, debug)
    g = _host_globals()
    lcs = [_host_layer_consts(i, P) for i in range(depth)]
    in_maps = []
    for c in range(ncores):
        m = dict(g)
        for i in range(depth):
            for k, v in lcs[i].items():
                m['L%d_%s' % (i, k)] = v
        for s, x in enumerate(seqs_per_core[c]):
            m['s%d_h0' % s] = make_seq_inputs(x, P['meta_tokens'])
            cos, sin = _rope_tables(x.shape[0] + 128)
            m['s%d_cos' % s] = cos
            m['s%d_sin' % s] = sin
        in_maps.append(m)
    res = run_bass_kernel_spmd(nc, in_maps, core_ids=list(range(ncores)))
    return res, nops


def kernel(**inputs):
    P = {k: np.asarray(v, dtype=np.float32) for k, v in inputs.items()}
    xp, xsm = P['x_prompt'], P['x_sample']
    depth = P['w_in'].shape[0]
    nb, ns = xp.shape[0], xsm.shape[0]
    seqs = [[xp[c % nb], xsm[c % ns]] for c in range(8)]
    res, _ = run(seqs, P, depth)
    yp = np.stack([np.ascontiguousarray(res.results[c]['s0_out'].T) for c in range(nb)], axis=0)
    ys = np.stack([np.ascontiguousarray(res.results[c]['s1_out'].T) for c in range(ns)], axis=0)
    return (yp.astype(np.float32), ys.astype(np.float32))
```

```python
import math
from contextlib import ExitStack
import numpy as np
import concourse.bass as bass
import concourse.mybir as mybir
from concourse.bass_utils import run_bass_kernel_spmd

F32 = mybir.dt.float32
BF16 = mybir.dt.bfloat16
AF = mybir.ActivationFunctionType
ALU = mybir.AluOpType

D = 1024
NMETA = 16
PADL = 112
NEG = -30000.0
EPS = 1e-6
DFF = 2816
NCOL = 5632
C_XBC, C_NQ, C_NK, C_WQ, C_WK, C_WQS, C_WKS, C_DT4, C_Z, C_WV, C_NV = 0, 1536, 2048, 2560, 3072, 3200, 3712, 3840, 3968, 4992, 5120


class _Rec:
    def __init__(self):
        self.call = None

    def __getattr__(self, name):
        def f(*a, **kw):
            self.call = (name, a, kw)
            return None
        return f


class Sched:
    COMPUTE = ('pe', 'act', 'dve', 'pool')
    KDMA = 8

    def __init__(self, nc, dma_queues=('sp', 'pool')):
        self.nc = nc
        self.eng = {'pe': nc.tensor, 'act': nc.scalar, 'dve': nc.vector, 'pool': nc.gpsimd, 'sp': nc.sync}
        self.ops = {e: [] for e in self.eng}
        self.sem = {}
        self.count = {e: 0 for e in self.COMPUTE}
        self.dma_n = {q: 0 for q in dma_queues}
        self.dma_sems = {}
        self.dma_last = {}
        self.waited = {e: {} for e in self.eng}
        self.lastw = {}
        self.readers = {}
        self.final = []
        self.nops = 0
        self.rr = 0

    def alloc(self, stack):
        nc = self.nc
        for e in self.COMPUTE:
            self.sem[e] = stack.enter_context(nc.semaphore('s_' + e))
        for q in self.dma_n:
            self.dma_sems[q] = [stack.enter_context(nc.semaphore('d_%s_%d' % (q, i))) for i in range(self.KDMA)]

    def _need(self, eng, ev, waits):
        if ev is None:
            return
        sem, val, src = ev
        if src == 'pe' and eng == 'pe':
            return
        key = id(sem)
        if self.waited[eng].get(key, 0) >= val:
            return
        cur = waits.get(key)
        if cur is None or cur[1] < val:
            waits[key] = (sem, val)

    def _deps(self, eng, reads, writes):
        waits = {}
        for b in reads:
            self._need(eng, self.lastw.get(b), waits)
        for b in writes:
            self._need(eng, self.lastw.get(b), waits)
            for ev in self.readers.get(b, ()):
                self._need(eng, ev, waits)
        for key, (sem, val) in waits.items():
            self.waited[eng][key] = val
        return list(waits.values())

    def _commit(self, ev, reads, writes):
        for b in reads:
            r = self.readers.setdefault(b, [])
            r.append(ev)
            if len(r) > 64:
                best = {}
                for e in r:
                    k = id(e[0])
                    if k not in best or best[k][1] < e[1]:
                        best[k] = e
                self.readers[b] = list(best.values())
        for b in writes:
            self.lastw[b] = ev
            self.readers[b] = []

    def op(self, eng, fn, reads=(), writes=()):
        rec = _Rec()
        fn(rec)
        name, a, kw = rec.call
        fn = (lambda e, name=name, a=a, kw=kw: getattr(e, name)(*a, **kw))
        waits = self._deps(eng, reads, writes)
        self.count[eng] += 1
        ev = (self.sem[eng], self.count[eng], eng)
        self.ops[eng].append((waits, fn, self.sem[eng], 1))
        self._commit(ev, reads, writes)
        self.nops += 1
        return ev

    def dma(self, out, in_, reads=(), writes=(), final=False, q=None):
        if q is None:
            q = ('sp', 'pool')[self.rr % 2]
            self.rr += 1
        j = self.dma_n[q]
        self.dma_n[q] += 1
        sem = self.dma_sems[q][j % self.KDMA]
        val = 16 * (j // self.KDMA + 1)
        waits = self._deps(q, reads, writes)
        if val > 16:
            key = id(sem)
            if self.waited[q].get(key, 0) < val - 16:
                waits.append((sem, val - 16))
                self.waited[q][key] = val - 16
        ev = (sem, val, 'dma')
        self.dma_last[id(sem)] = ev
        self.ops[q].append((waits, lambda e, o=out, i=in_: e.dma_start(out=o, in_=i), sem, 16))
        self._commit(ev, reads, writes)
        if final:
            self.final.append(ev)
        self.nops += 1
        return ev

    def coll(self, fn, reads=(), writes=()):
        q = 'pool'
        j = self.dma_n[q]
        self.dma_n[q] += 1
        sem = self.dma_sems[q][j % self.KDMA]
        val = 16 * (j // self.KDMA + 1)
        waits = self._deps(q, reads, writes)
        if val > 16:
            key = id(sem)
            if self.waited[q].get(key, 0) < val - 16:
                waits.append((sem, val - 16))
                self.waited[q][key] = val - 16
        ev = (sem, val, 'dma')
        self.dma_last[id(sem)] = ev
        self.ops[q].append((waits, fn, sem, 16))
        self._commit(ev, reads, writes)
        self.nops += 1
        return ev

    def barrier(self):
        evs = [(self.sem[e], self.count[e], e) for e in self.COMPUTE if self.count[e] > 0]
        evs += list(self.dma_last.values())
        for eng in self.eng:
            waits = []
            for (sem, val, src) in evs:
                if self.waited[eng].get(id(sem), 0) < val:
                    waits.append((sem, val))
                    self.waited[eng][id(sem)] = val
            if waits:
                self.ops[eng].append((waits, None, None, 0))
        self.lastw = {}
        self.readers = {}

    def emit(self, block):
        finals = list(self.final)

        def body_for(name):
            ops = self.ops[name]

            def body(e):
                for waits, fn, sem, inc in ops:
                    for (ws, wv) in waits:
                        e.wait_ge(ws, wv)
                    if fn is not None:
                        fn(e).then_inc(sem, inc)
                if name == 'sp':
                    for (s, v, _) in finals:
                        e.wait_ge(s, v)
            return body

        block.sync(body_for('sp'))
        block.tensor(body_for('pe'))
        block.scalar(body_for('act'))
        block.vector(body_for('dve'))
        block.gpsimd(body_for('pool'))


def _na_table_block(rpb, i, R, offsets):
    H = rpb.shape[0]
    out = np.full((128, H, len(offsets), 128), NEG, np.float32)
    q = np.arange(128)
    qr, qc = q // 64, q % 64
    r = 2 * i + qr
    rs = np.clip(r - 4, 0, R - 8)
    qcs = np.clip(qc - 8, 0, 64 - 16)
    k = np.arange(128)
    kr_l, kc = k // 64, k % 64
    for si, off in enumerate(offsets):
        kb = i + off
        krow = 2 * kb + kr_l
        ok = (krow[:, None] >= rs[None, :]) & (krow[:, None] < rs[None, :] + 8) & \
             (kc[:, None] >= qcs[None, :]) & (kc[:, None] < qcs[None, :] + 16)
        dr = np.clip(krow[:, None] - r[None, :] + 7, 0, 14)
        dc = np.clip(kc[:, None] - qc[None, :] + 15, 0, 30)
        for h in range(H):
            b = rpb[h][dr, dc]
            out[:, h, si, :] = np.where(ok, b, NEG)
    return out


NA_VARIANTS = {
    'int': [-2, -1, 0, 1, 2], 'top1': [0, 1, 2, 3], 'top2': [-1, 0, 1, 2], 'bot2': [-2, -1, 0, 1], 'bot1': [-3, -2, -1, 0],
    'metaq': [0, 1, 2, 3]}
NA_VORDER = ['int', 'top1', 'top2', 'bot2', 'bot1', 'metaq']


def _na_tables(rpb, mb):
    H = rpb.shape[0]
    tabs = np.full((6, 128, H, 6, 128), NEG, np.float32)
    R = 32
    NR = R // 2
    reps = {'int': 5, 'top1': 0, 'top2': 1, 'bot2': NR - 2, 'bot1': NR - 1}
    for vi, v in enumerate(NA_VORDER):
        offs = NA_VARIANTS[v]
        if v != 'metaq':
            tabs[vi, :, :, :len(offs), :] = _na_table_block(rpb, reps[v], R, offs)
        else:
            k = np.arange(128)
            for si in range(4):
                krow = 2 * si + k // 64
                kc = k % 64
                ok = (kc < 16) & (krow < 8)
                for h in range(H):
                    b = rpb[h][np.clip(7 + krow, 0, 14), np.clip(15 + kc, 0, 30)]
                    tabs[vi, :, h, si, :] = np.where(ok, b, NEG)[:, None]
        for h in range(H):
            col = np.full(128, NEG, np.float32)
            col[PADL:] = mb[h]
            tabs[vi, :, h, 5, :] = col[:, None]
    return tabs


def _win_tables():
    t = np.full((128, 4, 128), NEG, np.float32)
    k = np.arange(128)[:, None]
    q = np.arange(128)[None, :]
    t[:, 0, :] = np.where(k >= q, 0.0, NEG)
    t[:, 1, :] = 0.0
    t[:, 2, :] = np.where(k <= q, 0.0, NEG)
    t[:, 3, :] = np.where(k >= PADL, 0.0, NEG) + 0 * q
    return t


def _chunked(v, n):
    return np.ascontiguousarray(v.reshape(n, 128).T)


def _host_layer_consts(i, P):
    w_in = P['w_in'][i]
    z, xbc, dtr, wq, wk, wv, nq, nk, nv = np.split(w_in, np.cumsum([1024, 1536, 32, 512, 128, 128, 512, 512])[:].tolist(), axis=1)

    def swap(w):
        w = w.reshape(w.shape[0], -1, 64).copy()
        a = w[:, :, 0:8].copy()
        w[:, :, 0:8] = w[:, :, 8:16]
        w[:, :, 8:16] = a
        return w.reshape(w.shape[0], -1)
    dt4 = np.concatenate([dtr] * 4, axis=1)
    w_my = np.concatenate([xbc, nq, nk, wq, wk, swap(wq), swap(wk), dt4, z, wv, nv], axis=1)
    assert w_my.shape[1] == NCOL
    c = {}
    c['w_in'] = np.ascontiguousarray(w_my)
    c['npre'] = _chunked(P['norm_mix_pre'][i], 8)
    c['npost'] = _chunked(P['norm_mix_post'][i], 8)
    c['fpre'] = _chunked(P['norm_ffn_pre'][i], 8)
    c['fpost'] = _chunked(P['norm_ffn_post'][i], 8)
    cw = P['ssd_conv_w'][i]
    c['cw'] = np.ascontiguousarray(cw.T.reshape(12, 128, 5).transpose(1, 0, 2)).reshape(128, 60)
    c['cb'] = _chunked(P['ssd_conv_b'][i], 12)
    c['dtb4'] = np.tile(P['ssd_dt_bias'][i].reshape(32), 4).reshape(128, 1).astype(np.float32)
    c['alog4'] = np.tile(P['ssd_a_log'][i].reshape(32), 4).reshape(128, 1).astype(np.float32)
    c['dskip'] = P['ssd_d'][i].reshape(1, 16)
    c['snw'] = P['ssd_norm_w'][i].reshape(1, 1024)
    c['sink'] = P['win_sink'][i].reshape(1, 8)
    c['natab'] = _na_tables(P['na_rpb'][i], P['na_meta_bias'][i]).reshape(6, 128, 8 * 6 * 128)
    c['w_out'] = np.ascontiguousarray(P['w_out'][i])
    c['w_up'] = np.ascontiguousarray(P['ffn_w_up'][i])
    fw = P['ffn_conv_w'][i]
    c['fcw'] = np.ascontiguousarray(fw.T.reshape(44, 128, 3).transpose(1, 0, 2)).reshape(128, 132)
    c['fcb'] = _chunked(P['ffn_conv_b'][i], 44)
    c['w_down'] = np.ascontiguousarray(P['ffn_w_down'][i])
    return {k: np.ascontiguousarray(v, dtype=np.float32) for k, v in c.items()}


def _host_globals():
    g = {}
    g['ident'] = np.eye(128, dtype=np.float32)
    g['ones'] = np.ones((128, 128), np.float32)
    sel = np.zeros((128, 32, 128), np.float32)
    for w in range(32):
        row = w if w < 16 else 32 + w
        sel[row, w, :] = 1.0
    g['sel'] = sel.reshape(128, 32 * 128)
    k = np.arange(128)[:, None]
    l = np.arange(128)[None, :]
    g['maskf'] = np.where(k > l, NEG, 0.0).astype(np.float32)
    g['maskb'] = np.where(k < l, NEG, 0.0).astype(np.float32)
    gc = np.zeros((128, 4), np.float32)
    gc[0:32] = (1, 0, 0, 0)
    gc[32:64] = (-1, 1, 0, 1)
    gc[64:96] = (0, 1, 0, 0)
    gc[96:128] = (0, 0, 1, 0)
    g['gcoef'] = gc
    pm = np.ones((128, 512), np.float32)
    pm[:, :PADL] = 0.0
    g['padmask'] = pm
    g['wintab'] = _win_tables().reshape(128, 512)
    return g


def _rope_tables(TP):
    j = np.arange(TP)
    pos = np.maximum(j - PADL, 0).astype(np.float32)
    inv = np.power(np.float32(500000.0), -np.arange(8, dtype=np.float32) / np.float32(8)).astype(np.float32)
    ang = pos[None, :] * inv[:, None]
    cos = np.ones((128, TP), np.float32)
    sin = np.zeros((128, TP), np.float32)
    for hh in range(2):
        cos[hh * 64:hh * 64 + 8] = np.cos(ang)
        cos[hh * 64 + 8:hh * 64 + 16] = np.cos(ang)
        sin[hh * 64:hh * 64 + 8] = -np.sin(ang)
        sin[hh * 64 + 8:hh * 64 + 16] = np.sin(ang)
    return cos, sin


def tiles_of(TP, w=512):
    t = []
    t0 = 0
    while t0 < TP:
        t.append((t0, min(w, TP - t0)))
        t0 += w
    return t


def build(seq_lens, depth, debug=False):
    nc = bass.Bass("TRN2", target_bir_lowering=False)
    TPs = [L + 128 for L in seq_lens]
    NS = len(seq_lens)
    okind = "ExternalOutput" if debug else "Internal"

    def din(name, shape):
        return nc.dram_tensor(name, list(shape), F32, kind="ExternalInput").ap()

    def dscr(name, shape, dt):
        return nc.dram_tensor(name, list(shape), dt, kind=okind).ap()

    G = {k: din(k, s) for k, s in [('ident', (128, 128)), ('ones', (128, 128)), ('sel', (128, 4096)), ('maskf', (128, 128)),
                                   ('maskb', (128, 128)), ('gcoef', (128, 4)), ('padmask', (128, 512)), ('wintab', (128, 512))]}
    LC = []
    for i in range(depth):
        shapes = dict(w_in=(D, NCOL), npre=(128, 8), npost=(128, 8), fpre=(128, 8), fpost=(128, 8), cw=(128, 60), cb=(128, 12),
                      dtb4=(128, 1), alog4=(128, 1), dskip=(1, 16), snw=(1, 1024), sink=(1, 8), natab=(6, 128, 6144),
                      w_out=(2048, D), w_up=(D, 2 * DFF), fcw=(128, 132), fcb=(128, 44), w_down=(DFF, D))
        LC.append({k: din('L%d_%s' % (i, k), s) for k, s in shapes.items()})
    SEQ = []
    for s in range(NS):
        TP = TPs[s]
        d = {}
        d['h0'] = din('s%d_h0' % s, (D, TP))
        d['cos'] = din('s%d_cos' % s, (128, TP))
        d['sin'] = din('s%d_sin' % s, (128, TP))
        d['out'] = nc.dram_tensor('s%d_out' % s, [D, seq_lens[s]], F32, kind="ExternalOutput").ap()
        d['hmid'] = dscr('s%d_hmid' % s, (D, TP), F32)
        d['h1'] = dscr('s%d_h1' % s, (D, TP), F32)
        d['fm'] = dscr('s%d_fm' % s, (3200, TP), BF16)
        d['dt4'] = dscr('s%d_dt4' % s, (128, TP), F32)
        d['tm'] = dscr('s%d_tm' % s, (TP, 1664), BF16)
        d['xs'] = dscr('s%d_xs' % s, (TP, 1024), BF16)
        d['btm'] = dscr('s%d_btm' % s, (TP, 256), BF16)
        d['bct'] = dscr('s%d_bct' % s, (512, TP), BF16)
        d['yf'] = dscr('s%d_yf' % s, (TP, 1024), F32)
        d['yT'] = dscr('s%d_yT' % s, (2048, TP), BF16)
        d['act'] = dscr('s%d_act' % s, (DFF, TP), BF16)
        SEQ.append(d)

    with ExitStack() as top:
        S = Sched(nc)
        S.alloc(top)
        pb = [top.enter_context(nc.psum_tensor('pb%d' % i, [128, 512], F32)) for i in range(8)]
        pbk = ['pb%d' % i for i in range(8)]

        def act(fn, r, w):
            return S.op('act', fn, r, w)

        def dve(fn, r, w):
            return S.op('dve', fn, r, w)

        def pe(fn, r, w):
            return S.op('pe', fn, r, w)

        def mm(out, lhsT, rhs, start, stop, r, w):
            return S.op('pe', lambda e, o=out, l=lhsT, rr=rhs, s0=start, s1=stop: e.matmul(o, lhsT=l, rhs=rr, start=s0, stop=s1), r, w)

        def tr(out, in_, ident, r, w):
            return S.op('pe', lambda e, o=out, i=in_, d=ident: e.transpose(o, i, d), r, w)

        uniq = [0]

        def T(st, name, shape, dt=F32):
            uniq[0] += 1
            return st.enter_context(nc.sbuf_tensor('sb%d_%s' % (uniq[0], name), list(shape), dt))

        identf = T(top, 'identf', [128, 128])
        identb = T(top, 'identb', [128, 128], BF16)
        onesf = T(top, 'onesf', [128, 128])
        onesb = T(top, 'onesb', [128, 128], BF16)
        padmask = T(top, 'padmask', [128, 512])
        S.dma(identf[:], G['ident'][:, :], writes=['identf'])
        S.dma(onesf[:], G['ones'][:, :], writes=['onesf'])
        S.dma(padmask[:], G['padmask'][:, :], writes=['padmask'])
        dve(lambda e: e.tensor_copy(out=identb[:], in_=identf[:]), ['identf'], ['identb'])
        dve(lambda e: e.tensor_copy(out=onesb[:], in_=onesf[:]), ['onesf'], ['onesb'])

        def load_weight_bf16(st, name, wsrc, K, N, scale_src=None, piece=512):
            KC = K // 128
            wt = T(st, name, [128, KC, N], BF16)
            tmpst = ExitStack()
            if KC > 8:
                piece = 256
            stg = [T(tmpst, name + '_stg%d' % i, [128, KC, piece]) for i in range(2)]
            sc = None
            if scale_src is not None:
                sc = T(tmpst, name + '_sc', [128, KC])
                S.dma(sc[:], scale_src[:, :], writes=[name + '_sc'])
            src = wsrc.rearrange("(c p) n -> p c n", p=128)
            n0 = 0
            pi = 0
            while n0 < N:
                nw = min(piece, N - n0)
                sg = stg[pi % 2]
                sk = name + '_stg%d' % (pi % 2)
                S.dma(sg[:, :, :nw], src[:, :, n0:n0 + nw], writes=[sk])
                for c in range(KC):
                    eng = 'dve' if (c % 2 == 0) else 'act'
                    if sc is not None:
                        if eng == 'dve':
                            S.op('dve', lambda e, o=wt[:, c, n0:n0 + nw], i=sg[:, c, :nw], s=sc[:, c:c + 1]:
                                 e.tensor_scalar(out=o, in0=i, scalar1=s, scalar2=None, op0=ALU.mult), [sk, name + '_sc'], [name])
                        else:
                            S.op('act', lambda e, o=wt[:, c, n0:n0 + nw], i=sg[:, c, :nw], s=sc[:, c:c + 1]:
                                 e.activation(out=o, in_=i, func=AF.Copy, scale=s), [sk, name + '_sc'], [name])
                    else:
                        if eng == 'dve':
                            S.op('dve', lambda e, o=wt[:, c, n0:n0 + nw], i=sg[:, c, :nw]: e.tensor_copy(out=o, in_=i), [sk], [name])
                        else:
                            S.op('act', lambda e, o=wt[:, c, n0:n0 + nw], i=sg[:, c, :nw]: e.activation(out=o, in_=i, func=AF.Copy), [sk], [name])
                n0 += nw
                pi += 1
            S.barrier()
            tmpst.close()
            return wt

        def norm_tile(st_tiles, hsrc, t0, tw, slot, zero_cols=None, src_lo=None):
            ht, sq, rstd, aT = st_tiles['ht'][slot], st_tiles['sq'], st_tiles['rstd'], st_tiles['aT'][slot]
            hk, ak = 'ht%d' % slot, 'aT%d' % slot
            S.dma(ht[:, :, :tw], hsrc.rearrange("(c p) t -> p c t", p=128)[:, :, t0:t0 + tw], writes=[hk])
            act(lambda e: e.activation(out=sq[:, :, :tw], in_=ht[:, :, :tw], func=AF.Square), [hk], ['sq'])
            for c in range(8):
                mm(pb[7][:, :tw], onesb[:], sq[:, c, :tw], c == 0, c == 7, ['sq', 'onesb'], ['pb7'])
            act(lambda e: e.activation(out=rstd[:, :tw], in_=pb[7][:, :tw], func=AF.Ln, scale=1.0 / D, bias=EPS), ['pb7'], ['rstd'])
            act(lambda e: e.activation(out=rstd[:, :tw], in_=rstd[:, :tw], func=AF.Exp, scale=-0.5), ['rstd'], ['rstd'])
            dve(lambda e: e.tensor_tensor(out=aT[:, :, :tw], in0=ht[:, :, :tw],
                                          in1=rstd[:, :tw].unsqueeze(1).broadcast_to([128, 8, tw]), op=ALU.mult), [hk, 'rstd'], [ak])
            return ht, aT, hk, ak

        def post_norm_residual(mix, mixk, hres, hresk, wpost, wpostk, tw, sq, rstd, outt, outk, mask_pad):
            act(lambda e: e.activation(out=sq[:, :, :tw], in_=mix[:, :, :tw], func=AF.Square), [mixk], ['sq'])
            for c in range(8):
                mm(pb[7][:, :tw], onesb[:], sq[:, c, :tw], c == 0, c == 7, ['sq', 'onesb'], ['pb7'])
            act(lambda e: e.activation(out=rstd[:, :tw], in_=pb[7][:, :tw], func=AF.Ln, scale=1.0 / D, bias=EPS), ['pb7'], ['rstd'])
            act(lambda e: e.activation(out=rstd[:, :tw], in_=rstd[:, :tw], func=AF.Exp, scale=-0.5), ['rstd'], ['rstd'])
            dve(lambda e: e.tensor_tensor(out=mix[:, :, :tw], in0=mix[:, :, :tw],
                                          in1=rstd[:, :tw].unsqueeze(1).broadcast_to([128, 8, tw]), op=ALU.mult), [mixk, 'rstd'], [mixk])
            for c in range(8):
                dve(lambda e, c=c: e.scalar_tensor_tensor(out=outt[:, c, :tw], in0=mix[:, c, :tw], scalar=wpost[:, c:c + 1],
                                                         in1=hres[:, c, :tw], op0=ALU.mult, op1=ALU.add), [mixk, hresk, wpostk], [outk])
            if mask_pad:
                dve(lambda e: e.tensor_tensor(out=outt[:, :, :tw], in0=outt[:, :, :tw],
                                              in1=padmask[:, :tw].unsqueeze(1).broadcast_to([128, 8, tw]), op=ALU.mult), [outk, 'padmask'], [outk])

        for li in range(depth):
            C = LC[li]
            last_layer = (li == depth - 1)
            with ExitStack() as st:
                W = load_weight_bf16(st, 'W', C['w_in'], D, NCOL, C['npre'])
                tl = {'ht': [T(st, 'ht%d' % i, [128, 8, 512]) for i in range(2)], 'sq': T(st, 'sq', [128, 8, 512], BF16),
                      'rstd': T(st, 'rstd', [128, 512]), 'aT': [T(st, 'aT%d' % i, [128, 8, 512], BF16) for i in range(2)]}
                ev = [T(st, 'ev%d' % i, [128, 512], BF16) for i in range(4)]
                evf = [T(st, 'evf%d' % i, [128, 512]) for i in range(2)]
                cs = [T(st, 'cs%d' % i, [128, 512]) for i in range(2)]
                sn = [T(st, 'sn%d' % i, [128, 512]) for i in range(2)]
                r1 = T(st, 'r1', [128, 512])
                r2 = T(st, 'r2', [128, 512])
                dtb4 = T(st, 'dtb4', [128, 1])
                S.dma(dtb4[:], C['dtb4'][:, :], writes=['dtb4'])
                tme = [T(st, 'tme%d' % i, [128, 1664], BF16) for i in range(2)]
                nev = 0
                npb = 0
                ti = 0
                for s in range(NS):
                    Q = SEQ[s]
                    hsrc = Q['h0'] if li == 0 else Q['h1']
                    for (t0, tw) in tiles_of(TPs[s]):
                        slot = ti % 2
                        ti += 1
                        ht, aT, hk, ak = norm_tile(tl, hsrc, t0, tw, slot)
                        for m in range(20):
                            p = pb[npb % 4]
                            pk = pbk[npb % 4]
                            npb += 1
                            for c in range(8):
                                mm(p[:, :tw], W[:, c, m * 128:(m + 1) * 128], aT[:, c, :tw], c == 0, c == 7, ['W', ak], [pk])
                            e_ = ev[nev % 4]
                            ek = 'ev%d' % (nev % 4)
                            nev += 1
                            if m % 2 == 0:
                                act(lambda e, o=e_, p=p: e.activation(out=o[:, :tw], in_=p[:, :tw], func=AF.Copy), [pk], [ek])
                            else:
                                dve(lambda e, o=e_, p=p: e.tensor_copy(out=o[:, :tw], in_=p[:, :tw]), [pk], [ek])
                            S.dma(Q['fm'][m * 128:(m + 1) * 128, t0:t0 + tw], e_[:, :tw], reads=[ek], writes=['fm%d' % s])
                        cst, snt = cs[slot], sn[slot]
                        S.dma(cst[:, :tw], Q['cos'][:, t0:t0 + tw], writes=['cs%d' % slot])
                        S.dma(snt[:, :tw], Q['sin'][:, t0:t0 + tw], writes=['sn%d' % slot])
                        for m in range(5):
                            ca = C_WQ + m * 128
                            cbb = C_WQS + m * 128
                            pA, pB = pb[4], pb[5]
                            for c in range(8):
                                mm(pA[:, :tw], W[:, c, ca:ca + 128], aT[:, c, :tw], c == 0, c == 7, ['W', ak], ['pb4'])
                            for c in range(8):
                                mm(pB[:, :tw], W[:, c, cbb:cbb + 128], aT[:, c, :tw], c == 0, c == 7, ['W', ak], ['pb5'])
                            dve(lambda e, pA=pA: e.tensor_tensor(out=r1[:, :tw], in0=pA[:, :tw], in1=cst[:, :tw], op=ALU.mult), ['pb4', 'cs%d' % slot], ['r1'])
                            dve(lambda e, pB=pB: e.tensor_tensor(out=r2[:, :tw], in0=pB[:, :tw], in1=snt[:, :tw], op=ALU.mult), ['pb5', 'sn%d' % slot], ['r2'])
                            e_ = ev[nev % 4]
                            ek = 'ev%d' % (nev % 4)
                            nev += 1
                            dve(lambda e, o=e_: e.tensor_tensor(out=o[:, :tw], in0=r1[:, :tw], in1=r2[:, :tw], op=ALU.add), ['r1', 'r2'], [ek])
                            S.dma(Q['fm'][2560 + m * 128:2560 + (m + 1) * 128, t0:t0 + tw], e_[:, :tw], reads=[ek], writes=['fm%d' % s])
                        p = pb[6]
                        for c in range(8):
                            mm(p[:, :tw], W[:, c, C_DT4:C_DT4 + 128], aT[:, c, :tw], c == 0, c == 7, ['W', ak], ['pb6'])
                        ef = evf[slot]
                        efk = 'evf%d' % slot
                        act(lambda e, o=ef, p=p: e.activation(out=o[:, :tw], in_=p[:, :tw], func=AF.Exp, bias=dtb4[:, 0:1]), ['pb6', 'dtb4'], [efk])
                        act(lambda e, o=ef: e.activation(out=o[:, :tw], in_=o[:, :tw], func=AF.Ln, bias=1.0), [efk], [efk])
                        if t0 == 0:
                            dve(lambda e, o=ef: e.tensor_tensor(out=o[:, :tw], in0=o[:, :tw], in1=padmask[:, :tw], op=ALU.mult), [efk, 'padmask'], [efk])
                        S.dma(Q['dt4'][:, t0:t0 + tw], ef[:, :tw], reads=[efk], writes=['dt4%d' % s])
                        for sub in range(tw // 128):
                            te = tme[sub % 2]
                            tk = 'tme%d' % (sub % 2)
                            for (n0, nw) in [(0, 512), (512, 512), (1024, 512), (1536, 128)]:
                                p = pb[npb % 4]
                                pk = pbk[npb % 4]
                                npb += 1
                                for c in range(8):
                                    mm(p[:, :nw], aT[:, c, sub * 128:(sub + 1) * 128], W[:, c, C_Z + n0:C_Z + n0 + nw], c == 0, c == 7, ['W', ak], [pk])
                                if (n0 // 512) % 2 == 0:
                                    act(lambda e, o=te, p=p, n0=n0, nw=nw: e.activation(out=o[:, n0:n0 + nw], in_=p[:, :nw], func=AF.Copy), [pk], [tk])
                                else:
                                    dve(lambda e, o=te, p=p, n0=n0, nw=nw: e.tensor_copy(out=o[:, n0:n0 + nw], in_=p[:, :nw]), [pk], [tk])
                            S.dma(Q['tm'][t0 + sub * 128:t0 + (sub + 1) * 128, :], te[:, :], reads=[tk], writes=['tm%d' % s])
            S.barrier()

            with ExitStack() as st:
                cw = T(st, 'cw', [128, 60])
                cbt = T(st, 'cbt', [128, 12])
                S.dma(cw[:], C['cw'][:, :], writes=['cw'])
                S.dma(cbt[:], C['cb'][:, :], writes=['cbt'])
                xin = [T(st, 'xin%d' % i, [128, 12, 516], BF16) for i in range(2)]
                acc = [T(st, 'acc%d' % i, [128, 512]) for i in range(2)]
                xo = [T(st, 'xo%d' % i, [128, 12, 512], BF16) for i in range(2)]
                xt = [T(st, 'xt%d' % i, [128, 1280], BF16) for i in range(2)]
                ti = 0
                na = 0
                for s in range(NS):
                    Q = SEQ[s]
                    TP = TPs[s]
                    src = Q['fm'][0:1536, :].rearrange("(c p) t -> p c t", p=128)
                    for (t0, tw) in tiles_of(TP):
                        slot = ti % 2
                        ti += 1
                        xi, xik = xin[slot], 'xin%d' % slot
                        lo, hi = max(t0 - 2, 0), min(t0 + tw + 2, TP)
                        if lo != t0 - 2 or hi != t0 + tw + 2:
                            S.op('pool', lambda e, xi=xi: e.memset(xi[:], 0.0), [], [xik])
                        S.dma(xi[:, :, lo - (t0 - 2):hi - (t0 - 2)], src[:, :, lo:hi], writes=[xik])
                        xoo, xok = xo[slot], 'xo%d' % slot
                        for c in range(12):
                            a_, ak_ = acc[na % 2], 'acc%d' % (na % 2)
                            na += 1
                            dve(lambda e, a_=a_, c=c: e.tensor_scalar(out=a_[:, :tw], in0=xi[:, c, 0:tw], scalar1=cw[:, c * 5:c * 5 + 1],
                                                                    scalar2=cbt[:, c:c + 1], op0=ALU.mult, op1=ALU.add), [xik, 'cw', 'cbt'], [ak_])
                            for j in range(1, 5):
                                dve(lambda e, a_=a_, c=c, j=j: e.scalar_tensor_tensor(out=a_[:, :tw], in0=xi[:, c, j:j + tw], scalar=cw[:, c * 5 + j:c * 5 + j + 1],
                                                                                    in1=a_[:, :tw], op0=ALU.mult, op1=ALU.add), [xik, 'cw', ak_], [ak_])
                            act(lambda e, a_=a_, c=c: e.activation(out=xoo[:, c, :tw], in_=a_[:, :tw], func=AF.Silu), [ak_], [xok])
                        S.dma(Q['bct'].rearrange("(c p) t -> p c t", p=128)[:, :, t0:t0 + tw], xoo[:, 8:12, :tw], reads=[xok], writes=['bct%d' % s])
                        for sub in range(tw // 128):
                            xtt, xtk = xt[sub % 2], 'xt%d' % (sub % 2)
                            for half in range(3):
                                cl = [(0, 1, 2, 3), (4, 5, 6, 7), (8, 9)][half]
                                p = pb[half]
                                pv = p[:].bitcast(BF16)
                                for ii, c in enumerate(cl):
                                    tr(pv[:, ii * 128:(ii + 1) * 128], xoo[:, c, sub * 128:(sub + 1) * 128], identb[:], [xok, 'identb'], [pbk[half]])
                                n = len(cl) * 128
                                if half == 1:
                                    act(lambda e, pv=pv, n=n, o=xtt, c0=cl[0]: e.activation(out=o[:, c0 * 128:c0 * 128 + n], in_=pv[:, :n], func=AF.Copy), [pbk[half]], [xtk])
                                else:
                                    dve(lambda e, pv=pv, n=n, o=xtt, c0=cl[0]: e.tensor_copy(out=o[:, c0 * 128:c0 * 128 + n], in_=pv[:, :n]), [pbk[half]], [xtk])
                            S.dma(Q['xs'][t0 + sub * 128:t0 + (sub + 1) * 128, :], xtt[:, 0:1024], reads=[xtk], writes=['xs%d' % s])
                            S.dma(Q['btm'][t0 + sub * 128:t0 + (sub + 1) * 128, :], xtt[:, 1024:1280], reads=[xtk], writes=['btm%d' % s])
            S.barrier()

            with ExitStack() as st:
                selb = T(st, 'selb', [128, 32, 128], BF16)
                maskb_ = [T(st, 'mk%d' % i, [128, 128], BF16) for i in range(2)]
                gco = T(st, 'gco', [128, 4])
                a4 = T(st, 'a4c', [128, 1])
                dsk = T(st, 'dsk', [128, 16])
                snw = T(st, 'snw', [128, 1024])
                with ExitStack() as tmp:
                    stg = T(tmp, 'selstg', [128, 4096])
                    S.dma(stg[:], G['sel'][:, :], writes=['selstg'])
                    dve(lambda e: e.tensor_copy(out=selb[:].rearrange("p a b -> p (a b)"), in_=stg[:]), ['selstg'], ['selb'])
                    S.dma(stg[:, 0:128], G['maskf'][:, :], writes=['selstg'])
                    dve(lambda e: e.tensor_copy(out=maskb_[0][:], in_=stg[:, 0:128]), ['selstg'], ['mk0'])
                    S.dma(stg[:, 0:128], G['maskb'][:, :], writes=['selstg'])
                    dve(lambda e: e.tensor_copy(out=maskb_[1][:], in_=stg[:, 0:128]), ['selstg'], ['mk1'])
                    S.barrier()
                S.dma(gco[:], G['gcoef'][:, :], writes=['gco'])
                S.dma(a4[:], C['alog4'][:, :], writes=['a4c'])
                act(lambda e: e.activation(out=a4[:], in_=a4[:], func=AF.Exp), ['a4c'], ['a4c'])
                dve(lambda e: e.tensor_scalar(out=a4[:], in0=a4[:], scalar1=-1.0, scalar2=None, op0=ALU.mult), ['a4c'], ['a4c'])
                S.dma(dsk[:], C['dskip'].partition_broadcast(128).rearrange("p a b -> p (a b)"), writes=['dsk'])
                S.dma(snw[:], C['snw'].partition_broadcast(128).rearrange("p a b -> p (a b)"), writes=['snw'])
                dtt = [T(st, 'dtt%d' % i, [128, 128]) for i in range(2)]
                a4t = T(st, 'a4t', [128, 128])
                cum = T(st, 'cum', [128, 128])
                Gt = T(st, 'Gt', [128, 128])
                Ghi = T(st, 'Ghi', [128, 128], BF16)
                Glo = T(st, 'Glo', [128, 128], BF16)
                Gtmp = T(st, 'Gtmp', [128, 128])
                cols = T(st, 'cols', [128, 128])
                ncol = T(st, 'ncol', [128, 32])
                scol = T(st, 'scol', [128, 16])
                dcol = T(st, 'dcol', [128, 16])
                cdb = T(st, 'cdb', [128, 16])
                Lm = T(st, 'Lm', [128, 16, 128], BF16)
                CBt = T(st, 'CBt', [128, 2, 128], BF16)
                MT = T(st, 'MT', [128, 16, 128], BF16)
                xs_t = [T(st, 'xs_t%d' % i, [128, 1024], BF16) for i in range(2)]
                b_t = [T(st, 'b_t%d' % i, [128, 256], BF16) for i in range(2)]
                bc_t = [T(st, 'bc_t%d' % i, [128, 4, 128], BF16) for i in range(2)]
                xdt = T(st, 'xdt', [128, 1024], BF16)
                xdd = T(st, 'xdd', [128, 1024], BF16)
                Hs = T(st, 'Hs', [128, 1024])
                Hb = T(st, 'Hb', [128, 1024], BF16)
                yacc = [T(st, 'yacc%d' % i, [128, 1024]) for i in range(2)]
                ytmp = T(st, 'ytmp', [128, 1024])
                yfl = [T(st, 'yfl%d' % i, [128, 1024]) for i in range(2)]
                zt = [T(st, 'zt%d' % i, [128, 1024], BF16) for i in range(2)]
                zs = T(st, 'zs', [128, 1024])
                ssq = T(st, 'ssq', [128, 1])
                ybf = T(st, 'ybf', [128, 1024], BF16)
                yTs = [T(st, 'yTs%d' % i, [128, 8, 128], BF16) for i in range(2)]
                onesrow = T(st, 'onesrow', [128, 128])
                dve(lambda e: e.tensor_copy(out=onesrow[:], in_=onesf[:]), ['onesf'], ['onesrow'])
                bi = 0
                for s in range(NS):
                    Q = SEQ[s]
                    TP = TPs[s]
                    NB = TP // 128
                    for dr in range(2):
                        dve(lambda e: e.memset(Hs[:], 0.0), [], ['Hs'])
                        dve(lambda e: e.memset(Hb[:], 0.0), [], ['Hb'])
                        blocks = range(NB) if dr == 0 else range(NB - 1, -1, -1)
                        hd0 = dr * 16
                        for b in blocks:
                            slot = bi % 2
                            bi += 1
                            c0 = b * 128
                            dt_, dtk = dtt[slot], 'dtt%d' % slot
                            S.dma(dt_[:], Q['dt4'][:, c0:c0 + 128], reads=['dt4%d' % s], writes=[dtk])
                            xst, xsk = xs_t[slot], 'xs_t%d' % slot
                            S.dma(xst[:], Q['xs'][c0:c0 + 128, :], reads=['xs%d' % s], writes=[xsk])
                            bt, bk = b_t[slot], 'b_t%d' % slot
                            S.dma(bt[:], Q['btm'][c0:c0 + 128, :], reads=['btm%d' % s], writes=[bk])
                            bct, bck = bc_t[slot], 'bc_t%d' % slot
                            S.dma(bct[:], Q['bct'].rearrange("(c p) t -> p c t", p=128)[:, :, c0:c0 + 128], reads=['bct%d' % s], writes=[bck])
                            dve(lambda e, dt_=dt_: e.tensor_scalar(out=a4t[:], in0=dt_[:], scalar1=a4[:, 0:1], scalar2=None, op0=ALU.mult), [dtk, 'a4c'], ['a4t'])
                            dve(lambda e: e.tensor_tensor_scan(out=cum[:], data0=onesrow[:], data1=a4t[:], initial=0.0, op0=ALU.mult, op1=ALU.add), ['a4t', 'onesrow'], ['cum'])
                            dve(lambda e: e.tensor_scalar(out=Gtmp[:], in0=cum[:, 127:128].broadcast_to([128, 128]), scalar1=gco[:, 3:4], scalar2=None, op0=ALU.mult), ['cum', 'gco'], ['Gtmp'])
                            dve(lambda e: e.scalar_tensor_tensor(out=Gtmp[:], in0=cum[:], scalar=gco[:, 0:1], in1=Gtmp[:], op0=ALU.mult, op1=ALU.add), ['cum', 'gco', 'Gtmp'], ['Gtmp'])
                            dve(lambda e: e.scalar_tensor_tensor(out=Gtmp[:], in0=a4t[:], scalar=gco[:, 1:2], in1=Gtmp[:], op0=ALU.mult, op1=ALU.add), ['a4t', 'gco', 'Gtmp'], ['Gtmp'])
                            dve(lambda e, dt_=dt_: e.scalar_tensor_tensor(out=Gt[:], in0=dt_[:], scalar=gco[:, 2:3], in1=Gtmp[:], op0=ALU.mult, op1=ALU.add), [dtk, 'gco', 'Gtmp'], ['Gt'])
                            dve(lambda e: e.tensor_copy(out=Ghi[:], in_=Gt[:]), ['Gt'], ['Ghi'])
                            dve(lambda e: e.tensor_tensor(out=Gtmp[:], in0=Gt[:], in1=Ghi[:], op=ALU.subtract), ['Gt', 'Ghi'], ['Gtmp'])
                            dve(lambda e: e.tensor_copy(out=Glo[:], in_=Gtmp[:]), ['Gtmp'], ['Glo'])
                            tr(pb[6][:, 0:128], Gt[:], identf[:], ['Gt', 'identf'], ['pb6'])
                            act(lambda e: e.activation(out=cols[:], in_=pb[6][:, 0:128], func=AF.Copy), ['pb6'], ['cols'])
                            cx0 = 0 if dr == 0 else 48
                            ot0 = 32 if dr == 0 else 16
                            a0 = 64 + hd0
                            d0 = 96 + hd0
                            dve(lambda e, cx0=cx0: e.tensor_scalar(out=ncol[:, 0:16], in0=cols[:, cx0:cx0 + 16], scalar1=-1.0, scalar2=None, op0=ALU.mult), ['cols'], ['ncol'])
                            act(lambda e, cx0=cx0: e.activation(out=scol[:], in_=cols[:, cx0:cx0 + 16], func=AF.Exp), ['cols'], ['scol'])
                            dve(lambda e, ot0=ot0, a0=a0: e.tensor_tensor(out=dcol[:], in0=cols[:, ot0:ot0 + 16], in1=cols[:, a0:a0 + 16], op=ALU.subtract), ['cols'], ['dcol'])
                            act(lambda e: e.activation(out=dcol[:], in_=dcol[:], func=AF.Exp), ['dcol'], ['dcol'])
                            mm(pb[6][:, 256:272], onesf[:], cols[:, a0:a0 + 16], True, True, ['onesf', 'cols'], ['pb6'])
                            act(lambda e: e.activation(out=cdb[:], in_=pb[6][:, 256:272], func=AF.Exp), ['pb6'], ['cdb'])
                            for half in range(2):
                                for hh in range(8):
                                    h = half * 8 + hh
                                    w = hd0 + h
                                    o = pb[half * 2 + hh // 4][:, (hh % 4) * 128:(hh % 4 + 1) * 128]
                                    pk = pbk[half * 2 + hh // 4]
                                    mm(o, selb[:, w, :], Ghi[:], True, False, ['selb', 'Ghi'], [pk])
                                    mm(o, selb[:, w, :], Glo[:], False, False, ['selb', 'Glo'], [pk])
                                    mm(o, identb[:], maskb_[dr][:], False, True, ['identb', 'mk%d' % dr], [pk])
                                    act(lambda e, o=o, h=h: e.activation(out=Lm[:, h, :], in_=o, func=AF.Exp, bias=ncol[:, h:h + 1]), [pk, 'ncol'], ['Lm'])
                            for g in range(2):
                                mm(pb[4][:, g * 128:(g + 1) * 128], bct[:, g, :], bct[:, 2 + g, :], True, True, [bck], ['pb4'])
                            act(lambda e: e.activation(out=CBt[:].rearrange("p a b -> p (a b)"), in_=pb[4][:, 0:256], func=AF.Copy), ['pb4'], ['CBt'])
                            for g in range(2):
                                dve(lambda e, g=g: e.tensor_tensor(out=MT[:, g * 8:(g + 1) * 8, :], in0=Lm[:, g * 8:(g + 1) * 8, :],
                                                                  in1=CBt[:, g:g + 1, :].broadcast_to([128, 8, 128]), op=ALU.mult), ['Lm', 'CBt'], ['MT'])
                            dve(lambda e, xst=xst, d0=d0: e.tensor_tensor(out=xdt[:].rearrange("p (h d) -> p h d", d=64), in0=xst[:].rearrange("p (h d) -> p h d", d=64),
                                                                      in1=cols[:, d0:d0 + 16].unsqueeze(2).broadcast_to([128, 16, 64]), op=ALU.mult), [xsk, 'cols'], ['xdt'])
                            dve(lambda e: e.tensor_tensor(out=xdd[:].rearrange("p (h d) -> p h d", d=64), in0=xdt[:].rearrange("p (h d) -> p h d", d=64),
                                                          in1=dcol[:].unsqueeze(2).broadcast_to([128, 16, 64]), op=ALU.mult), ['xdt', 'dcol'], ['xdd'])
                            for h in range(16):
                                mm(pb[h // 8][:, (h % 8) * 64:(h % 8 + 1) * 64], MT[:, h, :], xdt[:, h * 64:(h + 1) * 64], True, True, ['MT', 'xdt'], [pbk[h // 8]])
                            for g in range(2):
                                mm(pb[2 + g][:, :], bct[:, 2 + g, :], Hb[:, g * 512:(g + 1) * 512], True, True, [bck, 'Hb'], [pbk[2 + g]])
                            ya, yak = yacc[slot], 'yacc%d' % slot
                            for g in range(2):
                                dve(lambda e, g=g: e.tensor_tensor(out=ytmp[:, g * 512:(g + 1) * 512].rearrange("p (h d) -> p h d", d=64),
                                                                  in0=pb[2 + g][:, :].rearrange("p (h d) -> p h d", d=64),
                                                                  in1=scol[:, g * 8:(g + 1) * 8].unsqueeze(2).broadcast_to([128, 8, 64]), op=ALU.mult), [pbk[2 + g], 'scol'], ['ytmp'])
                                dve(lambda e, g=g, ya=ya: e.tensor_tensor(out=ya[:, g * 512:(g + 1) * 512], in0=pb[g][:, :], in1=ytmp[:, g * 512:(g + 1) * 512], op=ALU.add), [pbk[g], 'ytmp'], [yak])
                            for g in range(2):
                                mm(pb[4 + g][:, :], bt[:, g * 128:(g + 1) * 128], xdd[:, g * 512:(g + 1) * 512], True, True, [bk, 'xdd'], [pbk[4 + g]])
                            dve(lambda e: e.tensor_tensor(out=Hs[:].rearrange("p (h d) -> p h d", d=64), in0=Hs[:].rearrange("p (h d) -> p h d", d=64),
                                                          in1=cdb[:].unsqueeze(2).broadcast_to([128, 16, 64]), op=ALU.mult), ['Hs', 'cdb'], ['Hs'])
                            for g in range(2):
                                dve(lambda e, g=g: e.tensor_tensor(out=Hs[:, g * 512:(g + 1) * 512], in0=Hs[:, g * 512:(g + 1) * 512], in1=pb[4 + g][:, :], op=ALU.add), ['Hs', pbk[4 + g]], ['Hs'])
                            act(lambda e: e.activation(out=Hb[:], in_=Hs[:], func=AF.Copy), ['Hs'], ['Hb'])
                            if dr == 0:
                                S.dma(Q['yf'][c0:c0 + 128, :], ya[:], reads=[yak], writes=['yf%d' % s])
                            else:
                                yf_, yfk = yfl[slot], 'yfl%d' % slot
                                S.dma(yf_[:], Q['yf'][c0:c0 + 128, :], reads=['yf%d' % s], writes=[yfk])
                                z_, zk = zt[slot], 'zt%d' % slot
                                S.dma(z_[:], Q['tm'][c0:c0 + 128, 0:1024], reads=['tm%d' % s], writes=[zk])
                                dve(lambda e, ya=ya, yf_=yf_: e.tensor_tensor(out=ya[:], in0=ya[:], in1=yf_[:], op=ALU.add), [yak, yfk], [yak])
                                dve(lambda e, xst=xst: e.tensor_tensor(out=ytmp[:].rearrange("p (h d) -> p h d", d=64), in0=xst[:].rearrange("p (h d) -> p h d", d=64),
                                                                  in1=dsk[:].unsqueeze(2).broadcast_to([128, 16, 64]), op=ALU.mult), [xsk, 'dsk'], ['ytmp'])
                                dve(lambda e, ya=ya: e.tensor_tensor(out=ya[:], in0=ya[:], in1=ytmp[:], op=ALU.add), [yak, 'ytmp'], [yak])
                                act(lambda e, z_=z_: e.activation(out=zs[:], in_=z_[:], func=AF.Silu), [zk], ['zs'])
                                dve(lambda e, ya=ya: e.tensor_tensor(out=ya[:], in0=ya[:], in1=zs[:], op=ALU.mult), [yak, 'zs'], [yak])
                                act(lambda e, ya=ya: e.activation(out=zs[:], in_=ya[:], func=AF.Square, accum_out=ssq[:]), [yak], ['zs', 'ssq'])
                                act(lambda e: e.activation(out=ssq[:], in_=ssq[:], func=AF.Ln, scale=1.0 / 1024, bias=EPS), ['ssq'], ['ssq'])
                                act(lambda e: e.activation(out=ssq[:], in_=ssq[:], func=AF.Exp, scale=-0.5), ['ssq'], ['ssq'])
                                dve(lambda e, ya=ya: e.scalar_tensor_tensor(out=ybf[:], in0=ya[:], scalar=ssq[:, 0:1], in1=snw[:], op0=ALU.mult, op1=ALU.mult), [yak, 'ssq', 'snw'], ['ybf'])
                                yT_, yTk = yTs[slot], 'yTs%d' % slot
                                pv = pb[7][:].bitcast(BF16)
                                for c in range(8):
                                    tr(pv[:, c * 128:(c + 1) * 128], ybf[:, c * 128:(c + 1) * 128], identb[:], ['ybf', 'identb'], ['pb7'])
                                act(lambda e, yT_=yT_, pv=pv: e.activation(out=yT_[:].rearrange("p a b -> p (a b)"), in_=pv[:, :], func=AF.Copy), ['pb7'], [yTk])
                                S.dma(Q['yT'][0:1024, :].rearrange("(c p) t -> p c t", p=128)[:, :, c0:c0 + 128], yT_[:], reads=[yTk], writes=['yT%d' % s])
            S.barrier()

            with ExitStack() as st:
                Ena = T(st, 'Ena', [128, 6, 8 * 6 * 128], BF16)
                Ewin = T(st, 'Ewin', [128, 4, 128], BF16)
                esink = T(st, 'esink', [128, 8])
                with ExitStack() as tmp:
                    stg = [T(tmp, 'nastg%d' % i, [128, 3072]) for i in range(2)]
                    k = 0
                    for v in range(6):
                        for hf in range(2):
                            sg, sk = stg[k % 2], 'nastg%d' % (k % 2)
                            k += 1
                            S.dma(sg[:], C['natab'][v, :, hf * 3072:(hf + 1) * 3072], writes=[sk])
                            act(lambda e, sg=sg, v=v, hf=hf: e.activation(out=Ena[:, v, hf * 3072:(hf + 1) * 3072], in_=sg[:], func=AF.Exp), [sk], ['Ena'])
                    S.dma(stg[0][:, 0:512], G['wintab'][:, :], writes=['nastg0'])
                    act(lambda e: e.activation(out=Ewin[:].rearrange("p a b -> p (a b)"), in_=stg[0][:, 0:512], func=AF.Exp), ['nastg0'], ['Ewin'])
                    S.dma(esink[:], C['sink'].partition_broadcast(128).rearrange("p a b -> p (a b)"), writes=['esink'])
                    act(lambda e: e.activation(out=esink[:], in_=esink[:], func=AF.Exp), ['esink'], ['esink'])
                    S.barrier()
                qn = [T(st, 'qn%d' % i, [128, 4, 128], BF16) for i in range(2)]
                qw = [T(st, 'qw%d' % i, [128, 4, 128], BF16) for i in range(2)]
                kn = [T(st, 'kn%d' % i, [128, 4, 6, 128], BF16) for i in range(2)]
                kw = [T(st, 'kw%d' % i, [128, 2, 4, 128], BF16) for i in range(2)]
                vn = [T(st, 'vn%d' % i, [128, 6, 8, 65], BF16) for i in range(2)]
                vw = [T(st, 'vw%d' % i, [128, 4, 2, 65], BF16) for i in range(2)]
                for i in range(2):
                    S.op('pool', lambda e, t=vn[i]: e.memset(t[:], 1.0), [], ['vn%d' % i])
                    S.op('pool', lambda e, t=vw[i]: e.memset(t[:], 1.0), [], ['vw%d' % i])
                knM = T(st, 'knM', [128, 4, 128], BF16)
                vnM = T(st, 'vnM', [128, 8, 65], BF16)
                kwM = T(st, 'kwM', [128, 2, 128], BF16)
                vwM = T(st, 'vwM', [128, 2, 65], BF16)
                S.op('pool', lambda e: e.memset(vnM[:], 1.0), [], ['vnM'])
                S.op('pool', lambda e: e.memset(vwM[:], 1.0), [], ['vwM'])
                Pt = [T(st, 'Pt%d' % i, [128, 768], BF16) for i in range(2)]
                P2 = [T(st, 'P2%d' % i, [128, 768], BF16) for i in range(2)]
                rec = [T(st, 'rec%d' % i, [128, 1]) for i in range(2)]
                yat = [T(st, 'yat%d' % i, [128, 1024], BF16) for i in range(2)]
                yTa = [T(st, 'yTa%d' % i, [128, 8, 128], BF16) for i in range(2)]
                qi = 0
                ui = 0
                for s in range(NS):
                    Q = SEQ[s]
                    TP = TPs[s]
                    NB = TP // 128
                    fmv = Q['fm'].rearrange("(c p) t -> p c t", p=128)
                    S.dma(knM[:], fmv[:, 16:20, 0:128], reads=['fm%d' % s], writes=['knM'])
                    S.dma(vnM[:, :, 0:64], Q['tm'][0:128, 1152:1664].rearrange("t (h d) -> t h d", d=64), reads=['tm%d' % s], writes=['vnM'])
                    for hf in range(2):
                        S.dma(kwM[hf * 64:(hf + 1) * 64, :, :], Q['fm'][3072:3200, 0:128].rearrange("(g p) t -> p g t", p=64), reads=['fm%d' % s], writes=['kwM'])
                    S.dma(vwM[:, :, 0:64], Q['tm'][0:128, 1024:1152].rearrange("t (h d) -> t h d", d=64), reads=['tm%d' % s], writes=['vwM'])
                    for qb in range(NB):
                        slot = qi % 2
                        qi += 1
                        c0 = qb * 128
                        if qb == 0:
                            vi, kbs = 5, [1, 2, 3, 4]
                        elif qb == 1:
                            vi, kbs = 1, [1, 2, 3, 4]
                        elif qb == 2:
                            vi, kbs = 2, [1, 2, 3, 4]
                        elif qb == NB - 1:
                            vi, kbs = 4, [qb - 3, qb - 2, qb - 1, qb]
                        elif qb == NB - 2:
                            vi, kbs = 3, [qb - 2, qb - 1, qb, qb + 1]
                        else:
                            vi, kbs = 0, [qb - 2, qb - 1, qb, qb + 1, qb + 2]
                        na_slots = [(si, kb) for si, kb in enumerate(kbs)] + [(5, 0)]
                        if qb == 0:
                            w_slots = [(2, 1), (3, 0)]
                        else:
                            w_slots = ([(0, qb - 1)] if qb >= 2 else []) + [(1, qb)] + ([(2, qb + 1)] if qb + 1 < NB else []) + [(3, 0)]
                        qn_, qnk = qn[slot], 'qn%d' % slot
                        qw_, qwk = qw[slot], 'qw%d' % slot
                        kn_, knk = kn[slot], 'kn%d' % slot
                        kw_, kwk = kw[slot], 'kw%d' % slot
                        vn_, vnk = vn[slot], 'vn%d' % slot
                        vw_, vwk = vw[slot], 'vw%d' % slot
                        S.dma(qn_[:], fmv[:, 12:16, c0:c0 + 128], reads=['fm%d' % s], writes=[qnk])
                        S.dma(qw_[:], fmv[:, 20:24, c0:c0 + 128], reads=['fm%d' % s], writes=[qwk])
                        k0, nkb = kbs[0], len(kbs)
                        S.dma(kn_[:, :, 0:nkb, :], fmv[:, 16:20, k0 * 128:(k0 + nkb) * 128].rearrange("p c (s t) -> p c s t", t=128), reads=['fm%d' % s], writes=[knk])
                        for (si, kb) in na_slots[:-1]:
                            S.dma(vn_[:, si, :, 0:64], Q['tm'][kb * 128:(kb + 1) * 128, 1152:1664].rearrange("t (h d) -> t h d", d=64), reads=['tm%d' % s], writes=[vnk])
                        wreal = [(si, kb) for (si, kb) in w_slots if si != 3]
                        ws0, wk0, wn = wreal[0][0], wreal[0][1], len(wreal)
                        for hf in range(2):
                            S.dma(kw_[hf * 64:(hf + 1) * 64, :, ws0:ws0 + wn, :],
                                  Q['fm'][3072:3200, wk0 * 128:(wk0 + wn) * 128].rearrange("(g p) (s t) -> p g s t", p=64, t=128), reads=['fm%d' % s], writes=[kwk])
                        for (si, kb) in w_slots[:-1]:
                            S.dma(vw_[:, si, :, 0:64], Q['tm'][kb * 128:(kb + 1) * 128, 1024:1152].rearrange("t (h d) -> t h d", d=64), reads=['tm%d' % s], writes=[vwk])
                        ya_, yk_ = yat[slot], 'yat%d' % slot
                        for hu in range(16):
                            u = ui % 2
                            ui += 1
                            is_win = hu < 8
                            h = hu if is_win else hu - 8
                            slots = w_slots if is_win else na_slots
                            ns = len(slots)
                            pS = [pb[u * 2], pb[u * 2 + 1]]
                            pSk = [pbk[u * 2], pbk[u * 2 + 1]]
                            pO, pOk = pb[4 + u], pbk[4 + u]
                            hp = (h % 2) * 64
                            for j, (si, kb) in enumerate(slots):
                                o = pS[j // 4][:, (j % 4) * 128:(j % 4 + 1) * 128]
                                if is_win:
                                    g = h // 4
                                    if si == 3:
                                        mm(o, kwM[hp:hp + 64, g, :], qw_[hp:hp + 64, h // 2, :], True, True, ['kwM', qwk], [pSk[j // 4]])
                                    else:
                                        mm(o, kw_[hp:hp + 64, g, si, :], qw_[hp:hp + 64, h // 2, :], True, True, [kwk, qwk], [pSk[j // 4]])
                                elif si == 5:
                                    mm(o, knM[hp:hp + 64, h // 2, :], qn_[hp:hp + 64, h // 2, :], True, True, ['knM', qnk], [pSk[j // 4]])
                                else:
                                    mm(o, kn_[hp:hp + 64, h // 2, si, :], qn_[hp:hp + 64, h // 2, :], True, True, [knk, qnk], [pSk[j // 4]])
                            P_, Pk = Pt[u], 'Pt%d' % u
                            P2_, P2k = P2[u], 'P2%d' % u
                            n1 = min(ns, 4) * 128
                            act(lambda e, P_=P_, p=pS[0], n1=n1: e.activation(out=P_[:, 0:n1], in_=p[:, 0:n1], func=AF.Exp, scale=0.125), [pSk[0]], [Pk])
                            if ns > 4:
                                n2 = (ns - 4) * 128
                                act(lambda e, P_=P_, p=pS[1], n2=n2: e.activation(out=P_[:, 512:512 + n2], in_=p[:, 0:n2], func=AF.Exp, scale=0.125), [pSk[1]], [Pk])
                            for j, (si, kb) in enumerate(slots):
                                if is_win:
                                    E = Ewin[:, si, :]
                                    ek = 'Ewin'
                                else:
                                    E = Ena[:, vi, (h * 6 + si) * 128:(h * 6 + si + 1) * 128]
                                    ek = 'Ena'
                                dve(lambda e, P2_=P2_, P_=P_, j=j, E=E: e.tensor_tensor(out=P2_[:, j * 128:(j + 1) * 128], in0=P_[:, j * 128:(j + 1) * 128], in1=E, op=ALU.mult), [Pk, ek], [P2k])
                            for j, (si, kb) in enumerate(slots):
                                if is_win:
                                    V = vw_[:, si, h // 4, :] if si != 3 else vwM[:, h // 4, :]
                                    vk = vwk if si != 3 else 'vwM'
                                else:
                                    V = vn_[:, si, h, :] if si != 5 else vnM[:, h, :]
                                    vk = vnk if si != 5 else 'vnM'
                                mm(pO[:, 0:65], P2_[:, j * 128:(j + 1) * 128], V, j == 0, j == ns - 1, [P2k, vk], [pOk])
                            r_, rk = rec[u], 'rec%d' % u
                            if is_win:
                                dve(lambda e, r_=r_, pO=pO, h=h: e.tensor_tensor(out=r_[:], in0=pO[:, 64:65], in1=esink[:, h:h + 1], op=ALU.add), [pOk, 'esink'], [rk])
                                dve(lambda e, r_=r_: e.reciprocal(out=r_[:], in_=r_[:]), [rk], [rk])
                            else:
                                dve(lambda e, r_=r_, pO=pO: e.reciprocal(out=r_[:], in_=pO[:, 64:65]), [pOk], [rk])
                            act(lambda e, ya_=ya_, pO=pO, r_=r_, hu=hu: e.activation(out=ya_[:, hu * 64:(hu + 1) * 64], in_=pO[:, 0:64], func=AF.Copy, scale=r_[:, 0:1]), [pOk, rk], [yk_])
                        yT_, yTk = yTa[slot], 'yTa%d' % slot
                        pv = pb[6 + slot][:].bitcast(BF16)
                        for c in range(8):
                            tr(pv[:, c * 128:(c + 1) * 128], ya_[:, c * 128:(c + 1) * 128], identb[:], [yk_, 'identb'], [pbk[6 + slot]])
                        dve(lambda e, yT_=yT_, pv=pv: e.tensor_copy(out=yT_[:].rearrange("p a b -> p (a b)"), in_=pv[:, :]), [pbk[6 + slot]], [yTk])
                        S.dma(Q['yT'][1024:2048, :].rearrange("(c p) t -> p c t", p=128)[:, :, c0:c0 + 128], yT_[:], reads=[yTk], writes=['yT%d' % s])
            S.barrier()

            with ExitStack() as st:
                Wo = load_weight_bf16(st, 'Wo', C['w_out'], 2048, D, None)
                npost = T(st, 'npost', [128, 8])
                S.dma(npost[:], C['npost'][:, :], writes=['npost'])
                yin = [T(st, 'yin%d' % i, [128, 16, 512], BF16) for i in range(2)]
                hres = [T(st, 'hres%d' % i, [128, 8, 512]) for i in range(2)]
                mix = T(st, 'mix', [128, 8, 512])
                sq = T(st, 'sq', [128, 8, 512], BF16)
                rstd = T(st, 'rstd', [128, 512])
                hout = [T(st, 'hout%d' % i, [128, 8, 512]) for i in range(2)]
                ti = 0
                npb = 0
                for s in range(NS):
                    Q = SEQ[s]
                    hsrc = Q['h0'] if li == 0 else Q['h1']
                    for (t0, tw) in tiles_of(TPs[s]):
                        slot = ti % 2
                        ti += 1
                        yi, yik = yin[slot], 'yin%d' % slot
                        S.dma(yi[:, :, :tw], Q['yT'].rearrange("(c p) t -> p c t", p=128)[:, :, t0:t0 + tw], reads=['yT%d' % s], writes=[yik])
                        hr, hrk = hres[slot], 'hres%d' % slot
                        S.dma(hr[:, :, :tw], hsrc.rearrange("(c p) t -> p c t", p=128)[:, :, t0:t0 + tw], writes=[hrk])
                        for m in range(8):
                            p, pk = pb[npb % 4], pbk[npb % 4]
                            npb += 1
                            for c in range(16):
                                mm(p[:, :tw], Wo[:, c, m * 128:(m + 1) * 128], yi[:, c, :tw], c == 0, c == 15, ['Wo', yik], [pk])
                            if m % 2 == 0:
                                act(lambda e, p=p, m=m: e.activation(out=mix[:, m, :tw], in_=p[:, :tw], func=AF.Copy), [pk], ['mix'])
                            else:
                                dve(lambda e, p=p, m=m: e.tensor_copy(out=mix[:, m, :tw], in_=p[:, :tw]), [pk], ['mix'])
                        ho, hok = hout[slot], 'hout%d' % slot
                        post_norm_residual(mix, 'mix', hr, hrk, npost, 'npost', tw, sq, rstd, ho, hok, t0 == 0)
                        S.dma(Q['hmid'].rearrange("(c p) t -> p c t", p=128)[:, :, t0:t0 + tw], ho[:, :, :tw], reads=[hok], writes=['hmid%d' % s])
            S.barrier()

            with ExitStack() as st:
                Wu = load_weight_bf16(st, 'Wu', C['w_up'], D, 2 * DFF, C['fpre'])
                fcw = T(st, 'fcw', [128, 132])
                fcb = T(st, 'fcb', [128, 44])
                S.dma(fcw[:], C['fcw'][:, :], writes=['fcw'])
                S.dma(fcb[:], C['fcb'][:, :], writes=['fcb'])
                tl = {'ht': [T(st, 'ht%d' % i, [128, 8, 512]) for i in range(2)], 'sq': T(st, 'sq', [128, 8, 512], BF16),
                      'rstd': T(st, 'rstd', [128, 512]), 'aT': [T(st, 'aT%d' % i, [128, 8, 512], BF16) for i in range(2)]}
                gpre = [T(st, 'gpre%d' % i, [128, 512]) for i in range(4)]
                gc = [T(st, 'gc%d' % i, [128, 512]) for i in range(4)]
                t3 = [T(st, 't3%d' % i, [128, 512]) for i in range(2)]
                ao = [T(st, 'ao%d' % i, [128, 512], BF16) for i in range(2)]
                ti = 0
                npb = 0
                ng = 0
                nt = 0
                for s in range(NS):
                    Q = SEQ[s]
                    TP = TPs[s]
                    t0 = 0
                    while t0 < TP:
                        tw = min(510, TP - t0)
                        slot = ti % 2
                        ti += 1
                        lo, hi = max(t0 - 1, 0), min(t0 + tw + 1, TP)
                        iw = tw + 2
                        ht_, aT, sq_, rstd_ = tl['ht'][slot], tl['aT'][slot], tl['sq'], tl['rstd']
                        hk, ak = 'ht%d' % slot, 'aT%d' % slot
                        if lo != t0 - 1 or hi != t0 + tw + 1:
                            S.op('pool', lambda e, ht_=ht_: e.memset(ht_[:], 0.0), [], [hk])
                        S.dma(ht_[:, :, lo - (t0 - 1):hi - (t0 - 1)], Q['hmid'].rearrange("(c p) t -> p c t", p=128)[:, :, lo:hi], reads=['hmid%d' % s], writes=[hk])
                        act(lambda e, ht_=ht_, iw=iw: e.activation(out=sq_[:, :, :iw], in_=ht_[:, :, :iw], func=AF.Square), [hk], ['sq'])
                        for c in range(8):
                            mm(pb[7][:, :iw], onesb[:], sq_[:, c, :iw], c == 0, c == 7, ['sq', 'onesb'], ['pb7'])
                        act(lambda e, iw=iw: e.activation(out=rstd_[:, :iw], in_=pb[7][:, :iw], func=AF.Ln, scale=1.0 / D, bias=EPS), ['pb7'], ['rstd'])
                        act(lambda e, iw=iw: e.activation(out=rstd_[:, :iw], in_=rstd_[:, :iw], func=AF.Exp, scale=-0.5), ['rstd'], ['rstd'])
                        dve(lambda e, ht_=ht_, aT=aT, iw=iw: e.tensor_tensor(out=aT[:, :, :iw], in0=ht_[:, :, :iw],
                                                                            in1=rstd_[:, :iw].unsqueeze(1).broadcast_to([128, 8, iw]), op=ALU.mult), [hk, 'rstd'], [ak])
                        for jj in range(22):
                            gcs = []
                            for which in range(2):
                                m = jj + 22 * which
                                p, pk = pb[npb % 4], pbk[npb % 4]
                                npb += 1
                                for c in range(8):
                                    mm(p[:, :iw], Wu[:, c, m * 128:(m + 1) * 128], aT[:, c, :iw], c == 0, c == 7, ['Wu', ak], [pk])
                                gp, gpk = gpre[ng % 4], 'gpre%d' % (ng % 4)
                                g_, gk = gc[ng % 4], 'gc%d' % (ng % 4)
                                ng += 1
                                act(lambda e, gp=gp, p=p, iw=iw: e.activation(out=gp[:, :iw], in_=p[:, :iw], func=AF.Copy), [pk], [gpk])
                                eng = 'dve' if which == 0 else 'pool'
                                S.op(eng, lambda e, g_=g_, gp=gp, m=m, tw=tw: e.tensor_scalar(out=g_[:, :tw], in0=gp[:, 0:tw], scalar1=fcw[:, m * 3:m * 3 + 1], scalar2=fcb[:, m:m + 1],
                                                                                        op0=ALU.mult, op1=ALU.add), [gpk, 'fcw', 'fcb'], [gk])
                                for j in (1, 2):
                                    dve(lambda e, g_=g_, gp=gp, m=m, j=j, tw=tw: e.scalar_tensor_tensor(out=g_[:, :tw], in0=gp[:, j:j + tw], scalar=fcw[:, m * 3 + j:m * 3 + j + 1],
                                                                                                  in1=g_[:, :tw], op0=ALU.mult, op1=ALU.add), [gpk, 'fcw', gk], [gk])
                                gcs.append((g_, gk))
                            (gg, ggk), (gu, guk) = gcs
                            t_, tk = t3[nt % 2], 't3%d' % (nt % 2)
                            a_, ak2 = ao[nt % 2], 'ao%d' % (nt % 2)
                            nt += 1
                            act(lambda e, t_=t_, gg=gg, tw=tw: e.activation(out=t_[:, :tw], in_=gg[:, :tw], func=AF.Square), [ggk], [tk])
                            S.op('pool', lambda e, t_=t_, tw=tw: e.tensor_scalar(out=t_[:, :tw], in0=t_[:, :tw], scalar1=0.044715, scalar2=1.0, op0=ALU.mult, op1=ALU.add), [tk], [tk])
                            S.op('pool', lambda e, t_=t_, gg=gg, tw=tw: e.tensor_tensor(out=t_[:, :tw], in0=t_[:, :tw], in1=gg[:, :tw], op=ALU.mult), [tk, ggk], [tk])
                            act(lambda e, t_=t_, tw=tw: e.activation(out=t_[:, :tw], in_=t_[:, :tw], func=AF.Sigmoid, scale=1.5957691216057308), [tk], [tk])
                            S.op('pool', lambda e, t_=t_, gg=gg, tw=tw: e.tensor_tensor(out=t_[:, :tw], in0=t_[:, :tw], in1=gg[:, :tw], op=ALU.mult), [tk, ggk], [tk])
                            dve(lambda e, t_=t_, gu=gu, a_=a_, tw=tw: e.tensor_tensor(out=a_[:, :tw], in0=t_[:, :tw], in1=gu[:, :tw], op=ALU.mult), [tk, guk], [ak2])
                            S.dma(Q['act'][jj * 128:(jj + 1) * 128, t0:t0 + tw], a_[:, :tw], reads=[ak2], writes=['actd%d' % s])
                        t0 += tw
            S.barrier()

            with ExitStack() as st:
                Wd = load_weight_bf16(st, 'Wd', C['w_down'], DFF, D, None)
                fpost = T(st, 'fpost', [128, 8])
                S.dma(fpost[:], C['fpost'][:, :], writes=['fpost'])
                ain = [T(st, 'ain%d' % i, [128, 22, 512], BF16) for i in range(2)]
                hres = [T(st, 'hres%d' % i, [128, 8, 512]) for i in range(2)]
                mix = T(st, 'mix', [128, 8, 512])
                sq = T(st, 'sq', [128, 8, 512], BF16)
                rstd = T(st, 'rstd', [128, 512])
                hout = [T(st, 'hout%d' % i, [128, 8, 512]) for i in range(2)]
                ti = 0
                npb = 0
                for s in range(NS):
                    Q = SEQ[s]
                    for (t0, tw) in tiles_of(TPs[s]):
                        slot = ti % 2
                        ti += 1
                        ai, aik = ain[slot], 'ain%d' % slot
                        S.dma(ai[:, :, :tw], Q['act'].rearrange("(c p) t -> p c t", p=128)[:, :, t0:t0 + tw], reads=['actd%d' % s], writes=[aik])
                        hr, hrk = hres[slot], 'hres%d' % slot
                        S.dma(hr[:, :, :tw], Q['hmid'].rearrange("(c p) t -> p c t", p=128)[:, :, t0:t0 + tw], reads=['hmid%d' % s], writes=[hrk])
                        for m in range(8):
                            p, pk = pb[npb % 4], pbk[npb % 4]
                            npb += 1
                            for c in range(22):
                                mm(p[:, :tw], Wd[:, c, m * 128:(m + 1) * 128], ai[:, c, :tw], c == 0, c == 21, ['Wd', aik], [pk])
                            if m % 2 == 0:
                                act(lambda e, p=p, m=m: e.activation(out=mix[:, m, :tw], in_=p[:, :tw], func=AF.Copy), [pk], ['mix'])
                            else:
                                dve(lambda e, p=p, m=m: e.tensor_copy(out=mix[:, m, :tw], in_=p[:, :tw]), [pk], ['mix'])
                        ho, hok = hout[slot], 'hout%d' % slot
                        post_norm_residual(mix, 'mix', hr, hrk, fpost, 'fpost', tw, sq, rstd, ho, hok, t0 == 0)
                        if last_layer:
                            lo = max(t0, 128)
                            if lo < t0 + tw:
                                S.dma(Q['out'].rearrange("(c p) t -> p c t", p=128)[:, :, lo - 128:t0 + tw - 128], ho[:, :, lo - t0:tw], reads=[hok], writes=['out%d' % s], final=True)
                        else:
                            S.dma(Q['h1'].rearrange("(c p) t -> p c t", p=128)[:, :, t0:t0 + tw], ho[:, :, :tw], reads=[hok], writes=['h1%d' % s])
            S.barrier()

        with nc.Block() as block:
            S.emit(block)
        nops = S.nops
    return nc, nops


def make_seq_inputs(x, meta):
    L = x.shape[0]
    h = np.zeros((D, L + 128), np.float32)
    h[:, PADL:128] = meta.T
    h[:, 128:] = x.T
    return h


def run(seqs_per_core, P, depth, debug=False):
    ncores = len(seqs_per_core)
    seq_lens = [x.shape[0] for x in seqs_per_core[0]]
    nc, nops = build(seq_lens, depth, debug)
    g = _host_globals()
    lcs = [_host_layer_consts(i, P) for i in range(depth)]
    in_maps = []
    for c in range(ncores):
        m = dict(g)
        for i in range(depth):
            for k, v in lcs[i].items():
                m['L%d_%s' % (i, k)] = v
        for s, x in enumerate(seqs_per_core[c]):
            m['s%d_h0' % s] = make_seq_inputs(x, P['meta_tokens'])
            cos, sin = _rope_tables(x.shape[0] + 128)
            m['s%d_cos' % s] = cos
            m['s%d_sin' % s] = sin
        in_maps.append(m)
    res = run_bass_kernel_spmd(nc, in_maps, core_ids=list(range(ncores)))
    return res, nops


def kernel(**inputs):
    P = {k: np.asarray(v, dtype=np.float32) for k, v in inputs.items()}
    xp, xsm = P['x_prompt'], P['x_sample']
    depth = P['w_in'].shape[0]
    nb, ns = xp.shape[0], xsm.shape[0]
    seqs = [[xp[c % nb], xsm[c % ns]] for c in range(8)]
    res, _ = run(seqs, P, depth)
    yp = np.stack([np.ascontiguousarray(res.results[c]['s0_out'].T) for c in range(nb)], axis=0)
    ys = np.stack([np.ascontiguousarray(res.results[c]['s1_out'].T) for c in range(ns)], axis=0)
    return (yp.astype(np.float32), ys.astype(np.float32))
```

```python
import math
from contextlib import ExitStack
import numpy as np
import concourse.bass as bass
import concourse.mybir as mybir
from concourse.bass_utils import run_bass_kernel_spmd

F32 = mybir.dt.float32
BF16 = mybir.dt.bfloat16
AF = mybir.ActivationFunctionType
ALU = mybir.AluOpType

D = 1024
NMETA = 16
PADL = 112
NEG = -30000.0
EPS = 1e-6
DFF = 2816
NCOL = 5632
C_XBC, C_NQ, C_NK, C_WQ, C_WK, C_WQS, C_WKS, C_DT4, C_Z, C_WV, C_NV = 0, 1536, 2048, 2560, 3072, 3200, 3712, 3840, 3968, 4992, 5120


class _Rec:
    def __init__(self):
        self.call = None

    def __getattr__(self, name):
        def f(*a, **kw):
            self.call = (name, a, kw)
            return None
        return f


class Sched:
    COMPUTE = ('pe', 'act', 'dve', 'pool')
    KDMA = 8

    def __init__(self, nc, dma_queues=('sp', 'pool')):
        self.nc = nc
        self.eng = {'pe': nc.tensor, 'act': nc.scalar, 'dve': nc.vector, 'pool': nc.gpsimd, 'sp': nc.sync}
        self.ops = {e: [] for e in self.eng}
        self.sem = {}
        self.count = {e: 0 for e in self.COMPUTE}
        self.dma_n = {q: 0 for q in dma_queues}
        self.dma_sems = {}
        self.dma_last = {}
        self.waited = {e: {} for e in self.eng}
        self.lastw = {}
        self.readers = {}
        self.final = []
        self.nops = 0
        self.rr = 0

    def alloc(self, stack):
        nc = self.nc
        for e in self.COMPUTE:
            self.sem[e] = stack.enter_context(nc.semaphore('s_' + e))
        for q in self.dma_n:
            self.dma_sems[q] = [stack.enter_context(nc.semaphore('d_%s_%d' % (q, i))) for i in range(self.KDMA)]

    def _need(self, eng, ev, waits):
        if ev is None:
            return
        sem, val, src = ev
        if src == 'pe' and eng == 'pe':
            return
        key = id(sem)
        if self.waited[eng].get(key, 0) >= val:
            return
        cur = waits.get(key)
        if cur is None or cur[1] < val:
            waits[key] = (sem, val)

    def _deps(self, eng, reads, writes):
        waits = {}
        for b in reads:
            self._need(eng, self.lastw.get(b), waits)
        for b in writes:
            self._need(eng, self.lastw.get(b), waits)
            for ev in self.readers.get(b, ()):
                self._need(eng, ev, waits)
        for key, (sem, val) in waits.items():
            self.waited[eng][key] = val
        return list(waits.values())

    def _commit(self, ev, reads, writes):
        for b in reads:
            r = self.readers.setdefault(b, [])
            r.append(ev)
            if len(r) > 64:
                best = {}
                for e in r:
                    k = id(e[0])
                    if k not in best or best[k][1] < e[1]:
                        best[k] = e
                self.readers[b] = list(best.values())
        for b in writes:
            self.lastw[b] = ev
            self.readers[b] = []

    def op(self, eng, fn, reads=(), writes=()):
        rec = _Rec()
        fn(rec)
        name, a, kw = rec.call
        fn = (lambda e, name=name, a=a, kw=kw: getattr(e, name)(*a, **kw))
        waits = self._deps(eng, reads, writes)
        self.count[eng] += 1
        ev = (self.sem[eng], self.count[eng], eng)
        self.ops[eng].append((waits, fn, self.sem[eng], 1))
        self._commit(ev, reads, writes)
        self.nops += 1
        return ev

    def dma(self, out, in_, reads=(), writes=(), final=False, q=None):
        if q is None:
            q = 'pool' if type(out.tensor).__name__.startswith('DRam') else 'sp'
        j = self.dma_n[q]
        self.dma_n[q] += 1
        sem = self.dma_sems[q][j % self.KDMA]
        val = 16 * (j // self.KDMA + 1)
        waits = self._deps(q, reads, writes)
        if val > 16:
            key = id(sem)
            if self.waited[q].get(key, 0) < val - 16:
                waits.append((sem, val - 16))
                self.waited[q][key] = val - 16
        ev = (sem, val, 'dma')
        self.dma_last[id(sem)] = ev
        self.ops[q].append((waits, lambda e, o=out, i=in_: e.dma_start(out=o, in_=i), sem, 16))
        self._commit(ev, reads, writes)
        if final:
            self.final.append(ev)
        self.nops += 1
        return ev

    def coll(self, fn, reads=(), writes=()):
        q = 'pool'
        j = self.dma_n[q]
        self.dma_n[q] += 1
        sem = self.dma_sems[q][j % self.KDMA]
        val = 16 * (j // self.KDMA + 1)
        waits = self._deps(q, reads, writes)
        if val > 16:
            key = id(sem)
            if self.waited[q].get(key, 0) < val - 16:
                waits.append((sem, val - 16))
                self.waited[q][key] = val - 16
        ev = (sem, val, 'dma')
        self.dma_last[id(sem)] = ev
        self.ops[q].append((waits, fn, sem, 16))
        self._commit(ev, reads, writes)
        self.nops += 1
        return ev

    def barrier(self):
        evs = [(self.sem[e], self.count[e], e) for e in self.COMPUTE if self.count[e] > 0]
        evs += list(self.dma_last.values())
        for eng in self.eng:
            waits = []
            for (sem, val, src) in evs:
                if self.waited[eng].get(id(sem), 0) < val:
                    waits.append((sem, val))
                    self.waited[eng][id(sem)] = val
            if waits:
                self.ops[eng].append((waits, None, None, 0))
        self.lastw = {}
        self.readers = {}

    def emit(self, block):
        finals = list(self.final)

        def body_for(name):
            ops = self.ops[name]

            def body(e):
                for waits, fn, sem, inc in ops:
                    for (ws, wv) in waits:
                        e.wait_ge(ws, wv)
                    if fn is not None:
                        fn(e).then_inc(sem, inc)
                if name == 'sp':
                    for (s, v, _) in finals:
                        e.wait_ge(s, v)
            return body

        block.sync(body_for('sp'))
        block.tensor(body_for('pe'))
        block.scalar(body_for('act'))
        block.vector(body_for('dve'))
        block.gpsimd(body_for('pool'))


def _na_table_block(rpb, i, R, offsets):
    H = rpb.shape[0]
    out = np.full((128, H, len(offsets), 128), NEG, np.float32)
    q = np.arange(128)
    qr, qc = q // 64, q % 64
    r = 2 * i + qr
    rs = np.clip(r - 4, 0, R - 8)
    qcs = np.clip(qc - 8, 0, 64 - 16)
    k = np.arange(128)
    kr_l, kc = k // 64, k % 64
    for si, off in enumerate(offsets):
        kb = i + off
        krow = 2 * kb + kr_l
        ok = (krow[:, None] >= rs[None, :]) & (krow[:, None] < rs[None, :] + 8) & \
             (kc[:, None] >= qcs[None, :]) & (kc[:, None] < qcs[None, :] + 16)
        dr = np.clip(krow[:, None] - r[None, :] + 7, 0, 14)
        dc = np.clip(kc[:, None] - qc[None, :] + 15, 0, 30)
        for h in range(H):
            b = rpb[h][dr, dc]
            out[:, h, si, :] = np.where(ok, b, NEG)
    return out


NA_VARIANTS = {
    'int': [-2, -1, 0, 1, 2], 'top1': [0, 1, 2, 3], 'top2': [-1, 0, 1, 2], 'bot2': [-2, -1, 0, 1], 'bot1': [-3, -2, -1, 0],
    'metaq': [0, 1, 2, 3]}
NA_VORDER = ['int', 'top1', 'top2', 'bot2', 'bot1', 'metaq']


def _na_tables(rpb, mb):
    H = rpb.shape[0]
    tabs = np.full((6, 128, H, 6, 128), NEG, np.float32)
    R = 32
    NR = R // 2
    reps = {'int': 5, 'top1': 0, 'top2': 1, 'bot2': NR - 2, 'bot1': NR - 1}
    for vi, v in enumerate(NA_VORDER):
        offs = NA_VARIANTS[v]
        if v != 'metaq':
            tabs[vi, :, :, :len(offs), :] = _na_table_block(rpb, reps[v], R, offs)
        else:
            k = np.arange(128)
            for si in range(4):
                krow = 2 * si + k // 64
                kc = k % 64
                ok = (kc < 16) & (krow < 8)
                for h in range(H):
                    b = rpb[h][np.clip(7 + krow, 0, 14), np.clip(15 + kc, 0, 30)]
                    tabs[vi, :, h, si, :] = np.where(ok, b, NEG)[:, None]
        for h in range(H):
            col = np.full(128, NEG, np.float32)
            col[PADL:] = mb[h]
            tabs[vi, :, h, 5, :] = col[:, None]
    return tabs


def _win_tables():
    t = np.full((128, 4, 128), NEG, np.float32)
    k = np.arange(128)[:, None]
    q = np.arange(128)[None, :]
    t[:, 0, :] = np.where(k >= q, 0.0, NEG)
    t[:, 1, :] = 0.0
    t[:, 2, :] = np.where(k <= q, 0.0, NEG)
    t[:, 3, :] = np.where(k >= PADL, 0.0, NEG) + 0 * q
    return t


def _chunked(v, n):
    return np.ascontiguousarray(v.reshape(n, 128).T)


def _host_layer_consts(i, P):
    w_in = P['w_in'][i]
    z, xbc, dtr, wq, wk, wv, nq, nk, nv = np.split(w_in, np.cumsum([1024, 1536, 32, 512, 128, 128, 512, 512])[:].tolist(), axis=1)

    def swap(w):
        w = w.reshape(w.shape[0], -1, 64).copy()
        a = w[:, :, 0:8].copy()
        w[:, :, 0:8] = w[:, :, 8:16]
        w[:, :, 8:16] = a
        return w.reshape(w.shape[0], -1)
    dt4 = np.concatenate([dtr] * 4, axis=1)
    w_my = np.concatenate([xbc, nq, nk, wq, wk, swap(wq), swap(wk), dt4, z, wv, nv], axis=1)
    assert w_my.shape[1] == NCOL
    c = {}
    c['w_in'] = np.ascontiguousarray(w_my)
    c['npre'] = _chunked(P['norm_mix_pre'][i], 8)
    c['npost'] = _chunked(P['norm_mix_post'][i], 8)
    c['fpre'] = _chunked(P['norm_ffn_pre'][i], 8)
    c['fpost'] = _chunked(P['norm_ffn_post'][i], 8)
    cw = P['ssd_conv_w'][i]
    c['cw'] = np.ascontiguousarray(cw.T.reshape(12, 128, 5).transpose(1, 0, 2)).reshape(128, 60)
    c['cb'] = _chunked(P['ssd_conv_b'][i], 12)
    c['dtb4'] = np.tile(P['ssd_dt_bias'][i].reshape(32), 4).reshape(128, 1).astype(np.float32)
    c['alog4'] = np.tile(P['ssd_a_log'][i].reshape(32), 4).reshape(128, 1).astype(np.float32)
    c['dskip'] = P['ssd_d'][i].reshape(1, 16)
    c['snw'] = P['ssd_norm_w'][i].reshape(1, 1024)
    c['sink'] = P['win_sink'][i].reshape(1, 8)
    c['natab'] = _na_tables(P['na_rpb'][i], P['na_meta_bias'][i]).reshape(6, 128, 8 * 6 * 128)
    c['w_out'] = np.ascontiguousarray(P['w_out'][i])
    c['w_up'] = np.ascontiguousarray(P['ffn_w_up'][i])
    fw = P['ffn_conv_w'][i]
    c['fcw'] = np.ascontiguousarray(fw.T.reshape(44, 128, 3).transpose(1, 0, 2)).reshape(128, 132)
    c['fcb'] = _chunked(P['ffn_conv_b'][i], 44)
    c['w_down'] = np.ascontiguousarray(P['ffn_w_down'][i])
    return {k: np.ascontiguousarray(v, dtype=np.float32) for k, v in c.items()}


def _host_globals():
    g = {}
    g['ident'] = np.eye(128, dtype=np.float32)
    g['ones'] = np.ones((128, 128), np.float32)
    sel = np.zeros((128, 32, 128), np.float32)
    for w in range(32):
        row = w if w < 16 else 32 + w
        sel[row, w, :] = 1.0
    g['sel'] = sel.reshape(128, 32 * 128)
    k = np.arange(128)[:, None]
    l = np.arange(128)[None, :]
    g['maskf'] = np.where(k > l, NEG, 0.0).astype(np.float32)
    g['maskb'] = np.where(k < l, NEG, 0.0).astype(np.float32)
    gc = np.zeros((128, 4), np.float32)
    gc[0:32] = (1, 0, 0, 0)
    gc[32:64] = (-1, 1, 0, 1)
    gc[64:96] = (0, 1, 0, 0)
    gc[96:128] = (0, 0, 1, 0)
    g['gcoef'] = gc
    pm = np.ones((128, 512), np.float32)
    pm[:, :PADL] = 0.0
    g['padmask'] = pm
    g['wintab'] = _win_tables().reshape(128, 512)
    return g


def _rope_tables(TP):
    j = np.arange(TP)
    pos = np.maximum(j - PADL, 0).astype(np.float32)
    inv = np.power(np.float32(500000.0), -np.arange(8, dtype=np.float32) / np.float32(8)).astype(np.float32)
    ang = pos[None, :] * inv[:, None]
    cos = np.ones((128, TP), np.float32)
    sin = np.zeros((128, TP), np.float32)
    for hh in range(2):
        cos[hh * 64:hh * 64 + 8] = np.cos(ang)
        cos[hh * 64 + 8:hh * 64 + 16] = np.cos(ang)
        sin[hh * 64:hh * 64 + 8] = -np.sin(ang)
        sin[hh * 64 + 8:hh * 64 + 16] = np.sin(ang)
    return cos, sin


def tiles_of(TP, w=512):
    t = []
    t0 = 0
    while t0 < TP:
        t.append((t0, min(w, TP - t0)))
        t0 += w
    return t


def build(seq_lens, depth, debug=False):
    nc = bass.Bass("TRN2", target_bir_lowering=False)
    TPs = [L + 128 for L in seq_lens]
    NS = len(seq_lens)
    okind = "ExternalOutput" if debug else "Internal"

    def din(name, shape):
        return nc.dram_tensor(name, list(shape), F32, kind="ExternalInput").ap()

    def dscr(name, shape, dt):
        return nc.dram_tensor(name, list(shape), dt, kind=okind).ap()

    G = {k: din(k, s) for k, s in [('ident', (128, 128)), ('ones', (128, 128)), ('sel', (128, 4096)), ('maskf', (128, 128)),
                                   ('maskb', (128, 128)), ('gcoef', (128, 4)), ('padmask', (128, 512)), ('wintab', (128, 512))]}
    LC = []
    for i in range(depth):
        shapes = dict(w_in=(D, NCOL), npre=(128, 8), npost=(128, 8), fpre=(128, 8), fpost=(128, 8), cw=(128, 60), cb=(128, 12),
                      dtb4=(128, 1), alog4=(128, 1), dskip=(1, 16), snw=(1, 1024), sink=(1, 8), natab=(6, 128, 6144),
                      w_out=(2048, D), w_up=(D, 2 * DFF), fcw=(128, 132), fcb=(128, 44), w_down=(DFF, D))
        LC.append({k: din('L%d_%s' % (i, k), s) for k, s in shapes.items()})
    SEQ = []
    for s in range(NS):
        TP = TPs[s]
        d = {}
        d['h0'] = din('s%d_h0' % s, (D, TP))
        d['cos'] = din('s%d_cos' % s, (128, TP))
        d['sin'] = din('s%d_sin' % s, (128, TP))
        d['out'] = nc.dram_tensor('s%d_out' % s, [D, seq_lens[s]], F32, kind="ExternalOutput").ap()
        d['hmid'] = dscr('s%d_hmid' % s, (D, TP), F32)
        d['h1'] = dscr('s%d_h1' % s, (D, TP), F32)
        d['fm'] = dscr('s%d_fm' % s, (3200, TP), BF16)
        d['dt4'] = dscr('s%d_dt4' % s, (128, TP), F32)
        d['tm'] = dscr('s%d_tm' % s, (TP, 1664), BF16)
        d['xs'] = dscr('s%d_xs' % s, (TP, 1024), BF16)
        d['btm'] = dscr('s%d_btm' % s, (TP, 256), BF16)
        d['bct'] = dscr('s%d_bct' % s, (512, TP), BF16)
        d['yf'] = dscr('s%d_yf' % s, (TP, 1024), F32)
        d['yT'] = dscr('s%d_yT' % s, (2048, TP), BF16)
        d['act'] = dscr('s%d_act' % s, (DFF, TP), BF16)
        SEQ.append(d)

    with ExitStack() as top:
        S = Sched(nc)
        S.alloc(top)
        pb = [top.enter_context(nc.psum_tensor('pb%d' % i, [128, 512], F32)) for i in range(8)]
        pbk = ['pb%d' % i for i in range(8)]

        def act(fn, r, w):
            return S.op('act', fn, r, w)

        def dve(fn, r, w):
            return S.op('dve', fn, r, w)

        def pe(fn, r, w):
            return S.op('pe', fn, r, w)

        def mm(out, lhsT, rhs, start, stop, r, w):
            return S.op('pe', lambda e, o=out, l=lhsT, rr=rhs, s0=start, s1=stop: e.matmul(o, lhsT=l, rhs=rr, start=s0, stop=s1), r, w)

        def tr(out, in_, ident, r, w):
            return S.op('pe', lambda e, o=out, i=in_, d=ident: e.transpose(o, i, d), r, w)

        uniq = [0]

        def T(st, name, shape, dt=F32):
            uniq[0] += 1
            return st.enter_context(nc.sbuf_tensor('sb%d_%s' % (uniq[0], name), list(shape), dt))

        identf = T(top, 'identf', [128, 128])
        identb = T(top, 'identb', [128, 128], BF16)
        onesf = T(top, 'onesf', [128, 128])
        onesb = T(top, 'onesb', [128, 128], BF16)
        padmask = T(top, 'padmask', [128, 512])
        S.dma(identf[:], G['ident'][:, :], writes=['identf'])
        S.dma(onesf[:], G['ones'][:, :], writes=['onesf'])
        S.dma(padmask[:], G['padmask'][:, :], writes=['padmask'])
        dve(lambda e: e.tensor_copy(out=identb[:], in_=identf[:]), ['identf'], ['identb'])
        dve(lambda e: e.tensor_copy(out=onesb[:], in_=onesf[:]), ['onesf'], ['onesb'])

        def load_weight_bf16(st, name, wsrc, K, N, scale_src=None, piece=512):
            KC = K // 128
            wt = T(st, name, [128, KC, N], BF16)
            tmpst = ExitStack()
            if KC > 8:
                piece = 256
            stg = [T(tmpst, name + '_stg%d' % i, [128, KC, piece]) for i in range(2)]
            sc = None
            if scale_src is not None:
                sc = T(tmpst, name + '_sc', [128, KC])
                S.dma(sc[:], scale_src[:, :], writes=[name + '_sc'])
            src = wsrc.rearrange("(c p) n -> p c n", p=128)
            n0 = 0
            pi = 0
            while n0 < N:
                nw = min(piece, N - n0)
                sg = stg[pi % 2]
                sk = name + '_stg%d' % (pi % 2)
                S.dma(sg[:, :, :nw], src[:, :, n0:n0 + nw], writes=[sk])
                for c in range(KC):
                    eng = 'dve' if (c % 2 == 0) else 'act'
                    if sc is not None:
                        if eng == 'dve':
                            S.op('dve', lambda e, o=wt[:, c, n0:n0 + nw], i=sg[:, c, :nw], s=sc[:, c:c + 1]:
                                 e.tensor_scalar(out=o, in0=i, scalar1=s, scalar2=None, op0=ALU.mult), [sk, name + '_sc'], [name])
                        else:
                            S.op('act', lambda e, o=wt[:, c, n0:n0 + nw], i=sg[:, c, :nw], s=sc[:, c:c + 1]:
                                 e.activation(out=o, in_=i, func=AF.Copy, scale=s), [sk, name + '_sc'], [name])
                    else:
                        if eng == 'dve':
                            S.op('dve', lambda e, o=wt[:, c, n0:n0 + nw], i=sg[:, c, :nw]: e.tensor_copy(out=o, in_=i), [sk], [name])
                        else:
                            S.op('act', lambda e, o=wt[:, c, n0:n0 + nw], i=sg[:, c, :nw]: e.activation(out=o, in_=i, func=AF.Copy), [sk], [name])
                n0 += nw
                pi += 1
            S.barrier()
            tmpst.close()
            return wt

        def norm_tile(st_tiles, hsrc, t0, tw, slot, zero_cols=None, src_lo=None):
            ht, sq, rstd, aT = st_tiles['ht'][slot], st_tiles['sq'], st_tiles['rstd'], st_tiles['aT'][slot]
            hk, ak = 'ht%d' % slot, 'aT%d' % slot
            S.dma(ht[:, :, :tw], hsrc.rearrange("(c p) t -> p c t", p=128)[:, :, t0:t0 + tw], writes=[hk])
            act(lambda e: e.activation(out=sq[:, :, :tw], in_=ht[:, :, :tw], func=AF.Square), [hk], ['sq'])
            for c in range(8):
                mm(pb[7][:, :tw], onesb[:], sq[:, c, :tw], c == 0, c == 7, ['sq', 'onesb'], ['pb7'])
            act(lambda e: e.activation(out=rstd[:, :tw], in_=pb[7][:, :tw], func=AF.Ln, scale=1.0 / D, bias=EPS), ['pb7'], ['rstd'])
            act(lambda e: e.activation(out=rstd[:, :tw], in_=rstd[:, :tw], func=AF.Exp, scale=-0.5), ['rstd'], ['rstd'])
            dve(lambda e: e.tensor_tensor(out=aT[:, :, :tw], in0=ht[:, :, :tw],
                                          in1=rstd[:, :tw].unsqueeze(1).broadcast_to([128, 8, tw]), op=ALU.mult), [hk, 'rstd'], [ak])
            return ht, aT, hk, ak

        def post_norm_residual(mix, mixk, hres, hresk, wpost, wpostk, tw, sq, rstd, outt, outk, mask_pad):
            act(lambda e: e.activation(out=sq[:, :, :tw], in_=mix[:, :, :tw], func=AF.Square), [mixk], ['sq'])
            for c in range(8):
                mm(pb[7][:, :tw], onesb[:], sq[:, c, :tw], c == 0, c == 7, ['sq', 'onesb'], ['pb7'])
            act(lambda e: e.activation(out=rstd[:, :tw], in_=pb[7][:, :tw], func=AF.Ln, scale=1.0 / D, bias=EPS), ['pb7'], ['rstd'])
            act(lambda e: e.activation(out=rstd[:, :tw], in_=rstd[:, :tw], func=AF.Exp, scale=-0.5), ['rstd'], ['rstd'])
            dve(lambda e: e.tensor_tensor(out=mix[:, :, :tw], in0=mix[:, :, :tw],
                                          in1=rstd[:, :tw].unsqueeze(1).broadcast_to([128, 8, tw]), op=ALU.mult), [mixk, 'rstd'], [mixk])
            for c in range(8):
                dve(lambda e, c=c: e.scalar_tensor_tensor(out=outt[:, c, :tw], in0=mix[:, c, :tw], scalar=wpost[:, c:c + 1],
                                                         in1=hres[:, c, :tw], op0=ALU.mult, op1=ALU.add), [mixk, hresk, wpostk], [outk])
            if mask_pad:
                dve(lambda e: e.tensor_tensor(out=outt[:, :, :tw], in0=outt[:, :, :tw],
                                              in1=padmask[:, :tw].unsqueeze(1).broadcast_to([128, 8, tw]), op=ALU.mult), [outk, 'padmask'], [outk])

        for li in range(depth):
            C = LC[li]
            last_layer = (li == depth - 1)
            with ExitStack() as st:
                W = load_weight_bf16(st, 'W', C['w_in'], D, NCOL, C['npre'])
                tl = {'ht': [T(st, 'ht%d' % i, [128, 8, 512]) for i in range(2)], 'sq': T(st, 'sq', [128, 8, 512], BF16),
                      'rstd': T(st, 'rstd', [128, 512]), 'aT': [T(st, 'aT%d' % i, [128, 8, 512], BF16) for i in range(2)]}
                ev = [T(st, 'ev%d' % i, [128, 512], BF16) for i in range(4)]
                evf = [T(st, 'evf%d' % i, [128, 512]) for i in range(2)]
                cs = [T(st, 'cs%d' % i, [128, 512]) for i in range(2)]
                sn = [T(st, 'sn%d' % i, [128, 512]) for i in range(2)]
                r1 = T(st, 'r1', [128, 512])
                r2 = T(st, 'r2', [128, 512])
                dtb4 = T(st, 'dtb4', [128, 1])
                S.dma(dtb4[:], C['dtb4'][:, :], writes=['dtb4'])
                tme = [T(st, 'tme%d' % i, [128, 1664], BF16) for i in range(2)]
                nev = 0
                npb = 0
                ti = 0
                for s in range(NS):
                    Q = SEQ[s]
                    hsrc = Q['h0'] if li == 0 else Q['h1']
                    for (t0, tw) in tiles_of(TPs[s]):
                        slot = ti % 2
                        ti += 1
                        ht, aT, hk, ak = norm_tile(tl, hsrc, t0, tw, slot)
                        for m in range(20):
                            p = pb[npb % 4]
                            pk = pbk[npb % 4]
                            npb += 1
                            for c in range(8):
                                mm(p[:, :tw], W[:, c, m * 128:(m + 1) * 128], aT[:, c, :tw], c == 0, c == 7, ['W', ak], [pk])
                            e_ = ev[nev % 4]
                            ek = 'ev%d' % (nev % 4)
                            nev += 1
                            if m % 2 == 0:
                                act(lambda e, o=e_, p=p: e.activation(out=o[:, :tw], in_=p[:, :tw], func=AF.Copy), [pk], [ek])
                            else:
                                dve(lambda e, o=e_, p=p: e.tensor_copy(out=o[:, :tw], in_=p[:, :tw]), [pk], [ek])
                            S.dma(Q['fm'][m * 128:(m + 1) * 128, t0:t0 + tw], e_[:, :tw], reads=[ek], writes=['fm%d' % s])
                        cst, snt = cs[slot], sn[slot]
                        S.dma(cst[:, :tw], Q['cos'][:, t0:t0 + tw], writes=['cs%d' % slot])
                        S.dma(snt[:, :tw], Q['sin'][:, t0:t0 + tw], writes=['sn%d' % slot])
                        for m in range(5):
                            ca = C_WQ + m * 128
                            cbb = C_WQS + m * 128
                            pA, pB = pb[4], pb[5]
                            for c in range(8):
                                mm(pA[:, :tw], W[:, c, ca:ca + 128], aT[:, c, :tw], c == 0, c == 7, ['W', ak], ['pb4'])
                            for c in range(8):
                                mm(pB[:, :tw], W[:, c, cbb:cbb + 128], aT[:, c, :tw], c == 0, c == 7, ['W', ak], ['pb5'])
                            dve(lambda e, pA=pA: e.tensor_tensor(out=r1[:, :tw], in0=pA[:, :tw], in1=cst[:, :tw], op=ALU.mult), ['pb4', 'cs%d' % slot], ['r1'])
                            dve(lambda e, pB=pB: e.tensor_tensor(out=r2[:, :tw], in0=pB[:, :tw], in1=snt[:, :tw], op=ALU.mult), ['pb5', 'sn%d' % slot], ['r2'])
                            e_ = ev[nev % 4]
                            ek = 'ev%d' % (nev % 4)
                            nev += 1
                            dve(lambda e, o=e_: e.tensor_tensor(out=o[:, :tw], in0=r1[:, :tw], in1=r2[:, :tw], op=ALU.add), ['r1', 'r2'], [ek])
                            S.dma(Q['fm'][2560 + m * 128:2560 + (m + 1) * 128, t0:t0 + tw], e_[:, :tw], reads=[ek], writes=['fm%d' % s])
                        p = pb[6]
                        for c in range(8):
                            mm(p[:, :tw], W[:, c, C_DT4:C_DT4 + 128], aT[:, c, :tw], c == 0, c == 7, ['W', ak], ['pb6'])
                        ef = evf[slot]
                        efk = 'evf%d' % slot
                        act(lambda e, o=ef, p=p: e.activation(out=o[:, :tw], in_=p[:, :tw], func=AF.Exp, bias=dtb4[:, 0:1]), ['pb6', 'dtb4'], [efk])
                        act(lambda e, o=ef: e.activation(out=o[:, :tw], in_=o[:, :tw], func=AF.Ln, bias=1.0), [efk], [efk])
                        if t0 == 0:
                            dve(lambda e, o=ef: e.tensor_tensor(out=o[:, :tw], in0=o[:, :tw], in1=padmask[:, :tw], op=ALU.mult), [efk, 'padmask'], [efk])
                        S.dma(Q['dt4'][:, t0:t0 + tw], ef[:, :tw], reads=[efk], writes=['dt4%d' % s])
                        for sub in range(tw // 128):
                            te = tme[sub % 2]
                            tk = 'tme%d' % (sub % 2)
                            for (n0, nw) in [(0, 512), (512, 512), (1024, 512), (1536, 128)]:
                                p = pb[npb % 4]
                                pk = pbk[npb % 4]
                                npb += 1
                                for c in range(8):
                                    mm(p[:, :nw], aT[:, c, sub * 128:(sub + 1) * 128], W[:, c, C_Z + n0:C_Z + n0 + nw], c == 0, c == 7, ['W', ak], [pk])
                                if (n0 // 512) % 2 == 0:
                                    act(lambda e, o=te, p=p, n0=n0, nw=nw: e.activation(out=o[:, n0:n0 + nw], in_=p[:, :nw], func=AF.Copy), [pk], [tk])
                                else:
                                    dve(lambda e, o=te, p=p, n0=n0, nw=nw: e.tensor_copy(out=o[:, n0:n0 + nw], in_=p[:, :nw]), [pk], [tk])
                            S.dma(Q['tm'][t0 + sub * 128:t0 + (sub + 1) * 128, :], te[:, :], reads=[tk], writes=['tm%d' % s])
            S.barrier()

            with ExitStack() as st:
                cw = T(st, 'cw', [128, 60])
                cbt = T(st, 'cbt', [128, 12])
                S.dma(cw[:], C['cw'][:, :], writes=['cw'])
                S.dma(cbt[:], C['cb'][:, :], writes=['cbt'])
                xin = [T(st, 'xin%d' % i, [128, 12, 516], BF16) for i in range(2)]
                acc = [T(st, 'acc%d' % i, [128, 512]) for i in range(2)]
                xo = [T(st, 'xo%d' % i, [128, 12, 512], BF16) for i in range(2)]
                xt = [T(st, 'xt%d' % i, [128, 1280], BF16) for i in range(2)]
                ti = 0
                na = 0
                for s in range(NS):
                    Q = SEQ[s]
                    TP = TPs[s]
                    src = Q['fm'][0:1536, :].rearrange("(c p) t -> p c t", p=128)
                    for (t0, tw) in tiles_of(TP):
                        slot = ti % 2
                        ti += 1
                        xi, xik = xin[slot], 'xin%d' % slot
                        lo, hi = max(t0 - 2, 0), min(t0 + tw + 2, TP)
                        if lo != t0 - 2 or hi != t0 + tw + 2:
                            S.op('pool', lambda e, xi=xi: e.memset(xi[:], 0.0), [], [xik])
                        S.dma(xi[:, :, lo - (t0 - 2):hi - (t0 - 2)], src[:, :, lo:hi], writes=[xik])
                        xoo, xok = xo[slot], 'xo%d' % slot
                        for c in range(12):
                            a_, ak_ = acc[na % 2], 'acc%d' % (na % 2)
                            na += 1
                            dve(lambda e, a_=a_, c=c: e.tensor_scalar(out=a_[:, :tw], in0=xi[:, c, 0:tw], scalar1=cw[:, c * 5:c * 5 + 1],
                                                                    scalar2=cbt[:, c:c + 1], op0=ALU.mult, op1=ALU.add), [xik, 'cw', 'cbt'], [ak_])
                            for j in range(1, 5):
                                dve(lambda e, a_=a_, c=c, j=j: e.scalar_tensor_tensor(out=a_[:, :tw], in0=xi[:, c, j:j + tw], scalar=cw[:, c * 5 + j:c * 5 + j + 1],
                                                                                    in1=a_[:, :tw], op0=ALU.mult, op1=ALU.add), [xik, 'cw', ak_], [ak_])
                            act(lambda e, a_=a_, c=c: e.activation(out=xoo[:, c, :tw], in_=a_[:, :tw], func=AF.Silu), [ak_], [xok])
                        S.dma(Q['bct'].rearrange("(c p) t -> p c t", p=128)[:, :, t0:t0 + tw], xoo[:, 8:12, :tw], reads=[xok], writes=['bct%d' % s])
                        for sub in range(tw // 128):
                            xtt, xtk = xt[sub % 2], 'xt%d' % (sub % 2)
                            for half in range(3):
                                cl = [(0, 1, 2, 3), (4, 5, 6, 7), (8, 9)][half]
                                p = pb[half]
                                pv = p[:].bitcast(BF16)
                                for ii, c in enumerate(cl):
                                    tr(pv[:, ii * 128:(ii + 1) * 128], xoo[:, c, sub * 128:(sub + 1) * 128], identb[:], [xok, 'identb'], [pbk[half]])
                                n = len(cl) * 128
                                if half == 1:
                                    act(lambda e, pv=pv, n=n, o=xtt, c0=cl[0]: e.activation(out=o[:, c0 * 128:c0 * 128 + n], in_=pv[:, :n], func=AF.Copy), [pbk[half]], [xtk])
                                else:
                                    dve(lambda e, pv=pv, n=n, o=xtt, c0=cl[0]: e.tensor_copy(out=o[:, c0 * 128:c0 * 128 + n], in_=pv[:, :n]), [pbk[half]], [xtk])
                            S.dma(Q['xs'][t0 + sub * 128:t0 + (sub + 1) * 128, :], xtt[:, 0:1024], reads=[xtk], writes=['xs%d' % s])
                            S.dma(Q['btm'][t0 + sub * 128:t0 + (sub + 1) * 128, :], xtt[:, 1024:1280], reads=[xtk], writes=['btm%d' % s])
            S.barrier()

            with ExitStack() as st:
                selb = T(st, 'selb', [128, 32, 128], BF16)
                maskb_ = [T(st, 'mk%d' % i, [128, 128], BF16) for i in range(2)]
                gco = T(st, 'gco', [128, 4])
                a4 = T(st, 'a4c', [128, 1])
                dsk = T(st, 'dsk', [128, 16])
                snw = T(st, 'snw', [128, 1024])
                with ExitStack() as tmp:
                    stg = T(tmp, 'selstg', [128, 4096])
                    S.dma(stg[:], G['sel'][:, :], writes=['selstg'])
                    dve(lambda e: e.tensor_copy(out=selb[:].rearrange("p a b -> p (a b)"), in_=stg[:]), ['selstg'], ['selb'])
                    S.dma(stg[:, 0:128], G['maskf'][:, :], writes=['selstg'])
                    dve(lambda e: e.tensor_copy(out=maskb_[0][:], in_=stg[:, 0:128]), ['selstg'], ['mk0'])
                    S.dma(stg[:, 0:128], G['maskb'][:, :], writes=['selstg'])
                    dve(lambda e: e.tensor_copy(out=maskb_[1][:], in_=stg[:, 0:128]), ['selstg'], ['mk1'])
                    S.barrier()
                S.dma(gco[:], G['gcoef'][:, :], writes=['gco'])
                S.dma(a4[:], C['alog4'][:, :], writes=['a4c'])
                act(lambda e: e.activation(out=a4[:], in_=a4[:], func=AF.Exp), ['a4c'], ['a4c'])
                dve(lambda e: e.tensor_scalar(out=a4[:], in0=a4[:], scalar1=-1.0, scalar2=None, op0=ALU.mult), ['a4c'], ['a4c'])
                S.dma(dsk[:], C['dskip'].partition_broadcast(128).rearrange("p a b -> p (a b)"), writes=['dsk'])
                S.dma(snw[:], C['snw'].partition_broadcast(128).rearrange("p a b -> p (a b)"), writes=['snw'])
                dtt = [T(st, 'dtt%d' % i, [128, 128]) for i in range(2)]
                a4t = T(st, 'a4t', [128, 128])
                cum = T(st, 'cum', [128, 128])
                Gt = T(st, 'Gt', [128, 128])
                Ghi = T(st, 'Ghi', [128, 128], BF16)
                Glo = T(st, 'Glo', [128, 128], BF16)
                Gtmp = T(st, 'Gtmp', [128, 128])
                cols = T(st, 'cols', [128, 128])
                ncol = T(st, 'ncol', [128, 32])
                scol = T(st, 'scol', [128, 16])
                dcol = T(st, 'dcol', [128, 16])
                cdb = T(st, 'cdb', [128, 16])
                Lm = T(st, 'Lm', [128, 16, 128], BF16)
                CBt = T(st, 'CBt', [128, 2, 128], BF16)
                MT = T(st, 'MT', [128, 16, 128], BF16)
                xs_t = [T(st, 'xs_t%d' % i, [128, 1024], BF16) for i in range(2)]
                b_t = [T(st, 'b_t%d' % i, [128, 256], BF16) for i in range(2)]
                bc_t = [T(st, 'bc_t%d' % i, [128, 4, 128], BF16) for i in range(2)]
                xdt = T(st, 'xdt', [128, 1024], BF16)
                xdd = T(st, 'xdd', [128, 1024], BF16)
                Hs = T(st, 'Hs', [128, 1024])
                Hb = T(st, 'Hb', [128, 1024], BF16)
                yacc = [T(st, 'yacc%d' % i, [128, 1024]) for i in range(2)]
                ytmp = T(st, 'ytmp', [128, 1024])
                yfl = [T(st, 'yfl%d' % i, [128, 1024]) for i in range(2)]
                zt = [T(st, 'zt%d' % i, [128, 1024], BF16) for i in range(2)]
                zs = T(st, 'zs', [128, 1024])
                ssq = T(st, 'ssq', [128, 1])
                ybf = T(st, 'ybf', [128, 1024], BF16)
                yTs = [T(st, 'yTs%d' % i, [128, 8, 128], BF16) for i in range(2)]
                onesrow = T(st, 'onesrow', [128, 128])
                dve(lambda e: e.tensor_copy(out=onesrow[:], in_=onesf[:]), ['onesf'], ['onesrow'])
                bi = 0
                for s in range(NS):
                    Q = SEQ[s]
                    TP = TPs[s]
                    NB = TP // 128
                    for dr in range(2):
                        dve(lambda e: e.memset(Hs[:], 0.0), [], ['Hs'])
                        dve(lambda e: e.memset(Hb[:], 0.0), [], ['Hb'])
                        blocks = range(NB) if dr == 0 else range(NB - 1, -1, -1)
                        hd0 = dr * 16
                        for b in blocks:
                            slot = bi % 2
                            bi += 1
                            c0 = b * 128
                            dt_, dtk = dtt[slot], 'dtt%d' % slot
                            S.dma(dt_[:], Q['dt4'][:, c0:c0 + 128], reads=['dt4%d' % s], writes=[dtk])
                            xst, xsk = xs_t[slot], 'xs_t%d' % slot
                            S.dma(xst[:], Q['xs'][c0:c0 + 128, :], reads=['xs%d' % s], writes=[xsk])
                            bt, bk = b_t[slot], 'b_t%d' % slot
                            S.dma(bt[:], Q['btm'][c0:c0 + 128, :], reads=['btm%d' % s], writes=[bk])
                            bct, bck = bc_t[slot], 'bc_t%d' % slot
                            S.dma(bct[:], Q['bct'].rearrange("(c p) t -> p c t", p=128)[:, :, c0:c0 + 128], reads=['bct%d' % s], writes=[bck])
                            dve(lambda e, dt_=dt_: e.tensor_scalar(out=a4t[:], in0=dt_[:], scalar1=a4[:, 0:1], scalar2=None, op0=ALU.mult), [dtk, 'a4c'], ['a4t'])
                            dve(lambda e: e.tensor_tensor_scan(out=cum[:], data0=onesrow[:], data1=a4t[:], initial=0.0, op0=ALU.mult, op1=ALU.add), ['a4t', 'onesrow'], ['cum'])
                            dve(lambda e: e.tensor_scalar(out=Gtmp[:], in0=cum[:, 127:128].broadcast_to([128, 128]), scalar1=gco[:, 3:4], scalar2=None, op0=ALU.mult), ['cum', 'gco'], ['Gtmp'])
                            dve(lambda e: e.scalar_tensor_tensor(out=Gtmp[:], in0=cum[:], scalar=gco[:, 0:1], in1=Gtmp[:], op0=ALU.mult, op1=ALU.add), ['cum', 'gco', 'Gtmp'], ['Gtmp'])
                            dve(lambda e: e.scalar_tensor_tensor(out=Gtmp[:], in0=a4t[:], scalar=gco[:, 1:2], in1=Gtmp[:], op0=ALU.mult, op1=ALU.add), ['a4t', 'gco', 'Gtmp'], ['Gtmp'])
                            dve(lambda e, dt_=dt_: e.scalar_tensor_tensor(out=Gt[:], in0=dt_[:], scalar=gco[:, 2:3], in1=Gtmp[:], op0=ALU.mult, op1=ALU.add), [dtk, 'gco', 'Gtmp'], ['Gt'])
                            dve(lambda e: e.tensor_copy(out=Ghi[:], in_=Gt[:]), ['Gt'], ['Ghi'])
                            dve(lambda e: e.tensor_tensor(out=Gtmp[:], in0=Gt[:], in1=Ghi[:], op=ALU.subtract), ['Gt', 'Ghi'], ['Gtmp'])
                            dve(lambda e: e.tensor_copy(out=Glo[:], in_=Gtmp[:]), ['Gtmp'], ['Glo'])
                            tr(pb[6][:, 0:128], Gt[:], identf[:], ['Gt', 'identf'], ['pb6'])
                            act(lambda e: e.activation(out=cols[:], in_=pb[6][:, 0:128], func=AF.Copy), ['pb6'], ['cols'])
                            cx0 = 0 if dr == 0 else 48
                            ot0 = 32 if dr == 0 else 16
                            a0 = 64 + hd0
                            d0 = 96 + hd0
                            dve(lambda e, cx0=cx0: e.tensor_scalar(out=ncol[:, 0:16], in0=cols[:, cx0:cx0 + 16], scalar1=-1.0, scalar2=None, op0=ALU.mult), ['cols'], ['ncol'])
                            act(lambda e, cx0=cx0: e.activation(out=scol[:], in_=cols[:, cx0:cx0 + 16], func=AF.Exp), ['cols'], ['scol'])
                            dve(lambda e, ot0=ot0, a0=a0: e.tensor_tensor(out=dcol[:], in0=cols[:, ot0:ot0 + 16], in1=cols[:, a0:a0 + 16], op=ALU.subtract), ['cols'], ['dcol'])
                            act(lambda e: e.activation(out=dcol[:], in_=dcol[:], func=AF.Exp), ['dcol'], ['dcol'])
                            mm(pb[6][:, 256:272], onesf[:], cols[:, a0:a0 + 16], True, True, ['onesf', 'cols'], ['pb6'])
                            act(lambda e: e.activation(out=cdb[:], in_=pb[6][:, 256:272], func=AF.Exp), ['pb6'], ['cdb'])
                            for half in range(2):
                                for hh in range(8):
                                    h = half * 8 + hh
                                    w = hd0 + h
                                    o = pb[half * 2 + hh // 4][:, (hh % 4) * 128:(hh % 4 + 1) * 128]
                                    pk = pbk[half * 2 + hh // 4]
                                    mm(o, selb[:, w, :], Ghi[:], True, False, ['selb', 'Ghi'], [pk])
                                    mm(o, selb[:, w, :], Glo[:], False, False, ['selb', 'Glo'], [pk])
                                    mm(o, identb[:], maskb_[dr][:], False, True, ['identb', 'mk%d' % dr], [pk])
                                    act(lambda e, o=o, h=h: e.activation(out=Lm[:, h, :], in_=o, func=AF.Exp, bias=ncol[:, h:h + 1]), [pk, 'ncol'], ['Lm'])
                            for g in range(2):
                                mm(pb[4][:, g * 128:(g + 1) * 128], bct[:, g, :], bct[:, 2 + g, :], True, True, [bck], ['pb4'])
                            act(lambda e: e.activation(out=CBt[:].rearrange("p a b -> p (a b)"), in_=pb[4][:, 0:256], func=AF.Copy), ['pb4'], ['CBt'])
                            for g in range(2):
                                dve(lambda e, g=g: e.tensor_tensor(out=MT[:, g * 8:(g + 1) * 8, :], in0=Lm[:, g * 8:(g + 1) * 8, :],
                                                                  in1=CBt[:, g:g + 1, :].broadcast_to([128, 8, 128]), op=ALU.mult), ['Lm', 'CBt'], ['MT'])
                            dve(lambda e, xst=xst, d0=d0: e.tensor_tensor(out=xdt[:].rearrange("p (h d) -> p h d", d=64), in0=xst[:].rearrange("p (h d) -> p h d", d=64),
                                                                      in1=cols[:, d0:d0 + 16].unsqueeze(2).broadcast_to([128, 16, 64]), op=ALU.mult), [xsk, 'cols'], ['xdt'])
                            dve(lambda e: e.tensor_tensor(out=xdd[:].rearrange("p (h d) -> p h d", d=64), in0=xdt[:].rearrange("p (h d) -> p h d", d=64),
                                                          in1=dcol[:].unsqueeze(2).broadcast_to([128, 16, 64]), op=ALU.mult), ['xdt', 'dcol'], ['xdd'])
                            for h in range(16):
                                mm(pb[h // 8][:, (h % 8) * 64:(h % 8 + 1) * 64], MT[:, h, :], xdt[:, h * 64:(h + 1) * 64], True, True, ['MT', 'xdt'], [pbk[h // 8]])
                            for g in range(2):
                                mm(pb[2 + g][:, :], bct[:, 2 + g, :], Hb[:, g * 512:(g + 1) * 512], True, True, [bck, 'Hb'], [pbk[2 + g]])
                            ya, yak = yacc[slot], 'yacc%d' % slot
                            for g in range(2):
                                dve(lambda e, g=g: e.tensor_tensor(out=ytmp[:, g * 512:(g + 1) * 512].rearrange("p (h d) -> p h d", d=64),
                                                                  in0=pb[2 + g][:, :].rearrange("p (h d) -> p h d", d=64),
                                                                  in1=scol[:, g * 8:(g + 1) * 8].unsqueeze(2).broadcast_to([128, 8, 64]), op=ALU.mult), [pbk[2 + g], 'scol'], ['ytmp'])
                                dve(lambda e, g=g, ya=ya: e.tensor_tensor(out=ya[:, g * 512:(g + 1) * 512], in0=pb[g][:, :], in1=ytmp[:, g * 512:(g + 1) * 512], op=ALU.add), [pbk[g], 'ytmp'], [yak])
                            for g in range(2):
                                mm(pb[4 + g][:, :], bt[:, g * 128:(g + 1) * 128], xdd[:, g * 512:(g + 1) * 512], True, True, [bk, 'xdd'], [pbk[4 + g]])
                            dve(lambda e: e.tensor_tensor(out=Hs[:].rearrange("p (h d) -> p h d", d=64), in0=Hs[:].rearrange("p (h d) -> p h d", d=64),
                                                          in1=cdb[:].unsqueeze(2).broadcast_to([128, 16, 64]), op=ALU.mult), ['Hs', 'cdb'], ['Hs'])
                            for g in range(2):
                                dve(lambda e, g=g: e.tensor_tensor(out=Hs[:, g * 512:(g + 1) * 512], in0=Hs[:, g * 512:(g + 1) * 512], in1=pb[4 + g][:, :], op=ALU.add), ['Hs', pbk[4 + g]], ['Hs'])
                            act(lambda e: e.activation(out=Hb[:], in_=Hs[:], func=AF.Copy), ['Hs'], ['Hb'])
                            if dr == 0:
                                S.dma(Q['yf'][c0:c0 + 128, :], ya[:], reads=[yak], writes=['yf%d' % s])
                            else:
                                yf_, yfk = yfl[slot], 'yfl%d' % slot
                                S.dma(yf_[:], Q['yf'][c0:c0 + 128, :], reads=['yf%d' % s], writes=[yfk])
                                z_, zk = zt[slot], 'zt%d' % slot
                                S.dma(z_[:], Q['tm'][c0:c0 + 128, 0:1024], reads=['tm%d' % s], writes=[zk])
                                dve(lambda e, ya=ya, yf_=yf_: e.tensor_tensor(out=ya[:], in0=ya[:], in1=yf_[:], op=ALU.add), [yak, yfk], [yak])
                                dve(lambda e, xst=xst: e.tensor_tensor(out=ytmp[:].rearrange("p (h d) -> p h d", d=64), in0=xst[:].rearrange("p (h d) -> p h d", d=64),
                                                                  in1=dsk[:].unsqueeze(2).broadcast_to([128, 16, 64]), op=ALU.mult), [xsk, 'dsk'], ['ytmp'])
                                dve(lambda e, ya=ya: e.tensor_tensor(out=ya[:], in0=ya[:], in1=ytmp[:], op=ALU.add), [yak, 'ytmp'], [yak])
                                act(lambda e, z_=z_: e.activation(out=zs[:], in_=z_[:], func=AF.Silu), [zk], ['zs'])
                                dve(lambda e, ya=ya: e.tensor_tensor(out=ya[:], in0=ya[:], in1=zs[:], op=ALU.mult), [yak, 'zs'], [yak])
                                act(lambda e, ya=ya: e.activation(out=zs[:], in_=ya[:], func=AF.Square, accum_out=ssq[:]), [yak], ['zs', 'ssq'])
                                act(lambda e: e.activation(out=ssq[:], in_=ssq[:], func=AF.Ln, scale=1.0 / 1024, bias=EPS), ['ssq'], ['ssq'])
                                act(lambda e: e.activation(out=ssq[:], in_=ssq[:], func=AF.Exp, scale=-0.5), ['ssq'], ['ssq'])
                                dve(lambda e, ya=ya: e.scalar_tensor_tensor(out=ybf[:], in0=ya[:], scalar=ssq[:, 0:1], in1=snw[:], op0=ALU.mult, op1=ALU.mult), [yak, 'ssq', 'snw'], ['ybf'])
                                yT_, yTk = yTs[slot], 'yTs%d' % slot
                                pv = pb[7][:].bitcast(BF16)
                                for c in range(8):
                                    tr(pv[:, c * 128:(c + 1) * 128], ybf[:, c * 128:(c + 1) * 128], identb[:], ['ybf', 'identb'], ['pb7'])
                                act(lambda e, yT_=yT_, pv=pv: e.activation(out=yT_[:].rearrange("p a b -> p (a b)"), in_=pv[:, :], func=AF.Copy), ['pb7'], [yTk])
                                S.dma(Q['yT'][0:1024, :].rearrange("(c p) t -> p c t", p=128)[:, :, c0:c0 + 128], yT_[:], reads=[yTk], writes=['yT%d' % s])
            S.barrier()

            with ExitStack() as st:
                Ena = T(st, 'Ena', [128, 6, 8 * 6 * 128], BF16)
                Ewin = T(st, 'Ewin', [128, 4, 128], BF16)
                esink = T(st, 'esink', [128, 8])
                with ExitStack() as tmp:
                    stg = [T(tmp, 'nastg%d' % i, [128, 3072]) for i in range(2)]
                    k = 0
                    for v in range(6):
                        for hf in range(2):
                            sg, sk = stg[k % 2], 'nastg%d' % (k % 2)
                            k += 1
                            S.dma(sg[:], C['natab'][v, :, hf * 3072:(hf + 1) * 3072], writes=[sk])
                            act(lambda e, sg=sg, v=v, hf=hf: e.activation(out=Ena[:, v, hf * 3072:(hf + 1) * 3072], in_=sg[:], func=AF.Exp), [sk], ['Ena'])
                    S.dma(stg[0][:, 0:512], G['wintab'][:, :], writes=['nastg0'])
                    act(lambda e: e.activation(out=Ewin[:].rearrange("p a b -> p (a b)"), in_=stg[0][:, 0:512], func=AF.Exp), ['nastg0'], ['Ewin'])
                    S.dma(esink[:], C['sink'].partition_broadcast(128).rearrange("p a b -> p (a b)"), writes=['esink'])
                    act(lambda e: e.activation(out=esink[:], in_=esink[:], func=AF.Exp), ['esink'], ['esink'])
                    S.barrier()
                qn = [T(st, 'qn%d' % i, [128, 4, 128], BF16) for i in range(2)]
                qw = [T(st, 'qw%d' % i, [128, 4, 128], BF16) for i in range(2)]
                kn = [T(st, 'kn%d' % i, [128, 4, 6, 128], BF16) for i in range(2)]
                kw = [T(st, 'kw%d' % i, [128, 2, 4, 128], BF16) for i in range(2)]
                vn = [T(st, 'vn%d' % i, [128, 6, 8, 65], BF16) for i in range(2)]
                vw = [T(st, 'vw%d' % i, [128, 4, 2, 65], BF16) for i in range(2)]
                for i in range(2):
                    S.op('pool', lambda e, t=vn[i]: e.memset(t[:], 1.0), [], ['vn%d' % i])
                    S.op('pool', lambda e, t=vw[i]: e.memset(t[:], 1.0), [], ['vw%d' % i])
                knM = T(st, 'knM', [128, 4, 128], BF16)
                vnM = T(st, 'vnM', [128, 8, 65], BF16)
                kwM = T(st, 'kwM', [128, 2, 128], BF16)
                vwM = T(st, 'vwM', [128, 2, 65], BF16)
                S.op('pool', lambda e: e.memset(vnM[:], 1.0), [], ['vnM'])
                S.op('pool', lambda e: e.memset(vwM[:], 1.0), [], ['vwM'])
                Pt = [T(st, 'Pt%d' % i, [128, 768], BF16) for i in range(2)]
                P2 = [T(st, 'P2%d' % i, [128, 768], BF16) for i in range(2)]
                rec = [T(st, 'rec%d' % i, [128, 1]) for i in range(2)]
                yat = [T(st, 'yat%d' % i, [128, 1024], BF16) for i in range(2)]
                yTa = [T(st, 'yTa%d' % i, [128, 8, 128], BF16) for i in range(2)]
                qi = 0
                ui = 0
                for s in range(NS):
                    Q = SEQ[s]
                    TP = TPs[s]
                    NB = TP // 128
                    fmv = Q['fm'].rearrange("(c p) t -> p c t", p=128)
                    S.dma(knM[:], fmv[:, 16:20, 0:128], reads=['fm%d' % s], writes=['knM'])
                    S.dma(vnM[:, :, 0:64], Q['tm'][0:128, 1152:1664].rearrange("t (h d) -> t h d", d=64), reads=['tm%d' % s], writes=['vnM'])
                    for hf in range(2):
                        S.dma(kwM[hf * 64:(hf + 1) * 64, :, :], Q['fm'][3072:3200, 0:128].rearrange("(g p) t -> p g t", p=64), reads=['fm%d' % s], writes=['kwM'])
                    S.dma(vwM[:, :, 0:64], Q['tm'][0:128, 1024:1152].rearrange("t (h d) -> t h d", d=64), reads=['tm%d' % s], writes=['vwM'])
                    for qb in range(NB):
                        slot = qi % 2
                        qi += 1
                        c0 = qb * 128
                        if qb == 0:
                            vi, kbs = 5, [1, 2, 3, 4]
                        elif qb == 1:
                            vi, kbs = 1, [1, 2, 3, 4]
                        elif qb == 2:
                            vi, kbs = 2, [1, 2, 3, 4]
                        elif qb == NB - 1:
                            vi, kbs = 4, [qb - 3, qb - 2, qb - 1, qb]
                        elif qb == NB - 2:
                            vi, kbs = 3, [qb - 2, qb - 1, qb, qb + 1]
                        else:
                            vi, kbs = 0, [qb - 2, qb - 1, qb, qb + 1, qb + 2]
                        na_slots = [(si, kb) for si, kb in enumerate(kbs)] + [(5, 0)]
                        if qb == 0:
                            w_slots = [(2, 1), (3, 0)]
                        else:
                            w_slots = ([(0, qb - 1)] if qb >= 2 else []) + [(1, qb)] + ([(2, qb + 1)] if qb + 1 < NB else []) + [(3, 0)]
                        qn_, qnk = qn[slot], 'qn%d' % slot
                        qw_, qwk = qw[slot], 'qw%d' % slot
                        kn_, knk = kn[slot], 'kn%d' % slot
                        kw_, kwk = kw[slot], 'kw%d' % slot
                        vn_, vnk = vn[slot], 'vn%d' % slot
                        vw_, vwk = vw[slot], 'vw%d' % slot
                        S.dma(qn_[:], fmv[:, 12:16, c0:c0 + 128], reads=['fm%d' % s], writes=[qnk])
                        S.dma(qw_[:], fmv[:, 20:24, c0:c0 + 128], reads=['fm%d' % s], writes=[qwk])
                        k0, nkb = kbs[0], len(kbs)
                        S.dma(kn_[:, :, 0:nkb, :], fmv[:, 16:20, k0 * 128:(k0 + nkb) * 128].rearrange("p c (s t) -> p c s t", t=128), reads=['fm%d' % s], writes=[knk])
                        for (si, kb) in na_slots[:-1]:
                            S.dma(vn_[:, si, :, 0:64], Q['tm'][kb * 128:(kb + 1) * 128, 1152:1664].rearrange("t (h d) -> t h d", d=64), reads=['tm%d' % s], writes=[vnk])
                        wreal = [(si, kb) for (si, kb) in w_slots if si != 3]
                        ws0, wk0, wn = wreal[0][0], wreal[0][1], len(wreal)
                        for hf in range(2):
                            S.dma(kw_[hf * 64:(hf + 1) * 64, :, ws0:ws0 + wn, :],
                                  Q['fm'][3072:3200, wk0 * 128:(wk0 + wn) * 128].rearrange("(g p) (s t) -> p g s t", p=64, t=128), reads=['fm%d' % s], writes=[kwk])
                        for (si, kb) in w_slots[:-1]:
                            S.dma(vw_[:, si, :, 0:64], Q['tm'][kb * 128:(kb + 1) * 128, 1024:1152].rearrange("t (h d) -> t h d", d=64), reads=['tm%d' % s], writes=[vwk])
                        ya_, yk_ = yat[slot], 'yat%d' % slot
                        for hu in range(16):
                            u = ui % 2
                            ui += 1
                            is_win = hu < 8
                            h = hu if is_win else hu - 8
                            slots = w_slots if is_win else na_slots
                            ns = len(slots)
                            pS = [pb[u * 2], pb[u * 2 + 1]]
                            pSk = [pbk[u * 2], pbk[u * 2 + 1]]
                            pO, pOk = pb[4 + u], pbk[4 + u]
                            hp = (h % 2) * 64
                            for j, (si, kb) in enumerate(slots):
                                o = pS[j // 4][:, (j % 4) * 128:(j % 4 + 1) * 128]
                                if is_win:
                                    g = h // 4
                                    if si == 3:
                                        mm(o, kwM[hp:hp + 64, g, :], qw_[hp:hp + 64, h // 2, :], True, True, ['kwM', qwk], [pSk[j // 4]])
                                    else:
                                        mm(o, kw_[hp:hp + 64, g, si, :], qw_[hp:hp + 64, h // 2, :], True, True, [kwk, qwk], [pSk[j // 4]])
                                elif si == 5:
                                    mm(o, knM[hp:hp + 64, h // 2, :], qn_[hp:hp + 64, h // 2, :], True, True, ['knM', qnk], [pSk[j // 4]])
                                else:
                                    mm(o, kn_[hp:hp + 64, h // 2, si, :], qn_[hp:hp + 64, h // 2, :], True, True, [knk, qnk], [pSk[j // 4]])
                            P_, Pk = Pt[u], 'Pt%d' % u
                            P2_, P2k = P2[u], 'P2%d' % u
                            n1 = min(ns, 4) * 128
                            act(lambda e, P_=P_, p=pS[0], n1=n1: e.activation(out=P_[:, 0:n1], in_=p[:, 0:n1], func=AF.Exp, scale=0.125), [pSk[0]], [Pk])
                            if ns > 4:
                                n2 = (ns - 4) * 128
                                act(lambda e, P_=P_, p=pS[1], n2=n2: e.activation(out=P_[:, 512:512 + n2], in_=p[:, 0:n2], func=AF.Exp, scale=0.125), [pSk[1]], [Pk])
                            for j, (si, kb) in enumerate(slots):
                                if is_win:
                                    E = Ewin[:, si, :]
                                    ek = 'Ewin'
                                else:
                                    E = Ena[:, vi, (h * 6 + si) * 128:(h * 6 + si + 1) * 128]
                                    ek = 'Ena'
                                dve(lambda e, P2_=P2_, P_=P_, j=j, E=E: e.tensor_tensor(out=P2_[:, j * 128:(j + 1) * 128], in0=P_[:, j * 128:(j + 1) * 128], in1=E, op=ALU.mult), [Pk, ek], [P2k])
                            for j, (si, kb) in enumerate(slots):
                                if is_win:
                                    V = vw_[:, si, h // 4, :] if si != 3 else vwM[:, h // 4, :]
                                    vk = vwk if si != 3 else 'vwM'
                                else:
                                    V = vn_[:, si, h, :] if si != 5 else vnM[:, h, :]
                                    vk = vnk if si != 5 else 'vnM'
                                mm(pO[:, 0:65], P2_[:, j * 128:(j + 1) * 128], V, j == 0, j == ns - 1, [P2k, vk], [pOk])
                            r_, rk = rec[u], 'rec%d' % u
                            if is_win:
                                dve(lambda e, r_=r_, pO=pO, h=h: e.tensor_tensor(out=r_[:], in0=pO[:, 64:65], in1=esink[:, h:h + 1], op=ALU.add), [pOk, 'esink'], [rk])
                                dve(lambda e, r_=r_: e.reciprocal(out=r_[:], in_=r_[:]), [rk], [rk])
                            else:
                                dve(lambda e, r_=r_, pO=pO: e.reciprocal(out=r_[:], in_=pO[:, 64:65]), [pOk], [rk])
                            act(lambda e, ya_=ya_, pO=pO, r_=r_, hu=hu: e.activation(out=ya_[:, hu * 64:(hu + 1) * 64], in_=pO[:, 0:64], func=AF.Copy, scale=r_[:, 0:1]), [pOk, rk], [yk_])
                        yT_, yTk = yTa[slot], 'yTa%d' % slot
                        pv = pb[6 + slot][:].bitcast(BF16)
                        for c in range(8):
                            tr(pv[:, c * 128:(c + 1) * 128], ya_[:, c * 128:(c + 1) * 128], identb[:], [yk_, 'identb'], [pbk[6 + slot]])
                        dve(lambda e, yT_=yT_, pv=pv: e.tensor_copy(out=yT_[:].rearrange("p a b -> p (a b)"), in_=pv[:, :]), [pbk[6 + slot]], [yTk])
                        S.dma(Q['yT'][1024:2048, :].rearrange("(c p) t -> p c t", p=128)[:, :, c0:c0 + 128], yT_[:], reads=[yTk], writes=['yT%d' % s])
            S.barrier()

            with ExitStack() as st:
                Wo = load_weight_bf16(st, 'Wo', C['w_out'], 2048, D, None)
                npost = T(st, 'npost', [128, 8])
                S.dma(npost[:], C['npost'][:, :], writes=['npost'])
                yin = [T(st, 'yin%d' % i, [128, 16, 512], BF16) for i in range(2)]
                hres = [T(st, 'hres%d' % i, [128, 8, 512]) for i in range(2)]
                mix = T(st, 'mix', [128, 8, 512])
                sq = T(st, 'sq', [128, 8, 512], BF16)
                rstd = T(st, 'rstd', [128, 512])
                hout = [T(st, 'hout%d' % i, [128, 8, 512]) for i in range(2)]
                ti = 0
                npb = 0
                for s in range(NS):
                    Q = SEQ[s]
                    hsrc = Q['h0'] if li == 0 else Q['h1']
                    for (t0, tw) in tiles_of(TPs[s]):
                        slot = ti % 2
                        ti += 1
                        yi, yik = yin[slot], 'yin%d' % slot
                        S.dma(yi[:, :, :tw], Q['yT'].rearrange("(c p) t -> p c t", p=128)[:, :, t0:t0 + tw], reads=['yT%d' % s], writes=[yik])
                        hr, hrk = hres[slot], 'hres%d' % slot
                        S.dma(hr[:, :, :tw], hsrc.rearrange("(c p) t -> p c t", p=128)[:, :, t0:t0 + tw], writes=[hrk])
                        for m in range(8):
                            p, pk = pb[npb % 4], pbk[npb % 4]
                            npb += 1
                            for c in range(16):
                                mm(p[:, :tw], Wo[:, c, m * 128:(m + 1) * 128], yi[:, c, :tw], c == 0, c == 15, ['Wo', yik], [pk])
                            if m % 2 == 0:
                                act(lambda e, p=p, m=m: e.activation(out=mix[:, m, :tw], in_=p[:, :tw], func=AF.Copy), [pk], ['mix'])
                            else:
                                dve(lambda e, p=p, m=m: e.tensor_copy(out=mix[:, m, :tw], in_=p[:, :tw]), [pk], ['mix'])
                        ho, hok = hout[slot], 'hout%d' % slot
                        post_norm_residual(mix, 'mix', hr, hrk, npost, 'npost', tw, sq, rstd, ho, hok, t0 == 0)
                        S.dma(Q['hmid'].rearrange("(c p) t -> p c t", p=128)[:, :, t0:t0 + tw], ho[:, :, :tw], reads=[hok], writes=['hmid%d' % s])
            S.barrier()

            with ExitStack() as st:
                Wu = load_weight_bf16(st, 'Wu', C['w_up'], D, 2 * DFF, C['fpre'])
                fcw = T(st, 'fcw', [128, 132])
                fcb = T(st, 'fcb', [128, 44])
                S.dma(fcw[:], C['fcw'][:, :], writes=['fcw'])
                S.dma(fcb[:], C['fcb'][:, :], writes=['fcb'])
                tl = {'ht': [T(st, 'ht%d' % i, [128, 8, 512]) for i in range(2)], 'sq': T(st, 'sq', [128, 8, 512], BF16),
                      'rstd': T(st, 'rstd', [128, 512]), 'aT': [T(st, 'aT%d' % i, [128, 8, 512], BF16) for i in range(2)]}
                gpre = [T(st, 'gpre%d' % i, [128, 512]) for i in range(4)]
                gc = [T(st, 'gc%d' % i, [128, 512]) for i in range(4)]
                t3 = [T(st, 't3%d' % i, [128, 512]) for i in range(2)]
                ao = [T(st, 'ao%d' % i, [128, 512], BF16) for i in range(2)]
                ti = 0
                npb = 0
                ng = 0
                nt = 0
                for s in range(NS):
                    Q = SEQ[s]
                    TP = TPs[s]
                    t0 = 0
                    while t0 < TP:
                        tw = min(510, TP - t0)
                        slot = ti % 2
                        ti += 1
                        lo, hi = max(t0 - 1, 0), min(t0 + tw + 1, TP)
                        iw = tw + 2
                        ht_, aT, sq_, rstd_ = tl['ht'][slot], tl['aT'][slot], tl['sq'], tl['rstd']
                        hk, ak = 'ht%d' % slot, 'aT%d' % slot
                        if lo != t0 - 1 or hi != t0 + tw + 1:
                            S.op('pool', lambda e, ht_=ht_: e.memset(ht_[:], 0.0), [], [hk])
                        S.dma(ht_[:, :, lo - (t0 - 1):hi - (t0 - 1)], Q['hmid'].rearrange("(c p) t -> p c t", p=128)[:, :, lo:hi], reads=['hmid%d' % s], writes=[hk])
                        act(lambda e, ht_=ht_, iw=iw: e.activation(out=sq_[:, :, :iw], in_=ht_[:, :, :iw], func=AF.Square), [hk], ['sq'])
                        for c in range(8):
                            mm(pb[7][:, :iw], onesb[:], sq_[:, c, :iw], c == 0, c == 7, ['sq', 'onesb'], ['pb7'])
                        act(lambda e, iw=iw: e.activation(out=rstd_[:, :iw], in_=pb[7][:, :iw], func=AF.Ln, scale=1.0 / D, bias=EPS), ['pb7'], ['rstd'])
                        act(lambda e, iw=iw: e.activation(out=rstd_[:, :iw], in_=rstd_[:, :iw], func=AF.Exp, scale=-0.5), ['rstd'], ['rstd'])
                        dve(lambda e, ht_=ht_, aT=aT, iw=iw: e.tensor_tensor(out=aT[:, :, :iw], in0=ht_[:, :, :iw],
                                                                            in1=rstd_[:, :iw].unsqueeze(1).broadcast_to([128, 8, iw]), op=ALU.mult), [hk, 'rstd'], [ak])
                        for jj in range(22):
                            gcs = []
                            for which in range(2):
                                m = jj + 22 * which
                                p, pk = pb[npb % 4], pbk[npb % 4]
                                npb += 1
                                for c in range(8):
                                    mm(p[:, :iw], Wu[:, c, m * 128:(m + 1) * 128], aT[:, c, :iw], c == 0, c == 7, ['Wu', ak], [pk])
                                gp, gpk = gpre[ng % 4], 'gpre%d' % (ng % 4)
                                g_, gk = gc[ng % 4], 'gc%d' % (ng % 4)
                                ng += 1
                                act(lambda e, gp=gp, p=p, iw=iw: e.activation(out=gp[:, :iw], in_=p[:, :iw], func=AF.Copy), [pk], [gpk])
                                eng = 'dve' if which == 0 else 'pool'
                                S.op(eng, lambda e, g_=g_, gp=gp, m=m, tw=tw: e.tensor_scalar(out=g_[:, :tw], in0=gp[:, 0:tw], scalar1=fcw[:, m * 3:m * 3 + 1], scalar2=fcb[:, m:m + 1],
                                                                                        op0=ALU.mult, op1=ALU.add), [gpk, 'fcw', 'fcb'], [gk])
                                for j in (1, 2):
                                    dve(lambda e, g_=g_, gp=gp, m=m, j=j, tw=tw: e.scalar_tensor_tensor(out=g_[:, :tw], in0=gp[:, j:j + tw], scalar=fcw[:, m * 3 + j:m * 3 + j + 1],
                                                                                                  in1=g_[:, :tw], op0=ALU.mult, op1=ALU.add), [gpk, 'fcw', gk], [gk])
                                gcs.append((g_, gk))
                            (gg, ggk), (gu, guk) = gcs
                            t_, tk = t3[nt % 2], 't3%d' % (nt % 2)
                            a_, ak2 = ao[nt % 2], 'ao%d' % (nt % 2)
                            nt += 1
                            act(lambda e, t_=t_, gg=gg, tw=tw: e.activation(out=t_[:, :tw], in_=gg[:, :tw], func=AF.Square), [ggk], [tk])
                            S.op('pool', lambda e, t_=t_, tw=tw: e.tensor_scalar(out=t_[:, :tw], in0=t_[:, :tw], scalar1=0.044715, scalar2=1.0, op0=ALU.mult, op1=ALU.add), [tk], [tk])
                            S.op('pool', lambda e, t_=t_, gg=gg, tw=tw: e.tensor_tensor(out=t_[:, :tw], in0=t_[:, :tw], in1=gg[:, :tw], op=ALU.mult), [tk, ggk], [tk])
                            act(lambda e, t_=t_, tw=tw: e.activation(out=t_[:, :tw], in_=t_[:, :tw], func=AF.Sigmoid, scale=1.5957691216057308), [tk], [tk])
                            S.op('pool', lambda e, t_=t_, gg=gg, tw=tw: e.tensor_tensor(out=t_[:, :tw], in0=t_[:, :tw], in1=gg[:, :tw], op=ALU.mult), [tk, ggk], [tk])
                            dve(lambda e, t_=t_, gu=gu, a_=a_, tw=tw: e.tensor_tensor(out=a_[:, :tw], in0=t_[:, :tw], in1=gu[:, :tw], op=ALU.mult), [tk, guk], [ak2])
                            S.dma(Q['act'][jj * 128:(jj + 1) * 128, t0:t0 + tw], a_[:, :tw], reads=[ak2], writes=['actd%d' % s])
                        t0 += tw
            S.barrier()

            with ExitStack() as st:
                Wd = load_weight_bf16(st, 'Wd', C['w_down'], DFF, D, None)
                fpost = T(st, 'fpost', [128, 8])
                S.dma(fpost[:], C['fpost'][:, :], writes=['fpost'])
                ain = [T(st, 'ain%d' % i, [128, 22, 512], BF16) for i in range(2)]
                hres = [T(st, 'hres%d' % i, [128, 8, 512]) for i in range(2)]
                mix = T(st, 'mix', [128, 8, 512])
                sq = T(st, 'sq', [128, 8, 512], BF16)
                rstd = T(st, 'rstd', [128, 512])
                hout = [T(st, 'hout%d' % i, [128, 8, 512]) for i in range(2)]
                ti = 0
                npb = 0
                for s in range(NS):
                    Q = SEQ[s]
                    for (t0, tw) in tiles_of(TPs[s]):
                        slot = ti % 2
                        ti += 1
                        ai, aik = ain[slot], 'ain%d' % slot
                        S.dma(ai[:, :, :tw], Q['act'].rearrange("(c p) t -> p c t", p=128)[:, :, t0:t0 + tw], reads=['actd%d' % s], writes=[aik])
                        hr, hrk = hres[slot], 'hres%d' % slot
                        S.dma(hr[:, :, :tw], Q['hmid'].rearrange("(c p) t -> p c t", p=128)[:, :, t0:t0 + tw], reads=['hmid%d' % s], writes=[hrk])
                        for m in range(8):
                            p, pk = pb[npb % 4], pbk[npb % 4]
                            npb += 1
                            for c in range(22):
                                mm(p[:, :tw], Wd[:, c, m * 128:(m + 1) * 128], ai[:, c, :tw], c == 0, c == 21, ['Wd', aik], [pk])
                            if m % 2 == 0:
                                act(lambda e, p=p, m=m: e.activation(out=mix[:, m, :tw], in_=p[:, :tw], func=AF.Copy), [pk], ['mix'])
                            else:
                                dve(lambda e, p=p, m=m: e.tensor_copy(out=mix[:, m, :tw], in_=p[:, :tw]), [pk], ['mix'])
                        ho, hok = hout[slot], 'hout%d' % slot
                        post_norm_residual(mix, 'mix', hr, hrk, fpost, 'fpost', tw, sq, rstd, ho, hok, t0 == 0)
                        if last_layer:
                            lo = max(t0, 128)
                            if lo < t0 + tw:
                                S.dma(Q['out'].rearrange("(c p) t -> p c t", p=128)[:, :, lo - 128:t0 + tw - 128], ho[:, :, lo - t0:tw], reads=[hok], writes=['out%d' % s], final=True)
                        else:
                            S.dma(Q['h1'].rearrange("(c p) t -> p c t", p=128)[:, :, t0:t0 + tw], ho[:, :, :tw], reads=[hok], writes=['h1%d' % s])
            S.barrier()

        with nc.Block() as block:
            S.emit(block)
        nops = S.nops
    return nc, nops


def make_seq_inputs(x, meta):
    L = x.shape[0]
    h = np.zeros((D, L + 128), np.float32)
    h[:, PADL:128] = meta.T
    h[:, 128:] = x.T
    return h


def run(seqs_per_core, P, depth, debug=False):
    ncores = len(seqs_per_core)
    seq_lens = [x.shape[0] for x in seqs_per_core[0]]
    nc, nops = build(seq_lens, depth, debug)
    g = _host_globals()
    lcs = [_host_layer_consts(i, P) for i in range(depth)]
    in_maps = []
    for c in range(ncores):
        m = dict(g)
        for i in range(depth):
            for k, v in lcs[i].items():
                m['L%d_%s' % (i, k)] = v
        for s, x in enumerate(seqs_per_core[c]):
            m['s%d_h0' % s] = make_seq_inputs(x, P['meta_tokens'])
            cos, sin = _rope_tables(x.shape[0] + 128)
            m['s%d_cos' % s] = cos
            m['s%d_sin' % s] = sin
        in_maps.append(m)
    res = run_bass_kernel_spmd(nc, in_maps, core_ids=list(range(ncores)))
    return res, nops


def kernel(**inputs):
    P = {k: np.asarray(v, dtype=np.float32) for k, v in inputs.items()}
    xp, xsm = P['x_prompt'], P['x_sample']
    depth = P['w_in'].shape[0]
    nb, ns = xp.shape[0], xsm.shape[0]
    seqs = [[xp[c % nb], xsm[c % ns]] for c in range(8)]
    res, _ = run(seqs, P, depth)
    yp = np.stack([np.ascontiguousarray(res.results[c]['s0_out'].T) for c in range(nb)], axis=0)
    ys = np.stack([np.ascontiguousarray(res.results[c]['s1_out'].T) for c in range(ns)], axis=0)
    return (yp.astype(np.float32), ys.astype(np.float32))
```
